# Optimizing a Trainium2 kernel written in Bass

```python
import math, functools
import jax, jax.numpy as jnp
from jax import lax
import numpy as np


D_MODEL = 1024
BATCH = 8
SEQ = 2048
DEPTH = 2
DEC_BATCH = 128
DEC_SEQ = 4
PAST_LEN = 16384
PAGE_SIZE = 128

N_META = 16
N_EVEN = (DEPTH + 1) // 2
N_ODD = DEPTH // 2
H_A = 4
DK_A = 128
DV_A = 128
H_B = 4
DK_B = 64
DV_B = 128
GLA_RANK = 16
GLA_TAU = 16.0
LA_CHUNK = 32
D_INNER = 2 * D_MODEL
HEAD_P = 64
H_C = D_INNER // HEAD_P
D_STATE = 128
N_GROUPS = 4
D_CONV = 4
CONV_DIM = D_INNER + 2 * N_GROUPS * D_STATE
SSD_CHUNK = 64
D_FF = -(-8 * D_MODEL // (3 * 256)) * 256
EPS = 1e-6

IN_EVEN = 2 * H_A * DK_A + 2 * H_A * DV_A + 2 * H_B * DK_B + 2 * H_B * DV_B + GLA_RANK
OUT_EVEN = H_A * DV_A + H_B * DV_B
IN_ODD = D_INNER + CONV_DIM + H_C

kernel_name = 'hybrid_hgrn2_gla_mamba2_step'


def rmsnorm(x, w):
    xf = x.astype(jnp.float32)
    y = xf * lax.rsqrt(jnp.mean(xf * xf, axis=-1, keepdims=True) + EPS)
    return (y * w.astype(jnp.float32)).astype(x.dtype)


def run_segments(scan_fn, arrays, state, seg_lens):
    outs = []
    start = 0
    for length in seg_lens:
        o, state = scan_fn(*[a[:, start:start + length] for a in arrays], state)
        outs.append(o)
        start += length
    return jnp.concatenate(outs, axis=1), state


def gated_linear_scan(q, k, v, log_f, s0):
    f32 = jnp.float32
    b, t, h, dk = q.shape
    dv = v.shape[-1]
    c = math.gcd(t, LA_CHUNK)
    n = t // c
    qc = q.astype(f32).reshape(b, n, c, h, dk)
    kc = k.astype(f32).reshape(b, n, c, h, dk)
    vc = v.astype(f32).reshape(b, n, c, h, dv)
    cum = jnp.cumsum(log_f.astype(f32).reshape(b, n, c, h, dk), axis=2)
    last = cum[:, :, -1:]
    q_e = qc * jnp.exp(cum)
    k_e = kc * jnp.exp(-cum)
    k_end = kc * jnp.exp(last - cum)
    causal = jnp.tril(jnp.ones((c, c), dtype=bool))
    scores = jnp.where(causal, jnp.einsum('bnihk,bnjhk->bnhij', q_e, k_e), 0.0)
    o_intra = jnp.einsum('bnhij,bnjhv->bnihv', scores, vc)
    u = jnp.einsum('bnjhk,bnjhv->bnhkv', k_end, vc)
    decay = jnp.exp(last[:, :, 0])

    def step(s, inp):
        d_n, u_n = inp
        return d_n[..., None] * s + u_n, s

    s_last, s_start = lax.scan(step, s0.astype(f32), (jnp.moveaxis(decay, 1, 0), jnp.moveaxis(u, 1, 0)))
    s_start = jnp.moveaxis(s_start, 0, 1)
    o_inter = jnp.einsum('bnihk,bnhkv->bnihv', q_e, s_start)
    return (o_intra + o_inter).reshape(b, t, h, dv), s_last


def ssd_scan(x, dt, bm, cm, s0, a_neg):
    f32 = jnp.float32
    b, t, h, p = x.shape
    g, ds = bm.shape[2], bm.shape[3]
    r = h // g
    c = math.gcd(t, SSD_CHUNK)
    n = t // c
    xc = x.astype(f32).reshape(b, n, c, g, r, p)
    dtc = dt.astype(f32).reshape(b, n, c, g, r)
    bc = bm.astype(f32).reshape(b, n, c, g, ds)
    cc = cm.astype(f32).reshape(b, n, c, g, ds)
    cum = jnp.cumsum(dtc * a_neg.reshape(g, r), axis=2)
    cum_t = jnp.moveaxis(cum, 2, -1)
    diff = cum_t[..., :, None] - cum_t[..., None, :]
    causal = jnp.tril(jnp.ones((c, c), dtype=bool))
    decay_ij = jnp.exp(jnp.where(causal, diff, -jnp.inf))
    xdt = xc * dtc[..., None]
    cb = jnp.einsum('bnigs,bnjgs->bngij', cc, bc)
    y_intra = jnp.einsum('bngrij,bnjgrp->bnigrp', cb[:, :, :, None] * decay_ij, xdt)
    decay_end = jnp.exp(cum[:, :, -1:] - cum)
    u = jnp.einsum('bnjgrp,bnjgs->bngrps', xdt * decay_end[..., None], bc)
    chunk_decay = jnp.exp(cum[:, :, -1])

    def step(s, inp):
        d_n, u_n = inp
        return d_n[..., None, None] * s + u_n, s

    s_init = s0.astype(f32).reshape(b, g, r, p, ds)
    s_last, s_start = lax.scan(step, s_init, (jnp.moveaxis(chunk_decay, 1, 0), jnp.moveaxis(u, 1, 0)))
    s_start = jnp.moveaxis(s_start, 0, 1)
    y_inter = jnp.einsum('bnigs,bngrps->bnigrp', cc, s_start) * jnp.exp(cum)[..., None]
    return (y_intra + y_inter).reshape(b, t, h, p), s_last.reshape(b, h, p, ds)


def even_mixer(h, s_hgrn, s_gla, seg, lb, w_in, w_alpha_up, b_alpha, norm_a, norm_b, w_out):
    f32 = jnp.float32
    b, t, _ = h.shape
    wa_k, wa_v, wb_k, wb_v = H_A * DK_A, H_A * DV_A, H_B * DK_B, H_B * DV_B
    sizes = [wa_k, wa_k, wa_v, wa_v, wb_k, wb_k, wb_v, wb_v, GLA_RANK]
    idx = [int(v) for v in np.cumsum(sizes)[:-1]]
    qa, fa, ia, ga, qb, kb, vb, gb, alow = jnp.split(h @ w_in, idx, axis=-1)
    lbh = lb.reshape(H_A, DK_A)
    f_a = lbh + (1.0 - lbh) * jax.nn.sigmoid(fa.astype(f32).reshape(b, t, H_A, DK_A))
    o_a, s_a = run_segments(gated_linear_scan,
                            [qa.reshape(b, t, H_A, DK_A), 1.0 - f_a, ia.reshape(b, t, H_A, DV_A), jnp.log(f_a)],
                            s_hgrn, seg)
    o_a = rmsnorm(o_a, norm_a) * jax.nn.silu(ga.astype(f32).reshape(b, t, H_A, DV_A))
    log_alpha = jax.nn.log_sigmoid((alow @ w_alpha_up + b_alpha).astype(f32)) / GLA_TAU
    o_b, s_b = run_segments(gated_linear_scan,
                            [qb.reshape(b, t, H_B, DK_B) * (DK_B ** -0.5), kb.reshape(b, t, H_B, DK_B),
                             vb.reshape(b, t, H_B, DV_B), log_alpha.reshape(b, t, H_B, DK_B)],
                            s_gla, seg)
    o_b = rmsnorm(o_b, norm_b) * jax.nn.silu(gb.astype(f32).reshape(b, t, H_B, DV_B))
    o = jnp.concatenate([o_a.reshape(b, t, wa_v), o_b.reshape(b, t, wb_v)], axis=-1).astype(h.dtype)
    return o @ w_out, s_a, s_b


def odd_mixer(h, s_ssm, s_conv, seg, w_in, conv_w, conv_b, dt_bias, a_log, d_skip, norm_w, w_out):
    f32 = jnp.float32
    b, t, _ = h.shape
    z, xbc, dt = jnp.split(h @ w_in, [D_INNER, D_INNER + CONV_DIM], axis=-1)
    xpad = jnp.concatenate([s_conv.astype(xbc.dtype), xbc], axis=1)
    new_conv = xpad[:, t:]
    conv = conv_b.astype(f32) + sum(xpad[:, k:k + t].astype(f32) * conv_w[k].astype(f32) for k in range(D_CONV))
    xbc = jax.nn.silu(conv)
    xs, bm, cm = jnp.split(xbc, [D_INNER, D_INNER + N_GROUPS * D_STATE], axis=-1)
    xs = xs.reshape(b, t, H_C, HEAD_P)
    bm = bm.reshape(b, t, N_GROUPS, D_STATE)
    cm = cm.reshape(b, t, N_GROUPS, D_STATE)
    dt = jax.nn.softplus(dt.astype(f32) + dt_bias.astype(f32))
    a_neg = -jnp.exp(a_log.astype(f32))
    y, s_new = run_segments(functools.partial(ssd_scan, a_neg=a_neg), [xs, dt, bm, cm], s_ssm, seg)
    y = y + d_skip.astype(f32)[:, None] * xs
    y = y.reshape(b, t, D_INNER) * jax.nn.silu(z.astype(f32))
    y = rmsnorm(y.reshape(b, t, N_GROUPS, D_INNER // N_GROUPS), norm_w.reshape(N_GROUPS, D_INNER // N_GROUPS))
    return y.reshape(b, t, D_INNER).astype(h.dtype) @ w_out, s_new, new_conv


def swiglu(x, w_gate, w_up, w_down):
    return (jax.nn.silu(x @ w_gate) * (x @ w_up)) @ w_down


def trunk(x, seg, st_hgrn, st_gla, st_ssm, st_conv, lb_all,
          norm_mix_pre, norm_mix_post, norm_ffn_pre, norm_ffn_post,
          ev_w_in, ev_w_alpha_up, ev_b_alpha, ev_norm_a, ev_norm_b, ev_w_out,
          od_w_in, od_conv_w, od_conv_b, od_dt_bias, od_a_log, od_d_skip, od_norm, od_w_out,
          ffn_w_gate, ffn_w_up, ffn_w_down):
    new_hgrn, new_gla, new_ssm, new_conv = [], [], [], []
    for l in range(DEPTH):
        hn = rmsnorm(x, norm_mix_pre[l])
        if l % 2 == 0:
            e = l // 2
            mix, s_a, s_b = even_mixer(hn, st_hgrn[e], st_gla[e], seg, lb_all[l],
                                       ev_w_in[e], ev_w_alpha_up[e], ev_b_alpha[e],
                                       ev_norm_a[e], ev_norm_b[e], ev_w_out[e])
            new_hgrn.append(s_a)
            new_gla.append(s_b)
        else:
            o = l // 2
            mix, s_c, c_c = odd_mixer(hn, st_ssm[o], st_conv[o], seg, od_w_in[o], od_conv_w[o], od_conv_b[o],
                                      od_dt_bias[o], od_a_log[o], od_d_skip[o], od_norm[o], od_w_out[o])
            new_ssm.append(s_c)
            new_conv.append(c_c)
        x = x + rmsnorm(mix.astype(x.dtype), norm_mix_post[l])
        ff = swiglu(rmsnorm(x, norm_ffn_pre[l]), ffn_w_gate[l], ffn_w_up[l], ffn_w_down[l])
        x = x + rmsnorm(ff.astype(x.dtype), norm_ffn_post[l])
    return x, jnp.stack(new_hgrn), jnp.stack(new_gla), jnp.stack(new_ssm), jnp.stack(new_conv)


def setup_inputs(seed: int = 0) -> dict:
    key = jax.random.key(seed)
    ks = jax.random.split(key, 32)
    f32 = jnp.float32

    def nrm(k, shape, scale):
        return jax.random.normal(k, shape, f32) * scale

    def gain(k, shape):
        return 1.0 + 0.05 * jax.random.normal(k, shape, f32)

    dt0 = jnp.exp(jax.random.uniform(ks[21], (N_ODD, H_C), f32, math.log(1e-3), math.log(1e-1)))
    return {
        'x_prompt': nrm(ks[0], (BATCH, SEQ, D_MODEL), 1.0),
        'x_sample': nrm(ks[1], (DEC_BATCH, DEC_SEQ, D_MODEL), 1.0),
        'state_hgrn': nrm(ks[2], (N_EVEN, DEC_BATCH, H_A, DK_A, DV_A), 0.5),
        'state_gla': nrm(ks[3], (N_EVEN, DEC_BATCH, H_B, DK_B, DV_B), 0.5),
        'state_ssm': nrm(ks[4], (N_ODD, DEC_BATCH, H_C, HEAD_P, D_STATE), 0.5),
        'state_conv': nrm(ks[5], (N_ODD, DEC_BATCH, D_CONV - 1, CONV_DIM), 1.0),
        'meta_tokens': nrm(ks[6], (N_META, D_MODEL), 1.0),
        'hgrn_gamma': nrm(ks[7], (DEPTH + 1, H_A * DK_A), 0.5),
        'norm_mix_pre': gain(ks[8], (DEPTH, D_MODEL)),
        'norm_mix_post': gain(ks[9], (DEPTH, D_MODEL)),
        'norm_ffn_pre': gain(ks[10], (DEPTH, D_MODEL)),
        'norm_ffn_post': gain(ks[11], (DEPTH, D_MODEL)),
        'ev_w_in': nrm(ks[12], (N_EVEN, D_MODEL, IN_EVEN), D_MODEL ** -0.5),
        'ev_w_alpha_up': nrm(ks[13], (N_EVEN, GLA_RANK, H_B * DK_B), GLA_RANK ** -0.5),
        'ev_b_alpha': nrm(ks[14], (N_EVEN, H_B * DK_B), 0.1),
        'ev_norm_a': gain(ks[15], (N_EVEN, DV_A)),
        'ev_norm_b': gain(ks[16], (N_EVEN, DV_B)),
        'ev_w_out': nrm(ks[17], (N_EVEN, OUT_EVEN, D_MODEL), OUT_EVEN ** -0.5),
        'od_w_in': nrm(ks[18], (N_ODD, D_MODEL, IN_ODD), D_MODEL ** -0.5),
        'od_conv_w': nrm(ks[19], (N_ODD, D_CONV, CONV_DIM), D_CONV ** -0.5),
        'od_conv_b': nrm(ks[20], (N_ODD, CONV_DIM), 0.02),
        'od_dt_bias': dt0 + jnp.log(-jnp.expm1(-dt0)),
        'od_a_log': jnp.log(jax.random.uniform(ks[22], (N_ODD, H_C), f32, 1.0, 16.0)),
        'od_d_skip': gain(ks[23], (N_ODD, H_C)),
        'od_norm': gain(ks[24], (N_ODD, D_INNER)),
        'od_w_out': nrm(ks[25], (N_ODD, D_INNER, D_MODEL), D_INNER ** -0.5),
        'ffn_w_gate': nrm(ks[26], (DEPTH, D_MODEL, D_FF), D_MODEL ** -0.5),
        'ffn_w_up': nrm(ks[27], (DEPTH, D_MODEL, D_FF), D_MODEL ** -0.5),
        'ffn_w_down': nrm(ks[28], (DEPTH, D_FF, D_MODEL), D_FF ** -0.5),
    }


def reference(x_prompt, x_sample, state_hgrn, state_gla, state_ssm, state_conv, meta_tokens, hgrn_gamma,
              norm_mix_pre, norm_mix_post, norm_ffn_pre, norm_ffn_post,
              ev_w_in, ev_w_alpha_up, ev_b_alpha, ev_norm_a, ev_norm_b, ev_w_out,
              od_w_in, od_conv_w, od_conv_b, od_dt_bias, od_a_log, od_d_skip, od_norm, od_w_out,
              ffn_w_gate, ffn_w_up, ffn_w_down):
    f32 = jnp.float32
    lb_all = jnp.cumsum(jax.nn.softmax(hgrn_gamma.astype(f32), axis=0), axis=0)
    w = (norm_mix_pre, norm_mix_post, norm_ffn_pre, norm_ffn_post,
         ev_w_in, ev_w_alpha_up, ev_b_alpha, ev_norm_a, ev_norm_b, ev_w_out,
         od_w_in, od_conv_w, od_conv_b, od_dt_bias, od_a_log, od_d_skip, od_norm, od_w_out,
         ffn_w_gate, ffn_w_up, ffn_w_down)
    bp, sp = x_prompt.shape[0], x_prompt.shape[1]
    meta = jnp.broadcast_to(meta_tokens.astype(x_prompt.dtype)[None], (bp, N_META, D_MODEL))
    xp = jnp.concatenate([meta, x_prompt], axis=1)
    z_hgrn = jnp.zeros((N_EVEN, bp, H_A, DK_A, DV_A), f32)
    z_gla = jnp.zeros((N_EVEN, bp, H_B, DK_B, DV_B), f32)
    z_ssm = jnp.zeros((N_ODD, bp, H_C, HEAD_P, D_STATE), f32)
    z_conv = jnp.zeros((N_ODD, bp, D_CONV - 1, CONV_DIM), x_prompt.dtype)
    yp, hgrn_p, gla_p, ssm_p, conv_p = trunk(xp, (N_META, sp), z_hgrn, z_gla, z_ssm, z_conv, lb_all, *w)
    ys, hgrn_s, gla_s, ssm_s, conv_s = trunk(x_sample, (x_sample.shape[1],), state_hgrn, state_gla,
                                             state_ssm, state_conv, lb_all, *w)
    return (yp[:, N_META:], ys, hgrn_p, gla_p, ssm_p, conv_p, hgrn_s, gla_s, ssm_s, conv_s)
```

```python
import contextlib
import os
import numpy as np
import concourse.bass as bass
import concourse.mybir as mybir
from concourse.bass_utils import run_bass_kernel_spmd

F32 = mybir.dt.float32
BF16 = mybir.dt.bfloat16
AF = mybir.ActivationFunctionType
ALU = mybir.AluOpType

ENGINES = ("sync", "scalar", "gpsimd", "vector", "tensor")
N_DMA_SEMS = 32
EPS = 1e-6
D = 1024
DFF = 2816
IN_EVEN = 3600
IN_ODD = 5152


class _Op:
    __slots__ = ("eng", "fn", "dma", "waits", "sem", "val", "ninc")


class Prog:
    def __init__(self, nc):
        self.nc = nc
        self.ops = []
        self.cnt = {e: 0 for e in ENGINES}
        self.dma_cnt = [0] * N_DMA_SEMS
        self.dma_rr = {"hw": 0, "sw": 0}
        self.last_w = {}
        self.readers = {}
        self.known = {e: {} for e in ENGINES}
        self.base_keys = {}
        self.base_deps = {}

    def _need(self, op, dep):
        if dep is None:
            return
        sk, v = dep
        if sk == ("e", "tensor") and op.eng == "tensor":
            return
        kn = self.known[op.eng]
        if kn.get(sk, 0) >= v:
            return
        kn[sk] = v
        op.waits.append((sk, v))

    def retire_deps(self, bases):
        deps = {}
        for b in bases:
            for k in self.base_keys.get(b, ()):
                lw = self.last_w.get(k)
                if lw is not None:
                    deps[lw[0]] = max(deps.get(lw[0], 0), lw[1])
                for r in self.readers.get(k, ()):
                    deps[r[0]] = max(deps.get(r[0], 0), r[1])
        return deps

    def set_base_deps(self, base, deps):
        if deps:
            cur = self.base_deps.setdefault(base, {})
            for sk, v in deps.items():
                cur[sk] = max(cur.get(sk, 0), v)

    def op(self, eng, fn, reads=(), writes=(), dma=False, ndma=1):
        o = _Op()
        o.eng, o.fn, o.dma, o.waits = eng, fn, dma, []
        reads, writes = list(reads), list(writes)
        for k in reads:
            if isinstance(k, tuple) and k[0] == "ps":
                for r in self.readers.get(k, ()):
                    if r[0] != ("e", eng):
                        self._need(o, r)
        for k in list(reads) + list(writes):
            b = k[0] if isinstance(k, tuple) else k
            self.base_keys.setdefault(b, set()).add(k)
            bd = self.base_deps.get(b)
            if bd:
                for sk, v in bd.items():
                    self._need(o, (sk, v))
        for k in reads:
            self._need(o, self.last_w.get(k))
        for k in writes:
            self._need(o, self.last_w.get(k))
            for r in self.readers.get(k, ()):
                self._need(o, r)
        if dma:
            half = N_DMA_SEMS // 2
            kind = "sw" if eng == "gpsimd" else "hw"
            s = self.dma_rr[kind] + (half if kind == "sw" else 0)
            self.dma_rr[kind] = (self.dma_rr[kind] + 1) % half
            if self.dma_cnt[s] > 0:
                self._need(o, (("d", s), 16 * self.dma_cnt[s]))
            self.dma_cnt[s] += ndma
            o.sem, o.val, o.ninc = ("d", s), 16 * self.dma_cnt[s], ndma
        else:
            self.cnt[eng] += 1
            o.sem, o.val, o.ninc = ("e", eng), self.cnt[eng], 1
        me = (o.sem, o.val)
        for k in writes:
            self.last_w[k] = me
            self.readers[k] = []
        for k in reads:
            self.readers.setdefault(k, []).append(me)
        self.ops.append(o)
        return o

    def finish_wait(self, eng, keys):
        o = _Op()
        o.eng, o.fn, o.dma, o.waits = eng, None, False, []
        for k in keys:
            self._need(o, self.last_w.get(k))
        o.sem = None
        self.ops.append(o)

    def emit(self, es):
        nc = self.nc
        sems = {}
        for e in ENGINES:
            sems[("e", e)] = es.enter_context(nc.semaphore("se_" + e))
        for i in range(N_DMA_SEMS):
            sems[("d", i)] = es.enter_context(nc.semaphore("sd_%d" % i))
        block = es.enter_context(nc.Block())
        per = {e: [o for o in self.ops if o.eng == e] for e in ENGINES}

        def run(eh, ops):
            for o in ops:
                for sk, v in o.waits:
                    eh.wait_ge(sems[sk], v)
                if o.fn is None:
                    continue
                r = o.fn(eh)
                if o.dma:
                    if not isinstance(r, (list, tuple)):
                        r = [r]
                    assert len(r) == o.ninc, (len(r), o.ninc)
                    for ins in r:
                        ins.then_inc(sems[o.sem], 16)
                else:
                    if isinstance(r, (list, tuple)):
                        r = r[-1]
                    r.then_inc(sems[o.sem], 1)

        @block.sync
        def _(e):
            run(e, per["sync"])

        @block.scalar
        def _(e):
            run(e, per["scalar"])

        @block.gpsimd
        def _(e):
            run(e, per["gpsimd"])

        @block.vector
        def _(e):
            run(e, per["vector"])

        @block.tensor
        def _(e):
            run(e, per["tensor"])


class _Cut(Exception):
    pass


CUT = float(os.environ.get("L1CUT", "99"))
ONECORE = int(os.environ.get("K1CORE", "0"))


def cutpoint(k):
    if CUT <= k:
        raise _Cut()


class Buf:
    __slots__ = ("ap", "name")

    def __init__(self, ap, name):
        self.ap, self.name = ap, name

    def k(self, *idx):
        return (self.name,) + idx if idx else self.name


class Arena:
    def __init__(self, P, big, nbytes):
        self.P, self.big, self.nbytes = P, big, nbytes
        self.off = 0
        self.stack = []
        self.live = []
        self.retired = []
        self.uid = 0
        self.peak = 0

    def alloc(self, name, shape, dt=F32):
        self.uid += 1
        name = "%s#%d" % (name, self.uid)
        esz = 4 if dt == F32 else 2
        n = 1
        for s in shape[1:]:
            n *= s
        nb = (n * esz + 63) // 64 * 64
        st = self.off
        self.off += nb
        self.peak = max(self.peak, self.off)
        assert self.off <= self.nbytes, ("SBUF arena overflow", name, self.off)
        ap = self.big[0:shape[0], st // 4:(st + n * esz + 3) // 4]
        if dt != F32:
            ap = ap.bitcast(dt)
            if n % 2:
                ap = ap[:, 0:n]
        if len(shape) == 3:
            ap = ap.rearrange("p (a b) -> p a b", a=shape[1])
        elif len(shape) == 4:
            ap = ap.rearrange("p (a b c) -> p a b c", a=shape[1], b=shape[2])
        deps = {}
        for (rs, re, rd) in self.retired:
            if rs < st + nb and st < re:
                for sk, v in rd.items():
                    deps[sk] = max(deps.get(sk, 0), v)
        self.P.set_base_deps(name, deps)
        self.live.append((st, st + nb, name))
        return Buf(ap, name)

    def push(self):
        self.stack.append((self.off, len(self.live)))

    def pop(self):
        off, nl = self.stack.pop()
        for (st, en, name) in self.live[nl:]:
            self.retired.append((st, en, self.P.retire_deps([name])))
        del self.live[nl:]
        self.off = off

    @contextlib.contextmanager
    def scope(self):
        self.push()
        try:
            yield
        finally:
            self.pop()


def _cvec_layout():
    names = [("nmpre", 16), ("nmpost", 16), ("nfpre", 16), ("nfpost", 16), ("gamma", 12), ("balpha", 4),
             ("norma", 1), ("normb", 1), ("convw", 96), ("convb", 24), ("dtb", 32), ("alog", 32),
             ("dskip", 32), ("odnorm", 2048)]
    off, lay = 0, {}
    for n, w in names:
        lay[n] = (off, w)
        off += w
    return lay, off


CV_LAY, CV_N = _cvec_layout()


def _mask_layout():
    names = [("identf", 128), ("ones", 128), ("mask64", 512), ("maskms", 80), ("bd64", 128), ("causal", 128),
             ("cms", 80), ("seg", 16), ("ncausal", 128), ("ncms", 80), ("sseg", 80)]
    off, lay = 0, {}
    for n, w in names:
        lay[n] = (off, w)
        off += w
    return lay, off


MK_LAY, MK_N = _mask_layout()


def _build_masks():
    m = np.zeros((128, MK_N), np.float32)

    def put(name, arr):
        o, w = MK_LAY[name]
        m[:arr.shape[0], o:o + arr.shape[1]] = arr

    put("identf", np.eye(128, dtype=np.float32))
    put("ones", np.ones((128, 128), np.float32))
    r = np.ones((128, 512), np.float32)
    r[:, ::64] = 0.0
    put("mask64", r)
    r = np.ones((128, 80), np.float32)
    r[:, 0:64:4] = 0.0
    r[:, 64] = 0.0
    put("maskms", r)
    j = np.arange(128)[:, None]
    i = np.arange(128)[None, :]
    causal = (i >= j).astype(np.float32)
    put("causal", causal)
    put("bd64", causal * ((i // 64) == (j // 64)))
    seg_id = np.concatenate([np.arange(64) // 4, np.full(16, 16)])
    cms = ((seg_id[:, None] == seg_id[None, :]) & (np.arange(80)[None, :] >= np.arange(80)[:, None])).astype(np.float32)
    put("cms", cms)
    seg = np.zeros((80, 16), np.float32)
    seg[np.arange(64), np.arange(64) // 4] = 1.0
    put("seg", seg)
    put("ncausal", (causal - 1.0) * 30000.0)
    put("ncms", (cms - 1.0) * 30000.0)
    put("sseg", (seg_id[:, None] == seg_id[None, :]).astype(np.float32))
    return m


def build_program(SEQ, stage=99):
    NT = SEQ // 128
    n0 = NT // 2
    n1 = NT - n0
    nc = bass.Bass("TRN2", target_bir_lowering=False)

    def din(name, shape, dt=F32):
        return nc.dram_tensor(name, shape, dt, kind="ExternalInput").ap()

    def dout(name, shape):
        return nc.dram_tensor(name, shape, F32, kind="ExternalOutput").ap()

    xp_d = din("xp", [SEQ, D])
    xs_d = din("xs", [64, D])
    meta_d = din("meta", [16, D])
    sth_d = din("st_h", [16, 4, 128, 128])
    stg_d = din("st_g", [16, 4, 64, 128])
    sts_d = din("st_s", [16, 32, 64, 128])
    stc_d = din("st_c", [16, 3, 3072])
    win0_d = din("w_in0", [D, IN_EVEN])
    wup_d = din("w_up", [16, 256])
    wout0_d = din("w_out0", [D, D])
    win1_d = din("w_in1", [D, IN_ODD])
    wout1_d = din("w_out1", [2048, D])
    wg_d = din("w_g", [2, D, DFF])
    wu_d = din("w_u", [2, D, DFF])
    wd_d = din("w_d", [2, DFF, D])
    cvec_d = din("cvec", [128, CV_N])
    mask_d = din("masks", [128, MK_N])

    yp_d = dout("yp", [SEQ, D])
    ys_d = dout("ys", [64, D])
    hp_d = dout("hp", [4, 128, 128])
    gp_d = dout("gp", [4, 64, 128])
    sp_d = dout("sp", [32, 64, 128])
    cp_d = dout("cp", [3, 3072])
    hs_d = dout("hs", [16, 4, 128, 128])
    gs_d = dout("gs", [16, 4, 64, 128])
    ss_d = dout("ss", [16, 32, 64, 128])
    cs_d = dout("cs", [16, 3, 3072])
    out_keys = []

    es = contextlib.ExitStack()
    ARENA_BYTES = 212000
    big = es.enter_context(nc.sbuf_tensor("arena", [128, ARENA_BYTES // 4], F32))
    banks = [es.enter_context(nc.psum_tensor("bank%d" % i, [128, 512], F32)) for i in range(8)]
    P = Prog(nc)
    A = Arena(P, big, ARENA_BYTES)

    ps_state = {"i": 0}

    def PS(pool=(0, 1, 2, 3, 4, 5, 6, 7)):
        ps_state["i"] += 1
        b = pool[ps_state["i"] % len(pool)]
        return banks[b], ("ps", b)

    def bfview(bank):
        return bank[:, :].bitcast(BF16)

    rr = {"dmaq": 0, "ev": 0}

    def dma(out, in_, reads, writes, eng=None):
        if eng is None:
            eng = "sync"
        P.op(eng, lambda e: e.dma_start(out=out, in_=in_), reads=reads, writes=writes, dma=True)

    def act(out, in_, func, reads, writes, scale=1.0, bias=0.0):
        reads = list(reads)
        if not isinstance(bias, float):
            reads.append("epsb_key")
        P.op("scalar", lambda e: e.activation(out=out, in_=in_, func=func, scale=scale, bias=bias),
             reads=reads, writes=writes)

    def tt(out, in0, in1, op, reads, writes, eng="vector"):
        P.op(eng, lambda e: e.tensor_tensor(out=out, in0=in0, in1=in1, op=op), reads=reads, writes=writes)

    def ts(out, in0, s1, s2, op0, op1, reads, writes, eng="vector"):
        P.op(eng, lambda e: e.tensor_scalar(out=out, in0=in0, scalar1=s1, scalar2=s2, op0=op0, op1=op1),
             reads=reads, writes=writes)

    def stt(out, in0, scalar, in1, op0, op1, reads, writes):
        P.op("vector", lambda e: e.scalar_tensor_tensor(out=out, in0=in0, scalar=scalar, in1=in1, op0=op0, op1=op1),
             reads=reads, writes=writes)

    def copy(out, in_, reads, writes, eng=None):
        if eng is None:
            rr["ev"] += 1
            eng = "vector" if rr["ev"] % 2 else "scalar"
        if eng == "scalar":
            act(out, in_, AF.Copy, reads, writes)
        else:
            P.op(eng, lambda e: e.tensor_copy(out=out, in_=in_), reads=reads, writes=writes)

    def mm(out, pairs, reads, writes, tile_position=None, first=True, last=True):
        def fn(e):
            r = None
            n = len(pairs)
            for i, (l, rh) in enumerate(pairs):
                kw = {}
                if tile_position is not None:
                    kw["tile_position"] = tile_position
                r = e.matmul(out, lhsT=l, rhs=rh, start=(first and i == 0), stop=(last and i == n - 1), **kw)
            return r
        P.op("tensor", fn, reads=reads, writes=writes)

    def mm_multi(items, reads, writes):
        def fn(e):
            r = None
            for (o, l, rh, st, sp, tp) in items:
                kw = {}
                if tp is not None:
                    kw["tile_position"] = tp
                r = e.matmul(o, lhsT=l, rhs=rh, start=st, stop=sp, **kw)
            return r
        P.op("tensor", fn, reads=reads, writes=writes)

    def transpose_multi(items, reads, writes):
        def fn(e):
            r = None
            for (o, i_, idn) in items:
                r = e.transpose(out=o, in_=i_, identity=idn)
            return r
        P.op("tensor", fn, reads=reads, writes=writes)

    cv = A.alloc("cvec", [128, CV_N])
    mk = A.alloc("masks", [128, MK_N])
    dma(cv.ap, cvec_d[:, :], [], [cv.k()])
    dma(mk.ap, mask_d[:, :], [], [mk.k()])

    def CV(name, parts=128):
        o, w = CV_LAY[name]
        return cv.ap[0:parts, o:o + w]

    def MK(name, parts=128, cols=None):
        o, w = MK_LAY[name]
        if cols is not None:
            w = cols
        return mk.ap[0:parts, o:o + w]

    cb = A.alloc("cbf", [128, 128 * 4 + 16], BF16)
    identb = cb.ap[:, 0:128]
    onesb = cb.ap[:, 128:256]
    ncausb = cb.ap[:, 256:384]
    ncmsb = cb.ap[:, 384:464]
    segb = cb.ap[:, 512:528]
    copy(identb, MK("identf"), [mk.k()], [cb.k(0)], eng="vector")
    copy(onesb, MK("ones"), [mk.k()], [cb.k(1)], eng="vector")
    copy(ncausb, MK("ncausal"), [mk.k()], [cb.k(2)], eng="vector")
    copy(ncmsb, MK("ncms"), [mk.k()], [cb.k(3)], eng="vector")
    copy(segb, MK("seg"), [mk.k()], [cb.k(4)], eng="vector")
    CBK = [cb.k(i) for i in range(5)]
    identf = MK("identf")
    onesf = MK("ones")

    lbb = A.alloc("lb", [128, 16])
    g_o, _ = CV_LAY["gamma"]
    gam = cv.ap[:, g_o:g_o + 12].rearrange("p (l h) -> p l h", l=3)
    eg = lbb.ap[:, 0:12].rearrange("p (l h) -> p l h", l=3)
    act(lbb.ap[:, 0:12], cv.ap[:, g_o:g_o + 12], AF.Exp, [cv.k()], [lbb.k()])
    sm = A.alloc("lbtmp", [128, 8])
    tt(sm.ap[:, 0:4], eg[:, 0, :], eg[:, 1, :], ALU.add, [lbb.k()], [sm.k()])
    tt(sm.ap[:, 0:4], sm.ap[:, 0:4], eg[:, 2, :], ALU.add, [lbb.k(), sm.k()], [sm.k()])
    P.op("vector", lambda e: e.reciprocal(out=sm.ap[:, 4:8], in_=sm.ap[:, 0:4]), reads=[sm.k()], writes=[sm.k()])
    LB = lbb.ap[:, 12:16]
    tt(LB, eg[:, 0, :], sm.ap[:, 4:8], ALU.mult, [lbb.k(), sm.k()], [lbb.k()])
    OML = sm.ap[:, 0:4]
    ts(OML, LB, -1.0, 1.0, ALU.mult, ALU.add, [lbb.k(), sm.k()], [sm.k()])
    LBK = [lbb.k(), sm.k()]

    S_h = A.alloc("S_h", [128, 4, 128])
    S_g = A.alloc("S_g", [64, 4, 128])
    Sbf_h = A.alloc("Sbf_h", [128, 4, 128], BF16)
    Sbf_g = A.alloc("Sbf_g", [64, 4, 128], BF16)

    def run_pass(pi):
        npt = n0 if pi == 0 else n1
        tile0 = 0 if pi == 0 else n0
        PC = 128 * npt
        has_ms = (pi == 0)
        Tp = PC + (80 if has_ms else 0)
        MS0 = PC
        groups = []
        c = 0
        while c < PC:
            n = min(512, PC - c)
            groups.append(("P", c, n))
            c += n
        if has_ms:
            groups = [("MS", MS0, 80)] + groups

        xT = A.alloc("xT", [128, 8, Tp])
        A.push()
        W0 = A.alloc("W0", [128, 8, IN_EVEN], BF16)
        for kc in range(8):
            dma(W0.ap[:, kc, :], win0_d[kc * 128:(kc + 1) * 128, :], [], [W0.k(kc)], eng="gpsimd")
        W0K = [W0.k(kc) for kc in range(8)]
        wup = A.alloc("wup", [16, 256], BF16)
        dma(wup.ap[:, :], wup_d[:, :], [], [wup.k()], eng="gpsimd")
        XK = lambda g: xT.k(g)

        def gkey(buf, g):
            return buf.k(g[1])

        with A.scope():
            stg = [A.alloc("instage", [128, D]) for _ in range(2)]
            units = []
            if has_ms:
                units.append(("MS", MS0, 80))
            for t in range(npt):
                units.append(("T", 128 * t, 128))
            for ui, (kind, c0, nr) in enumerate(units):
                sb = stg[ui % 2]
                if kind == "MS":
                    dma(sb.ap[0:64, :], xs_d[:, :], [], [sb.k()])
                    dma(sb.ap[64:80, :], meta_d[:, :], [], [sb.k(1)], eng="scalar")
                    rk = [sb.k(), sb.k(1)]
                else:
                    r0 = (tile0 + c0 // 128) * 128
                    dma(sb.ap[:, :], xp_d[r0:r0 + 128, :], [], [sb.k()], eng=("sync" if ui % 2 else "scalar"))
                    rk = [sb.k()]
                gk = xT.k(("MS", MS0) if kind == "MS" else (c0 // 512) * 512)
                for half in range(2):
                    bank, bk = PS()
                    transpose_multi([(bank[:, q * 128:q * 128 + nr], sb.ap[0:nr, (half * 4 + q) * 128:(half * 4 + q + 1) * 128],
                                      identf[0:nr, 0:nr]) for q in range(4)], rk + [mk.k()], [bk])
                    copy(xT.ap[:, half * 4:half * 4 + 4, c0:c0 + nr],
                         bank[:, :].rearrange("p (q c) -> p q c", q=4)[:, :, 0:nr], [bk], [(xT.name, "in", ui, half)])
            XIN_KEYS = [(xT.name, "in", ui, h) for ui in range(len(units)) for h in range(2)]

        def xkeys_for(g):
            return XIN_KEYS + [xT.k(g[1])]

        def fm_rstd(src_fn, nchunks, n, scale_div, reads, sq, rstd, tagk):
            for c in range(nchunks):
                act(sq.ap[:, c, 0:n], src_fn(c), AF.Square, reads, [sq.k(c)])
            bank, bk = PS()
            mm(bank[:, 0:n], [(onesb, sq.ap[:, c, 0:n]) for c in range(nchunks)],
               [sq.k(c) for c in range(nchunks)] + CBK, [bk])
            act(rstd.ap[:, 0:n], bank[:, 0:n], AF.Ln, [bk], [rstd.k()], scale=1.0 / scale_div, bias=epsb.ap[:, 0:1])
            act(rstd.ap[:, 0:n], rstd.ap[:, 0:n], AF.Exp, [rstd.k()], [rstd.k()], scale=-0.5)

        def prenorm(g, wname, layer, hn, hn_c0, sq, rstd):
            kind, c0, n = g
            o, _ = CV_LAY[wname]
            fm_rstd(lambda c: xT.ap[:, c, c0:c0 + n], 8, n, float(D), xkeys_for(g), sq, rstd, None)
            for c in range(8):
                stt(hn.ap[:, c, hn_c0:hn_c0 + n], xT.ap[:, c, c0:c0 + n], cv.ap[:, o + layer * 8 + c:o + layer * 8 + c + 1],
                    rstd.ap[:, 0:n], ALU.mult, ALU.mult, xkeys_for(g) + [rstd.k(), cv.k()], [hn.k(g[1], c)])

        def postnorm_add(g, wname, layer, mix, sq, rstd):
            kind, c0, n = g
            o, _ = CV_LAY[wname]
            fm_rstd(lambda c: mix.ap[:, c, 0:n], 8, n, float(D), [mix.k(c) for c in range(8)], sq, rstd, None)
            for c in range(8):
                tt(mix.ap[:, c, 0:n], mix.ap[:, c, 0:n], rstd.ap[:, 0:n], ALU.mult, [mix.k(c), rstd.k()], [mix.k(c)],
                   eng=("gpsimd" if c % 2 else "vector"))
            for c in range(8):
                stt(xT.ap[:, c, c0:c0 + n], mix.ap[:, c, 0:n], cv.ap[:, o + layer * 8 + c:o + layer * 8 + c + 1],
                    xT.ap[:, c, c0:c0 + n], ALU.mult, ALU.add, xkeys_for(g) + [mix.k(c), cv.k()], [xT.k(g[1])])

        def layer0_mixer():
            for g in groups:
                layer0_group(g, W0, W0K, None, None, wup)

        def layer0_group(g, W0, W0K, WO, WOK, wup):
            kind, c0, n = g
            isms = (kind == "MS")
            ntile = 1 if isms else n // 128
            nrow = 80 if isms else 128
            with A.scope():
                hn = A.alloc("hn", [128, 8, n], BF16)
                yT = A.alloc("yT", [128, 8, n], BF16)
                with A.scope():
                    sq = A.alloc("sq", [128, 8, n], BF16)
                    rstd = A.alloc("rstd", [128, n])
                    prenorm(g, "nmpre", 0, hn, 0, sq, rstd)
                HNK = [hn.k(g[1], c) for c in range(8)]

                def proj_fm(col0, m, nn=n):
                    bank, bk = PS()
                    mm(bank[0:m, 0:nn], [(W0.ap[:, kc, col0:col0 + m], hn.ap[:, kc, 0:nn]) for kc in range(8)],
                       HNK + W0K, [bk])
                    return bank, bk

                with A.scope():
                    sg = A.alloc("sg", [128, 8, n], BF16)
                    qe = A.alloc("qe", [128, 8, n], BF16)
                    ke = A.alloc("ke", [128, 8, n], BF16)
                    Eall = A.alloc("Eall", [128, 8, 20])
                    alow = A.alloc("alow", [16, n], BF16)
                    with A.scope():
                        G1 = A.alloc("G1", [128, 8, n])
                        for h in range(8):
                            col = (1536 + 128 * h) if h < 4 else (3072 + 128 * (h - 4))
                            bank, bk = proj_fm(col, 128)
                            act(sg.ap[:, h, 0:n], bank[:, 0:n], AF.Silu, [bk], [sg.k(h)])
                        for h in range(4):
                            bank, bk = proj_fm(512 + 128 * h, 128)
                            act(G1.ap[:, h, 0:n], bank[:, 0:n], AF.Sigmoid, [bk], [G1.k(h)])
                        bank, bk = proj_fm(3584, 16)
                        copy(alow.ap[:, 0:n], bank[0:16, 0:n], [bk], [alow.k()], eng="vector")
                        bo, _ = CV_LAY["balpha"]
                        for h in range(4):
                            bank, bk = PS()
                            mm(bank[0:64, 0:n], [(wup.ap[0:16, 64 * h:64 * h + 64], alow.ap[0:16, 0:n])],
                               [wup.k(), alow.k()], [bk])
                            act(G1.ap[0:64, 4 + h, 0:n], bank[0:64, 0:n], AF.Sigmoid, [bk, cv.k()], [G1.k(4 + h)],
                                bias=cv.ap[0:64, bo + h:bo + h + 1])
                        rmask = MK("maskms", cols=80) if isms else MK("mask64", cols=n)
                        gsets = [[A.alloc(nm, [128, n]) for nm in ("lf", "cum", "eq", "ek")] for _ in range(2)]
                        for h in range(8):
                            dk = 128 if h < 4 else 64
                            if True:
                                lf, cum, eq, ek = gsets[h % 2]
                                if h < 4:
                                    ts(G1.ap[:, h, 0:n], G1.ap[:, h, 0:n], OML[:, h:h + 1], LB[:, h:h + 1], ALU.mult, ALU.add,
                                       [G1.k(h)] + LBK, [G1.k(h)])
                                    act(lf.ap[:, 0:n], G1.ap[:, h, 0:n], AF.Ln, [G1.k(h)], [lf.k()])
                                    ts(G1.ap[:, h, 0:n], G1.ap[:, h, 0:n], -1.0, 1.0, ALU.mult, ALU.add, [G1.k(h), lf.k()], [G1.k(h)])
                                    esc = 1.0
                                else:
                                    act(lf.ap[0:dk, 0:n], G1.ap[0:dk, h, 0:n], AF.Ln, [G1.k(h)], [lf.k()])
                                    esc = 1.0 / 16.0
                                if isms or h < 4:
                                    P.op("vector", lambda e, cum=cum, lf=lf, dk=dk: e.tensor_tensor_scan(
                                        out=cum.ap[0:dk, 0:n], data0=rmask[0:dk, 0:n], data1=lf.ap[0:dk, 0:n], initial=0.0,
                                        op0=ALU.mult, op1=ALU.add), reads=[lf.k(), mk.k()], writes=[cum.k()])
                                else:
                                    def scan_fn(e, cum=cum, lf=lf, dk=dk):
                                        r_ = None
                                        for t_ in range(n // 128):
                                            r_ = e.tensor_tensor_scan(out=cum.ap[0:dk, 128 * t_:128 * t_ + 128],
                                                                      data0=onesf[0:dk, 0:128], data1=lf.ap[0:dk, 128 * t_:128 * t_ + 128],
                                                                      initial=0.0, op0=ALU.mult, op1=ALU.add)
                                        return r_
                                    P.op("vector", scan_fn, reads=[lf.k(), mk.k()], writes=[cum.k()])
                                act(eq.ap[0:dk, 0:n], cum.ap[0:dk, 0:n], AF.Exp, [cum.k()], [eq.k()], scale=esc)
                                act(ek.ap[0:dk, 0:n], cum.ap[0:dk, 0:n], AF.Exp, [cum.k()], [ek.k()], scale=-esc)
                                if isms:
                                    copy(Eall.ap[0:dk, h, 0:16], eq.ap[0:dk, 0:64].rearrange("p (s j) -> p s j", j=4)[:, :, 3], [eq.k()], [Eall.k(h)], eng="vector")
                                    copy(Eall.ap[0:dk, h, 16:17], eq.ap[0:dk, 79:80], [eq.k()], [Eall.k(h)], eng="vector")
                                else:
                                    CHh = 64 if h < 4 else 128
                                    copy(Eall.ap[0:dk, h, 0:n // CHh], eq.ap[0:dk, 0:n].rearrange("p (c j) -> p c j", j=CHh)[:, :, CHh - 1], [eq.k()], [Eall.k(h)], eng="vector")
                                if h < 4:
                                    bank, bk = proj_fm(128 * h, 128)
                                    tt(qe.ap[:, h, 0:n], bank[:, 0:n], eq.ap[:, 0:n], ALU.mult, [bk, eq.k()], [qe.k(h)])
                                    tt(ke.ap[:, h, 0:n], G1.ap[:, h, 0:n], ek.ap[:, 0:n], ALU.mult, [G1.k(h), ek.k()], [ke.k(h)])
                                else:
                                    bank, bk = proj_fm(2048 + 64 * (h - 4), 64)
                                    stt(qe.ap[0:64, h, 0:n], bank[0:64, 0:n], 0.125, eq.ap[0:64, 0:n], ALU.mult, ALU.mult,
                                        [bk, eq.k()], [qe.k(h)])
                                    bank, bk = proj_fm(2304 + 64 * (h - 4), 64)
                                    tt(ke.ap[0:64, h, 0:n], bank[0:64, 0:n], ek.ap[0:64, 0:n], ALU.mult, [bk, ek.k()], [ke.k(h)])

                    WO = A.alloc("WO0", [128, 8, D], BF16)
                    for kc in range(8):
                        dma(WO.ap[:, kc, :], wout0_d[kc * 128:(kc + 1) * 128, :], [], [WO.k(kc)], eng="gpsimd")
                    WOK = [WO.k(kc) for kc in range(8)]
                    for t in range(ntile):
                        l0 = 128 * t
                        for fam in range(2):
                            dk = 128 if fam == 0 else 64
                            S = S_h if fam == 0 else S_g
                            Sbf = Sbf_h if fam == 0 else Sbf_g
                            nwname = "norma" if fam == 0 else "normb"
                            vcol = 1024 if fam == 0 else 2560
                            hs = [4 * fam + q for q in range(4)]
                            with A.scope():
                                ktok = A.alloc("ktok", [128, 4, 128], BF16)
                                vtok = A.alloc("vtok", [128, 4, 128], BF16)
                                scm = A.alloc("scm", [128, 4, 128], BF16)
                                Sst = A.alloc("Sst", [128, 4, 4, 128], BF16)
                                Ttmp = A.alloc("Ttmp", [128, 4, 128])
                                bank, bk = PS()
                                bv = bfview(bank)
                                transpose_multi([(bv[0:nrow, q * dk:(q + 1) * dk], ke.ap[0:dk, hs[q], l0:l0 + nrow],
                                                  identb[0:dk, 0:dk]) for q in range(4)],
                                                [ke.k(h) for h in hs] + CBK, [bk])
                                copy(ktok.ap[0:nrow, :, 0:dk], bv[0:nrow, 0:4 * dk].rearrange("p (q d) -> p q d", q=4),
                                     [bk], [ktok.k()])
                                bank, bk = PS()
                                mm(bank[0:nrow, 0:512], [(hn.ap[:, kc, l0:l0 + nrow], W0.ap[:, kc, vcol:vcol + 512]) for kc in range(8)],
                                   HNK + W0K, [bk])
                                copy(vtok.ap[0:nrow, :, :], bank[0:nrow, :].rearrange("p (q d) -> p q d", q=4), [bk], [vtok.k()])
                                bank, bk = PS()
                                mm_multi([(bank[0:nrow, q * 128:q * 128 + nrow], ke.ap[0:dk, hs[q], l0:l0 + nrow],
                                           qe.ap[0:dk, hs[q], l0:l0 + nrow], True, True, None) for q in range(4)],
                                         [ke.k(h) for h in hs] + [qe.k(h) for h in hs], [bk])
                                cmask = MK("cms", parts=80, cols=80) if isms else (MK("bd64") if fam == 0 else MK("causal"))
                                CH = 64 if fam == 0 else 128
                                nch = 128 // CH
                                tt(scm.ap[0:nrow, :, 0:nrow], bank[0:nrow, :].rearrange("p (q d) -> p q d", q=4)[:, :, 0:nrow],
                                   cmask.unsqueeze(1).to_broadcast([nrow, 4, nrow]), ALU.mult, [bk, mk.k()], [scm.k()])

                                if not isms:
                                    for c in range(nch):
                                        if c == 0:
                                            copy(Sst.ap[0:dk, 0, :, :], Sbf.ap[0:dk, :, :], [Sbf.k()], [Sst.k(0)], eng="gpsimd")
                                        bank, bk = PS()
                                        mm_multi([(bank[0:dk, q * 128:(q + 1) * 128], ktok.ap[CH * c:CH * c + CH, q, 0:dk],
                                                   vtok.ap[CH * c:CH * c + CH, q, :], True, True, (CH * c, 0)) for q in range(4)],
                                                 [ktok.k(), vtok.k()], [bk])
                                        tt(Ttmp.ap[0:dk, :, :], S.ap[0:dk, :, :], bank[0:dk, :].rearrange("p (q d) -> p q d", q=4),
                                           ALU.add, [S.k(), bk], [Ttmp.k()])
                                        ci = t * nch + c
                                        tt(S.ap[0:dk, :, :], Ttmp.ap[0:dk, :, :],
                                           Eall.ap[0:dk, 4 * fam:4 * fam + 4, ci:ci + 1].to_broadcast([dk, 4, 128]),
                                           ALU.mult, [Ttmp.k()] + [Eall.k(h) for h in hs], [S.k()])
                                        if c < nch - 1:
                                            copy(Sst.ap[0:dk, c + 1, :, :], S.ap[0:dk, :, :], [S.k()], [Sst.k(c + 1)], eng="scalar")
                                        else:
                                            copy(Sbf.ap[0:dk, :, :], S.ap[0:dk, :, :], [S.k()], [Sbf.k()], eng="scalar")
                                    obank, obk = PS()
                                    items = []
                                    for q in range(4):
                                        items.append((obank[:, q * 128:(q + 1) * 128], vtok.ap[:, q, :], scm.ap[:, q, :], True, False, None))
                                        for c in range(nch):
                                            items.append((obank[:, q * 128 + CH * c:q * 128 + CH * c + CH], Sst.ap[0:dk, c, q, :],
                                                          qe.ap[0:dk, hs[q], l0 + CH * c:l0 + CH * c + CH], False, c == nch - 1, None))
                                    mm_multi(items, [vtok.k(), scm.k()] + [Sst.k(c) for c in range(nch)] + [qe.k(h) for h in hs], [obk])
                                else:
                                    bank, bk = PS()
                                    mm_multi([(bank[0:dk, q * 128:(q + 1) * 128], ktok.ap[64:80, q, 0:dk], vtok.ap[64:80, q, :],
                                               True, True, (64, 0)) for q in range(4)], [ktok.k(), vtok.k()], [bk])
                                    tt(S.ap[0:dk, :, :], bank[0:dk, :].rearrange("p (q d) -> p q d", q=4),
                                       Eall.ap[0:dk, 4 * fam:4 * fam + 4, 16:17].to_broadcast([dk, 4, 128]), ALU.mult,
                                       [bk] + [Eall.k(h) for h in hs], [S.k()])
                                    copy(Sbf.ap[0:dk, :, :], S.ap[0:dk, :, :], [S.k()], [Sbf.k()], eng="scalar")
                                    obank, obk = PS(pool=(6, 7))
                                    st_d = sth_d if fam == 0 else stg_d
                                    so_d = hs_d if fam == 0 else gs_d
                                    S0s_ = [A.alloc("S0", [128, 16, 128]) for _ in range(2)]
                                    S0bs_ = [A.alloc("S0b", [128, 16, 128], BF16) for _ in range(2)]
                                    Vbds_ = [A.alloc("Vbd", [64, 16, 128], BF16)] * 2
                                    Sns_ = [A.alloc("Sn", [128, 16, 128]) for _ in range(2)]

                                    def s0_load(q_):
                                        dma(S0s_[q_ % 2].ap[0:dk, :, :], st_d[:, q_, :, :].rearrange("s k v -> k s v"), [],
                                            [S0s_[q_ % 2].k()], eng="sync")
                                    s0_load(0)
                                    for q in range(4):
                                        if True:
                                            S0, S0b, Vbd, Sn = S0s_[q % 2], S0bs_[q % 2], Vbds_[q % 2], Sns_[q % 2]
                                            if q + 1 < 4:
                                                s0_load(q + 1)
                                            copy(S0b.ap[0:dk, :, :], S0.ap[0:dk, :, :], [S0.k()], [S0b.k()], eng="gpsimd")
                                            tt(Vbd.ap[:, :, :], vtok.ap[0:64, q, :].unsqueeze(1).to_broadcast([64, 16, 128]),
                                               segb[0:64, 0:16].unsqueeze(2).to_broadcast([64, 16, 128]), ALU.mult,
                                               [vtok.k()] + CBK, [Vbd.k()])
                                            for qq in range(4):
                                                bank, bk = PS(pool=(0, 1, 2, 3, 4, 5))
                                                mm(bank[0:dk, 0:512], [(ktok.ap[0:64, q, 0:dk],
                                                                        Vbd.ap[:, 4 * qq:4 * qq + 4, :].rearrange("p s d -> p (s d)"))],
                                                   [ktok.k(), Vbd.k()], [bk])
                                                tt(Sn.ap[0:dk, 4 * qq:4 * qq + 4, :], S0.ap[0:dk, 4 * qq:4 * qq + 4, :],
                                                   bank[0:dk, :].rearrange("p (s d) -> p s d", s=4), ALU.add, [S0.k(), bk], [Sn.k(qq)])
                                                tt(Sn.ap[0:dk, 4 * qq:4 * qq + 4, :], Sn.ap[0:dk, 4 * qq:4 * qq + 4, :],
                                                   Eall.ap[0:dk, hs[q], 4 * qq:4 * qq + 4].unsqueeze(2).to_broadcast([dk, 4, 128]),
                                                   ALU.mult, [Sn.k(qq), Eall.k(hs[q])], [Sn.k(qq)])
                                            okey = ("out_s", fam, q)
                                            dma(so_d[:, q, :, :].rearrange("s k v -> k s v"), Sn.ap[0:dk, :, :],
                                                [Sn.k(qq) for qq in range(4)], [okey], eng="gpsimd")
                                            out_keys.append(okey)
                                            items = [(obank[:, q * 128:q * 128 + 80], vtok.ap[0:80, q, :], scm.ap[0:80, q, 0:80],
                                                      True, False, None)]
                                            for s in range(16):
                                                items.append((obank[:, q * 128 + 4 * s:q * 128 + 4 * s + 4], S0b.ap[0:dk, s, :],
                                                              qe.ap[0:dk, hs[q], 4 * s:4 * s + 4], False, s == 15, None))
                                            mm_multi(items, [vtok.k(), scm.k(), S0b.k(), qe.k(hs[q])], [obk])
                                with A.scope():
                                    sqb = A.alloc("sqb", [128, 4, 128], BF16)
                                    rs = A.alloc("rs", [128, 4, 128])
                                    y1 = A.alloc("y1", [128, 4, 128])
                                    o3 = obank[:, :].rearrange("p (q d) -> p q d", q=4)[:, :, 0:nrow]
                                    act(sqb.ap[:, :, 0:nrow], o3, AF.Square, [obk], [sqb.k()])
                                    bank, bk = PS(pool=(0, 1, 2, 3, 4, 5))
                                    mm_multi([(bank[:, q * 128:q * 128 + nrow], onesb, sqb.ap[:, q, 0:nrow], True, True, None)
                                              for q in range(4)], [sqb.k()] + CBK, [bk])
                                    b3 = bank[:, :].rearrange("p (q d) -> p q d", q=4)[:, :, 0:nrow]
                                    act(rs.ap[:, :, 0:nrow], b3, AF.Ln, [bk], [rs.k()], scale=1.0 / 128.0, bias=epsb.ap[:, 0:1])
                                    act(rs.ap[:, :, 0:nrow], rs.ap[:, :, 0:nrow], AF.Exp, [rs.k()], [rs.k()], scale=-0.5)
                                    no, _ = CV_LAY[nwname]
                                    for q in range(4):
                                        stt(y1.ap[:, q, 0:nrow], obank[:, q * 128:q * 128 + nrow], cv.ap[:, no:no + 1],
                                            rs.ap[:, q, 0:nrow], ALU.mult, ALU.mult, [obk, rs.k(), cv.k()], [y1.k(q)])
                                    tt(yT.ap[:, 4 * fam:4 * fam + 4, l0:l0 + nrow], y1.ap[:, :, 0:nrow],
                                       sg.ap[:, 4 * fam:4 * fam + 4, l0:l0 + nrow], ALU.mult,
                                       [y1.k(q) for q in range(4)] + [sg.k(h) for h in hs], [yT.k(fam, t)], eng="gpsimd")
                    YK = [yT.k(fam, t) for fam in range(2) for t in range(ntile)]
                    with A.scope():
                        mix = A.alloc("mix", [128, 8, n])

                        class _HnAlias:
                            ap = hn.ap
                            name = hn.name

                            @staticmethod
                            def k(c):
                                return hn.k(g[1], c)
                        sq = _HnAlias
                        rstd = A.alloc("rstd", [128, n])
                        for oc in range(8):
                            bank, bk = PS()
                            mm(bank[:, 0:n], [(WO.ap[:, hc, oc * 128:(oc + 1) * 128], yT.ap[:, hc, 0:n]) for hc in range(8)],
                               YK + WOK, [bk])
                            copy(mix.ap[:, oc, 0:n], bank[:, 0:n], [bk], [mix.k(oc)])
                        postnorm_add(g, "nmpost", 0, mix, sq, rstd)

        def ffn(layer):
            with A.scope():
                hnF = A.alloc("hnF", [128, 8, Tp], BF16)
                h1 = A.alloc("h1", [128, 22, Tp], BF16)
                with A.scope():
                    sq = A.alloc("sq", [128, 8, 512], BF16)
                    rstd = A.alloc("rstd", [128, 512])
                    for g in groups:
                        prenorm(g, "nfpre", layer, hnF, g[1], sq, rstd)
                with A.scope():
                    wgb = [A.alloc("wgb", [128, 8, 256], BF16) for _ in range(3)]
                    wub = [A.alloc("wub", [128, 8, 256], BF16) for _ in range(3)]
                    sgt = [A.alloc("sgt", [128, 512]) for _ in range(2)]
                    it = 0
                    for jb in range(11):
                        wb, ub = wgb[jb % 3], wub[jb % 3]
                        dma(wb.ap[:, :, :], wg_d[layer, :, jb * 256:(jb + 1) * 256].rearrange("(k p) n -> p k n", p=128),
                            [], [wb.k()], eng="gpsimd")
                        dma(ub.ap[:, :, :], wu_d[layer, :, jb * 256:(jb + 1) * 256].rearrange("(k p) n -> p k n", p=128),
                            [], [ub.k()], eng="gpsimd")
                        for jj in range(2):
                            j = jb * 2 + jj
                            for g in groups:
                                _, c0, n = g
                                hk = [hnF.k(g[1], c) for c in range(8)]
                                gb_, gk = PS()
                                mm(gb_[:, 0:n], [(wb.ap[:, kc, jj * 128:(jj + 1) * 128], hnF.ap[:, kc, c0:c0 + n]) for kc in range(8)],
                                   hk + [wb.k()], [gk])
                                ub_, uk = PS()
                                mm(ub_[:, 0:n], [(ub.ap[:, kc, jj * 128:(jj + 1) * 128], hnF.ap[:, kc, c0:c0 + n]) for kc in range(8)],
                                   hk + [ub.k()], [uk])
                                st_ = sgt[it % 2]
                                it += 1
                                act(st_.ap[:, 0:n], gb_[:, 0:n], AF.Silu, [gk], [st_.k()])
                                tt(h1.ap[:, j, c0:c0 + n], st_.ap[:, 0:n], ub_[:, 0:n], ALU.mult, [st_.k(), uk], [h1.k(j, g[1])])
                with A.scope():
                    mixF = A.alloc("mixF", [128, 8, Tp])
                    wdb = [A.alloc("wdb", [128, 22, 128], BF16) for _ in range(3)]
                    for oc in range(8):
                        wd = wdb[oc % 3]
                        dma(wd.ap[:, :, :], wd_d[layer, :, oc * 128:(oc + 1) * 128].rearrange("(j p) n -> p j n", p=128),
                            [], [wd.k()], eng="gpsimd")
                        for g in groups:
                            _, c0, n = g
                            bank, bk = PS()
                            mm(bank[:, 0:n], [(wd.ap[:, j, :], h1.ap[:, j, c0:c0 + n]) for j in range(22)],
                               [h1.k(j, g[1]) for j in range(22)] + [wd.k()], [bk])
                            copy(mixF.ap[:, oc, c0:c0 + n], bank[:, 0:n], [bk], [mixF.k(g[1], oc)])
                    with A.scope():
                        sq = A.alloc("sq", [128, 8, 512], BF16)
                        rstd = A.alloc("rstd", [128, 512])
                        for g in groups:
                            _, c0, n = g
                            o, _ = CV_LAY["nfpost"]
                            fm_rstd(lambda c: mixF.ap[:, c, c0:c0 + n], 8, n, float(D), [mixF.k(g[1], c) for c in range(8)],
                                    sq, rstd, None)
                            for c in range(8):
                                tt(mixF.ap[:, c, c0:c0 + n], mixF.ap[:, c, c0:c0 + n], rstd.ap[:, 0:n], ALU.mult,
                                   [mixF.k(g[1], c), rstd.k()], [mixF.k(g[1], c)], eng=("gpsimd" if c % 2 else "vector"))
                            for c in range(8):
                                stt(xT.ap[:, c, c0:c0 + n], mixF.ap[:, c, c0:c0 + n],
                                    cv.ap[:, o + layer * 8 + c:o + layer * 8 + c + 1], xT.ap[:, c, c0:c0 + n], ALU.mult, ALU.add,
                                    xkeys_for(g) + [mixF.k(g[1], c), cv.k()], [xT.k(g[1])])

        def store_out():
            with A.scope():
                ost = [A.alloc("ostage", [128, D]) for _ in range(2)]
                units = []
                if has_ms:
                    units.append(("MS", MS0, 80))
                for t in range(npt):
                    units.append(("T", 128 * t, 128))
                for ui, (kind, c0, nr) in enumerate(units):
                    ob = ost[ui % 2]
                    gk = xT.k(MS0) if kind == "MS" else xT.k((c0 // 512) * 512)
                    for half in range(2):
                        bank, bk = PS()
                        transpose_multi([(bank[0:nr, q * 128:(q + 1) * 128], xT.ap[:, half * 4 + q, c0:c0 + nr], identf)
                                         for q in range(4)], XIN_KEYS + [gk, mk.k()], [bk])
                        copy(ob.ap[0:nr, half * 512:(half + 1) * 512], bank[0:nr, :], [bk], [ob.k(half)])
                    if kind == "MS":
                        okey = ("out_ys",)
                        dma(ys_d[:, :], ob.ap[0:64, :], [ob.k(0), ob.k(1)], [okey], eng="sync")
                    else:
                        r0 = (tile0 + c0 // 128) * 128
                        okey = ("out_yp", r0)
                        dma(yp_d[r0:r0 + 128, :], ob.ap[:, :], [ob.k(0), ob.k(1)], [okey], eng=("sync" if ui % 2 else "scalar"))
                    out_keys.append(okey)

        def act_acc(out, in_, func, accum_out, reads, writes):
            P.op("scalar", lambda e: e.activation(out=out, in_=in_, func=func, accum_out=accum_out),
                 reads=reads, writes=writes)

        def layer1_mixer():
            with A.scope():
                negA = A.alloc("negA", [128, 32])
                act(negA.ap[:, :], CV("alog"), AF.Exp, [cv.k()], [negA.k()])
                diagD = A.alloc("diagD", [128, 32, 128], BF16)
                dso_, _ = CV_LAY["dskip"]
                for h_ in range(32):
                    ts(diagD.ap[:, h_, :], identb, cv.ap[:, dso_ + h_:dso_ + h_ + 1], 1.0, ALU.mult, ALU.mult, CBK + [cv.k()],
                       [diagD.k(h_)], eng=("vector" if h_ % 2 else "gpsimd"))
                for g in groups:
                    try:
                        layer1_group(g, negA, diagD)
                    except _Cut:
                        pass

        def layer1_group(g, negA, diagD):
            kind, c0, n = g
            isms = (kind == "MS")
            ntile = 1 if isms else n // 128
            R = 80 if isms else 128
            cwo, _ = CV_LAY["convw"]
            cbo, _ = CV_LAY["convb"]
            onecol = MK("ones")[:, 0:1]
            with A.scope():
                hn = A.alloc("hn1", [128, 8, n], BF16)
                sq = A.alloc("sq1", [128, 8, n], BF16)
                rstd = A.alloc("rstd1", [128, n])
                cutpoint(0.3)
                prenorm(g, "nmpre", 1, hn, 0, sq, rstd)
                cutpoint(0.5)
                HNK = [hn.k(g[1], c) for c in range(8)]
                BT = A.alloc("BT", [128, 4, n], BF16)
                CT = A.alloc("CT", [128, 4, n], BF16)
                xtok = A.alloc("xtok", [128, ntile, 2048], BF16)
                btok = A.alloc("btok", [128, ntile, 512], BF16)
                zs = A.alloc("zs", [128, ntile, 2048], BF16)
                y3T = A.alloc("y3T", [128, 16, n], BF16)
                dtall = A.alloc("dtall", [128, ntile, 32])
                if isms:
                    hsT = A.alloc("hs4", [128, 24, 64], BF16)
                    rawSf = A.alloc("rawSf", [128, 24, 64])
                    with A.scope():
                        stc = A.alloc("stc", [64, 3072])
                        P.op("vector", lambda e: e.memset(stc.ap[:, :], 0.0), writes=[stc.k()])
                        for s_ in range(16):
                            dma(stc.ap[4 * s_ + 1:4 * s_ + 4, :], stc_d[s_, :, :], [stc.k()], [stc.k(1, s_)],
                                eng=("sync" if s_ % 2 else "scalar"))
                        STCK = [stc.k()] + [stc.k(1, s_) for s_ in range(16)]
                        for b6 in range(6):
                            bank, bk = PS()
                            transpose_multi([(bank[:, q * 64:(q + 1) * 64], stc.ap[0:64, (4 * b6 + q) * 128:(4 * b6 + q + 1) * 128],
                                              identf[0:64, 0:64]) for q in range(4)], STCK + [mk.k()], [bk])
                            copy(hsT.ap[:, 4 * b6:4 * b6 + 4, :], bank[:, 0:256].rearrange("p (q r) -> p q r", q=4), [bk],
                                 [hsT.k(b6)])
                cutpoint(1)
                with A.scope():
                    wblk = [A.alloc("wblk", [128, 8, 512], BF16) for _ in range(2)]
                    wdt = A.alloc("wdt", [128, 8, 32], BF16)
                    dma(wdt.ap[:, :, :], win1_d[:, 5120:5152].rearrange("(k p) n -> p k n", p=128), [], [wdt.k()], eng="gpsimd")
                    rawb = [A.alloc("rawb", [128, n + 4], BF16) for _ in range(3)]
                    xcb = [A.alloc("xcb", [128, n], BF16) for _ in range(3)]
                    dgs = [A.alloc("dg", [128, 4, 128], BF16) for _ in range(3)]
                    if isms:
                        rawS = [A.alloc("rawS", [128, 16, 8], BF16) for _ in range(3)]
                        rawM = [A.alloc("rawM", [128, 20], BF16) for _ in range(3)]
                        for rm in rawM:
                            P.op("vector", lambda e, rm=rm: e.memset(rm.ap[:, :], 0.0), writes=[rm.k()])
                    cutpoint(1.2)

                    def load_wblk(b):
                        wb = wblk[b % 2]
                        dma(wb.ap[:, :, :], win1_d[:, 2048 + 512 * b:2048 + 512 * (b + 1)].rearrange("(k p) n -> p k n", p=128),
                            [], [wb.k()], eng="gpsimd")

                    st1 = {}

                    def s1(cc):
                        b, q = cc // 4, cc % 4
                        wb = wblk[b % 2]
                        if q == 0 and b + 1 < 6 and b >= 1:
                            load_wblk(b + 1)
                        bank, bk = PS()
                        mm(bank[:, 0:n], [(wb.ap[:, kc, q * 128:(q + 1) * 128], hn.ap[:, kc, 0:n]) for kc in range(8)],
                           HNK + [wb.k()], [bk])
                        dg = dgs[cc % 3]
                        for k in range(4):
                            ts(dg.ap[:, k, :], identb, cv.ap[:, cwo + k * 24 + cc:cwo + k * 24 + cc + 1], 1.0, ALU.mult, ALU.mult,
                               CBK + [cv.k()], [dg.k(k)], eng="vector")
                        if not isms:
                            rb = rawb[cc % 3]
                            copy(rb.ap[:, 0:4], hist32.ap[:, cc, :], [hist32.k(cc)], [rb.k(0)], eng="vector")
                            copy(rb.ap[:, 4:4 + n], bank[:, 0:n], [bk], [rb.k(1)])
                            copy(hist32.ap[:, cc, :], bank[:, n - 4:n], [bk, rb.k(0)], [hist32.k(cc)], eng="vector")
                        else:
                            rS, rM = rawS[cc % 3], rawM[cc % 3]
                            copy(rS.ap[:, :, 0:4], hsT.ap[:, cc, :].rearrange("p (s k) -> p s k", k=4), [hsT.k(cc // 4)], [rS.k(0)],
                                 eng="vector")
                            copy(rS.ap[:, :, 4:8], bank[:, 0:64].rearrange("p (s j) -> p s j", j=4), [bk], [rS.k(1)], eng="vector")
                            copy(rawSf.ap[:, cc, :], bank[:, 0:64], [bk], [rawSf.k(cc)], eng="scalar")
                            copy(rM.ap[:, 4:20], bank[:, 64:80], [bk], [rM.k()], eng="vector")
                            copy(hist32.ap[:, cc, :], bank[:, 76:80], [bk], [hist32.k(cc)], eng="vector")

                    def s2(cc):
                        dg = dgs[cc % 3]
                        DGK = [dg.k(k) for k in range(4)]
                        cbank, cbk = PS()
                        if not isms:
                            rb = rawb[cc % 3]
                            mm(cbank[:, 0:n], [(dg.ap[:, k, :], rb.ap[:, 1 + k:1 + k + n]) for k in range(4)],
                               DGK + [rb.k(0), rb.k(1)], [cbk])
                        else:
                            rS, rM = rawS[cc % 3], rawM[cc % 3]
                            items = []
                            for k in range(4):
                                items.append((cbank[:, 0:124], dg.ap[:, k, :],
                                              rS.ap[:, :, :].rearrange("p s r -> p (s r)")[:, 1 + k:1 + k + 124],
                                              k == 0, k == 3, None))
                            for k in range(4):
                                items.append((cbank[:, 128:144], dg.ap[:, k, :], rM.ap[:, 1 + k:1 + k + 16], k == 0, k == 3, None))
                            mm_multi(items, DGK + [rS.k(0), rS.k(1), rM.k()], [cbk])
                        if cc < 16:
                            dst, dk_ = xcb[cc % 3].ap[:, 0:n], xcb[cc % 3].k()
                        elif cc < 20:
                            dst, dk_ = BT.ap[:, cc - 16, 0:n], BT.k(cc - 16)
                        else:
                            dst, dk_ = CT.ap[:, cc - 20, 0:n], CT.k(cc - 20)
                        if not isms:
                            act(dst, cbank[:, 0:n], AF.Silu, [cbk, cv.k()], [dk_], bias=cv.ap[:, cbo + cc:cbo + cc + 1])
                        else:
                            act(dst[:, 0:64].rearrange("p (s j) -> p s j", j=4),
                                cbank[:, 0:128].rearrange("p (s r) -> p s r", r=8)[:, :, 0:4], AF.Silu, [cbk, cv.k()], [dk_],
                                bias=cv.ap[:, cbo + cc:cbo + cc + 1])
                            act(dst[:, 64:80], cbank[:, 128:144], AF.Silu, [cbk, cv.k(), dk_], [dk_],
                                bias=cv.ap[:, cbo + cc:cbo + cc + 1])
                        st1[cc] = (dst, dk_)

                    def s3(cc):
                        dst, dk_ = st1.pop(cc)
                        if cc >= 20:
                            return
                        tbank, tbk = PS()
                        tv = bfview(tbank)
                        transpose_multi([(tv[0:R, t * 128:(t + 1) * 128], dst[:, 128 * t:128 * t + R], identb)
                                         for t in range(ntile)], [dk_] + CBK, [tbk])
                        if cc < 16:
                            copy(xtok.ap[0:R, :, cc * 128:(cc + 1) * 128],
                                 tv[0:R, 0:ntile * 128].rearrange("p (t d) -> p t d", t=ntile), [tbk], [xtok.k(cc)])
                        else:
                            copy(btok.ap[0:R, :, (cc - 16) * 128:(cc - 15) * 128],
                                 tv[0:R, 0:ntile * 128].rearrange("p (t d) -> p t d", t=ntile), [tbk], [btok.k(cc - 16)])

                    load_wblk(0)
                    load_wblk(1)
                    for step in range(24 + 2):
                        if step < 24:
                            s1(step)
                        if 0 <= step - 1 < 24:
                            s2(step - 1)
                        if 0 <= step - 2 < 24:
                            s3(step - 2)
                    cutpoint(2)
                    dto, _ = CV_LAY["dtb"]
                    for t in range(ntile):
                        bank, bk = PS()
                        mm(bank[0:R, 0:32], [(hn.ap[:, kc, 128 * t:128 * t + R], wdt.ap[:, kc, :]) for kc in range(8)],
                           HNK + [wdt.k()], [bk])
                        tt(dtall.ap[0:R, t, :], bank[0:R, 0:32], cv.ap[0:R, dto:dto + 32], ALU.add, [bk, cv.k()], [dtall.k(t)])
                    for b in range(4):
                        wb = wblk[b % 2]
                        dma(wb.ap[:, :, :], win1_d[:, 512 * b:512 * (b + 1)].rearrange("(k p) n -> p k n", p=128),
                            [], [wb.k()], eng="gpsimd")
                        for t in range(ntile):
                            bank, bk = PS()
                            mm(bank[0:R, 0:512], [(hn.ap[:, kc, 128 * t:128 * t + R], wb.ap[:, kc, :]) for kc in range(8)],
                               HNK + [wb.k()], [bk])
                            act(zs.ap[0:R, t, 512 * b:512 * (b + 1)], bank[0:R, 0:512], AF.Silu, [bk], [zs.k(t, b)])
                cutpoint(3)
                XTK = [xtok.k(cc) for cc in range(16)]
                BTK = [btok.k(q) for q in range(4)]
                with A.scope():
                    nmaskb = ncmsb if isms else ncausb
                    dso, _ = CV_LAY["dskip"]
                    ono, _ = CV_LAY["odnorm"]
                    ybk = [("ps", 4 + gq) for gq in range(4)]
                    smts = [A.alloc("smt", [128, 8, 32]) for _ in range(2)]
                    xws = [A.alloc("xw", [128, 2048], BF16) for _ in range(2)]
                    cbms = [A.alloc("cbm", [128, 4, 128], BF16) for _ in range(2)]
                    yis = A.alloc("yis", [128, 2048])
                    Dgs = [A.alloc("Dg", [128, 128]) for _ in range(4)]
                    Ls = [A.alloc("L", [128, 128]) for _ in range(4)]
                    Ms = [A.alloc("M", [128, 128], BF16) for _ in range(4)]
                    y = A.alloc("y", [128, 2048])
                    ssq = A.alloc("ssq", [128, 8])
                    junk = A.alloc("junk", [128, 512], BF16)
                    y3 = A.alloc("y3", [128, 2048], BF16)

                    def prologue(t):
                        l0 = 128 * t
                        smt, xw, cbm = smts[t % 2], xws[t % 2], cbms[t % 2]
                        dtS, aS, cumS, ncum = smt.ap[0:R, 0, :], smt.ap[0:R, 1, :], smt.ap[0:R, 2, :], smt.ap[0:R, 3, :]
                        ecum, wj, tmp = smt.ap[0:R, 4, :], smt.ap[0:R, 5, :], smt.ap[0:R, 7, :]
                        dec = smt.ap[:, 6, :]
                        act(tmp, dtall.ap[0:R, t, :], AF.Exp, [dtall.k(t)], [smt.k(7)])
                        act(dtS, tmp, AF.Ln, [smt.k(7), mk.k()], [smt.k(0)], bias=onecol[0:R, :])
                        stt(aS, dtS, -1.0, negA.ap[0:R, :], ALU.mult, ALU.mult, [smt.k(0), negA.k()], [smt.k(1)])
                        tri = MK("cms", parts=80, cols=80) if isms else MK("causal")
                        segm = MK("sseg", parts=80, cols=80) if isms else onesf
                        TR = R if isms else 128
                        bank, bk = PS(pool=(0, 1))
                        mm_multi([(bank[0:R, 0:32], tri[0:R, 0:R], aS, True, True, None),
                                  (bank[0:TR, 32:64], segm[0:R, 0:TR], aS, True, True, None)], [smt.k(1), mk.k()], [bk])
                        copy(cumS, bank[0:R, 0:32], [bk], [smt.k(2)], eng="vector")
                        act(ncum, dtS, AF.Ln, [smt.k(0)], [smt.k(3)])
                        tt(ncum, ncum, bank[0:R, 0:32], ALU.subtract, [smt.k(3), bk], [smt.k(3)])
                        act(ecum, cumS, AF.Exp, [smt.k(2)], [smt.k(4)])
                        tt(tmp, bank[0:R, 32:64], cumS, ALU.subtract, [bk, smt.k(2), smt.k(0)], [smt.k(7)])
                        if not isms:
                            act(dec, bank[:, 32:64], AF.Exp, [bk], [smt.k(6)])
                        act(tmp, tmp, AF.Exp, [smt.k(7)], [smt.k(7)])
                        tt(wj, tmp, dtS, ALU.mult, [smt.k(7), smt.k(0)], [smt.k(5)])
                        tt(xw.ap[0:R, :].rearrange("p (h q) -> p h q", q=64),
                           xtok.ap[0:R, t, :].rearrange("p (h q) -> p h q", q=64),
                           wj.unsqueeze(2).to_broadcast([R, 32, 64]), ALU.mult, XTK + [smt.k(5)], [xw.k()])
                        bank, bk = PS(pool=(0, 1))
                        mm_multi([(bank[0:R, gq * 128:gq * 128 + R], BT.ap[:, gq, l0:l0 + R], CT.ap[:, gq, l0:l0 + R], True, True, None)
                                  for gq in range(4)], [BT.k(q) for q in range(4)] + [CT.k(q) for q in range(4)], [bk])
                        copy(cbm.ap[0:R, :, 0:R], bank[0:R, :].rearrange("p (q d) -> p q d", q=4)[:, :, 0:R], [bk], [cbm.k()])

                    def middle(t, part, extras=None):
                        l0 = 128 * t
                        smt, xw, cbm = smts[t % 2], xws[t % 2], cbms[t % 2]
                        dtS, aS, cumS, ncum = smt.ap[0:R, 0, :], smt.ap[0:R, 1, :], smt.ap[0:R, 2, :], smt.ap[0:R, 3, :]
                        ecum = smt.ap[0:R, 4, :]
                        dec = smt.ap[:, 6, :]
                        if part == "pre":
                            cutpoint(4)
                            if isms:
                                sample_ssd(CT, btok, xw, smt, aS, ecum, yis)
                                for gq in range(4):
                                    bank, bk = PS(pool=(0, 1))
                                    mm(bank[:, 0:512], [(btok.ap[64:80, 0, gq * 128:(gq + 1) * 128], xw.ap[64:80, gq * 512:(gq + 1) * 512])],
                                       BTK + [xw.k()], [bk], tile_position=(64, 0))
                                    copy(ST.ap[:, gq * 512:(gq + 1) * 512], bank[:, 0:512], [bk], [ST.k(gq)], eng="vector")
                                    copy(STb.ap[:, gq * 512:(gq + 1) * 512], bank[:, 0:512], [bk], [STb.k(gq)], eng="scalar")
                            else:
                                for gq in range(4):
                                    cs_ = slice(gq * 512, (gq + 1) * 512)
                                    bank, bk = PS(pool=(0, 1))
                                    mm(bank[0:R, 0:512], [(CT.ap[:, gq, l0:l0 + R], STb.ap[:, cs_])], [CT.k(gq), STb.k(gq)], [bk])
                                    tt(yis.ap[0:R, cs_].rearrange("p (h q) -> p h q", q=64),
                                       bank[0:R, 0:512].rearrange("p (h q) -> p h q", q=64),
                                       ecum[:, 8 * gq:8 * gq + 8].unsqueeze(2).to_broadcast([R, 8, 64]), ALU.mult, [bk, smt.k(4)],
                                       [yis.k(gq)])
                                for gq in range(4):
                                    cs_ = slice(gq * 512, (gq + 1) * 512)
                                    bank, bk = PS(pool=(0, 1))
                                    mm(bank[:, 0:512], [(btok.ap[0:R, t, gq * 128:(gq + 1) * 128], xw.ap[0:R, cs_])], BTK + [xw.k()], [bk])
                                    tt(ST.ap[:, cs_].rearrange("p (h q) -> p h q", q=64), ST.ap[:, cs_].rearrange("p (h q) -> p h q", q=64),
                                       dec[:, 8 * gq:8 * gq + 8].unsqueeze(2).to_broadcast([128, 8, 64]), ALU.mult, [ST.k(gq), smt.k(6)],
                                       [ST.k(gq)])
                                    tt(ST.ap[:, cs_], ST.ap[:, cs_], bank[:, 0:512], ALU.add, [ST.k(gq), bk], [ST.k(gq)])
                                    copy(STb.ap[:, cs_], ST.ap[:, cs_], [ST.k(gq)], [STb.k(gq)], eng="scalar")
                            return
                        cutpoint(5)

                        def stageA(h):
                            Dg = Dgs[h % 4]
                            ts(Dg.ap[0:R, 0:R], identf[0:R, 0:R], cumS[:, h:h + 1], 1.0, ALU.mult, ALU.mult, [smt.k(2), mk.k()],
                               [Dg.k()], eng="gpsimd")
                            rbank, rbk = PS(pool=(1, 2, 3))
                            mm_multi([(rbank[0:R, 0:R], onesf[0:R, 0:R], Dg.ap[0:R, 0:R], True, False, None),
                                      (rbank[0:R, 0:R], identb[0:R, 0:R], nmaskb[0:R, 0:R], False, True, None)],
                                     [Dg.k(), mk.k()] + CBK, [rbk])
                            return rbank, rbk

                        def stageB(h, rbank, rbk):
                            gq = h // 8
                            L, M = Ls[h % 4], Ms[h % 4]
                            act(L.ap[0:R, 0:R], rbank[0:R, 0:R], AF.Exp, [rbk, smt.k(3)], [L.k()], bias=ncum[:, h:h + 1])
                            tt(M.ap[0:R, 0:R], L.ap[0:R, 0:R], cbm.ap[0:R, gq, 0:R], ALU.mult, [L.k(), cbm.k()], [M.k()])

                        def stageC(h):
                            gq = h // 8
                            M = Ms[h % 4]
                            mm(banks[4 + gq][0:R, (h % 8) * 64:(h % 8) * 64 + 64],
                               [(M.ap[0:R, 0:R], xtok.ap[0:R, t, h * 64:(h + 1) * 64]),
                                (diagD.ap[0:R, h, 0:R], xtok.ap[0:R, t, h * 64:(h + 1) * 64])],
                               [M.k(), diagD.k(h)] + XTK, [ybk[gq]])

                        DLY = 3
                        rbs = {}
                        for step in range(32 + DLY):
                            if step < 32:
                                rbs[step] = stageA(step)
                            if 0 <= step - 1 < 32:
                                stageB(step - 1, *rbs.pop(step - 1))
                            if 0 <= step - DLY < 32:
                                stageC(step - DLY)
                            for fn_ in (extras or {}).get(step, ()):
                                fn_()
                        cutpoint(6)

                    def epiA(t):
                        for gq in range(4):
                            cs_ = slice(gq * 512, (gq + 1) * 512)
                            if not isms:
                                tt(y.ap[0:R, cs_], banks[4 + gq][0:R, 0:512], yis.ap[0:R, cs_], ALU.add, [ybk[gq], yis.k(gq)], [y.k(gq)])
                            else:
                                tt(y.ap[0:64, cs_], banks[4 + gq][0:64, 0:512], yis.ap[0:64, cs_], ALU.add, [ybk[gq], yis.k(gq)],
                                   [y.k(gq)])
                                copy(y.ap[64:80, cs_], banks[4 + gq][64:80, 0:512], [ybk[gq]], [y.k(gq, 1)], eng="vector")

                    def epi_pieces(t):
                        l0 = 128 * t
                        pcs = {}

                        def gate(gq):
                            cs_ = slice(gq * 512, (gq + 1) * 512)
                            yk = [y.k(gq)] + ([y.k(gq, 1)] if isms else [])
                            tt(y.ap[0:R, cs_], y.ap[0:R, cs_], zs.ap[0:R, t, cs_], ALU.mult, yk + [zs.k(t, gq)], yk, eng="vector")
                            act_acc(junk.ap[0:R, :], y.ap[0:R, cs_], AF.Square, ssq.ap[0:R, gq:gq + 1], yk, [junk.k(), ssq.k(gq)])

                        def rstd_():
                            SSK = [ssq.k(gq) for gq in range(4)]
                            act(ssq.ap[0:R, 4:8], ssq.ap[0:R, 0:4], AF.Ln, SSK, [ssq.k(9)], scale=1.0 / 512.0, bias=epsb.ap[0:R, 0:1])
                            act(ssq.ap[0:R, 4:8], ssq.ap[0:R, 4:8], AF.Exp, [ssq.k(9)], [ssq.k(9)], scale=-0.5)

                        def y3_(gq):
                            cs_ = slice(gq * 512, (gq + 1) * 512)
                            yk = [y.k(gq)] + ([y.k(gq, 1)] if isms else [])
                            stt(y3.ap[0:R, cs_], y.ap[0:R, cs_], ssq.ap[0:R, 4 + gq:5 + gq], cv.ap[0:R, ono + gq * 512:ono + (gq + 1) * 512],
                                ALU.mult, ALU.mult, yk + [ssq.k(9), cv.k()], [y3.k(gq)])

                        def tr_(half):
                            bank, bk = PS(pool=(0,))
                            bv = bfview(bank)
                            transpose_multi([(bv[:, q * R:(q + 1) * R], y3.ap[0:R, (8 * half + q) * 128:(8 * half + q + 1) * 128],
                                              identb[0:R, 0:R]) for q in range(8)], [y3.k(2 * half), y3.k(2 * half + 1)] + CBK, [bk])
                            copy(y3T.ap[:, 8 * half:8 * half + 8, l0:l0 + R], bv[:, 0:8 * R].rearrange("p (q r) -> p q r", q=8), [bk],
                                 [y3T.k(t, half)])
                        for gq in range(4):
                            pcs[2 + 2 * gq] = [lambda gq=gq: gate(gq)]
                        pcs[11] = [rstd_]
                        for gq in range(4):
                            pcs[13 + 2 * gq] = [lambda gq=gq: y3_(gq)]
                        pcs[23] = [lambda: tr_(0)]
                        pcs[28] = [lambda: tr_(1)]
                        return pcs

                    prologue(0)
                    for t in range(ntile):
                        middle(t, "pre")
                        middle(t, "loop", epi_pieces(t - 1) if t > 0 else {})
                        epiA(t)
                        if t + 1 < ntile:
                            prologue(t + 1)
                    last = epi_pieces(ntile - 1)
                    for st_ in sorted(last):
                        for fn_ in last[st_]:
                            fn_()
                cutpoint(8)
                if isms:
                    with A.scope():
                        tokS = A.alloc("tokS", [64, 3072])
                        for b6 in range(6):
                            bank, bk = PS()
                            transpose_multi([(bank[0:64, q * 128:(q + 1) * 128], rawSf.ap[:, 4 * b6 + q, :], identf) for q in range(4)],
                                            [rawSf.k(4 * b6 + q) for q in range(4)] + [mk.k()], [bk])
                            copy(tokS.ap[0:64, 512 * b6:512 * (b6 + 1)], bank[0:64, :], [bk], [tokS.k(b6)])
                        for s in range(16):
                            okey = ("out_cs", s)
                            dma(cs_d[s, :, :], tokS.ap[4 * s + 1:4 * s + 4, :], [tokS.k(b6) for b6 in range(6)], [okey],
                                eng=("sync" if s % 2 else "scalar"))
                            out_keys.append(okey)
                cutpoint(9)
                Y3K = [y3T.k(t, half) for t in range(ntile) for half in range(2)]
                with A.scope():
                    mix = A.alloc("mix1", [128, 8, n])
                    wob = [A.alloc("wob", [128, 16, 128], BF16) for _ in range(3)]
                    for oc in range(8):
                        wo = wob[oc % 3]
                        dma(wo.ap[:, :, :], wout1_d[:, oc * 128:(oc + 1) * 128].rearrange("(k p) n -> p k n", p=128), [], [wo.k()],
                            eng="gpsimd")
                        bank, bk = PS()
                        mm(bank[:, 0:n], [(wo.ap[:, kc, :], y3T.ap[:, kc, 0:n]) for kc in range(16)], Y3K + [wo.k()], [bk])
                        copy(mix.ap[:, oc, 0:n], bank[:, 0:n], [bk], [mix.k(oc)])
                    postnorm_add(g, "nmpost", 1, mix, sq, rstd)

        def sample_ssd(CT, btok, xw, smt, aS, ecum, yis):
            with A.scope():
                CmT = A.alloc("CmT", [128, 4, 1088], BF16)
                P.op("gpsimd", lambda e: e.memset(CmT.ap[:, :, :], 0.0), writes=[CmT.k()])
                for gq in range(4):
                    copy(CmT.ap[:, gq, :].rearrange("p (s r) -> p s r", r=68)[:, :, 0:4],
                         CT.ap[:, gq, 0:64].rearrange("p (s j) -> p s j", j=4), [CT.k(gq), CmT.k()], [CmT.k(gq)], eng="vector")
                CMK = [CmT.k(gq) for gq in range(4)]
                decn = A.alloc("decn", [128, 256])
                with A.scope():
                    aexp = A.alloc("aexp", [64, 2048])
                    copy(aexp.ap[:, :].rearrange("p (h q) -> p h q", q=64), aS[0:64, :].unsqueeze(2).to_broadcast([64, 32, 64]),
                         [smt.k(1)], [aexp.k()], eng="vector")
                    bank, bk = PS(pool=(0, 1))
                    mm_multi([(bank[:, hb * 16:(hb + 1) * 16], aexp.ap[0:64, hb * 128:(hb + 1) * 128], MK("seg", parts=64, cols=16),
                               True, True, None) for hb in range(16)], [aexp.k(), mk.k()], [bk])
                    act(decn.ap[:, :], bank[:, 0:256], AF.Exp, [bk], [decn.k()])
                S0s = [A.alloc("S0s", [128, 16, 128]) for _ in range(3)]
                S0Ts = [A.alloc("S0T", [128, 2048], BF16) for _ in range(2)]
                Sns = [A.alloc("Sns", [128, 16, 128]) for _ in range(2)]
                Bms = [A.alloc("Bm", [64, 512], BF16) for _ in range(2)]
                yk = [("ps", 4 + gq) for gq in range(4)]
                def sload(s):
                    S0 = S0s[s % 3]
                    dma(S0.ap[:, :, :], sts_d[s, :, :, :].rearrange("(hb two) p n -> (two p) hb n", two=2), [], [S0.k()], eng="sync")

                def sfront(s):
                    S0, S0T = S0s[s % 3], S0Ts[s % 2]
                    for qd in range(4):
                        bank, bk = PS(pool=(0, 1))
                        transpose_multi([(bank[:, q * 128:(q + 1) * 128], S0.ap[:, 4 * qd + q, :], identf) for q in range(4)],
                                        [S0.k(), mk.k()], [bk])
                        copy(S0T.ap[:, qd * 512:(qd + 1) * 512], bank[:, :], [bk], [S0T.k(qd)])

                def sback(s):
                    S0, S0T, Sn, Bm = S0s[s % 3], S0Ts[s % 2], Sns[s % 2], Bms[s % 2]
                    mm_multi([(banks[4 + gq][0:64, 0:512], CmT.ap[:, gq, s * 64:(s + 1) * 64], S0T.ap[:, gq * 512:(gq + 1) * 512],
                               s == 0, s == 15, None) for gq in range(4)], CMK + [S0T.k(qd) for qd in range(4)], yk)
                    ts(Bm.ap[:, :], btok.ap[0:64, 0, :], MK("seg", parts=64, cols=16)[:, s:s + 1], 1.0, ALU.mult, ALU.mult,
                       [btok.k(q) for q in range(4)] + [mk.k()], [Bm.k()], eng="gpsimd")
                    for qd in range(4):
                        bank, bk = PS(pool=(2, 3))
                        mm_multi([(bank[:, q * 128:(q + 1) * 128], xw.ap[0:64, (4 * qd + q) * 128:(4 * qd + q + 1) * 128],
                                   Bm.ap[0:64, qd * 128:(qd + 1) * 128], True, True, None) for q in range(4)], [xw.k(), Bm.k()], [bk])
                        for q in range(4):
                            hb = 4 * qd + q
                            stt(Sn.ap[:, hb, :], S0.ap[:, hb, :], decn.ap[:, hb * 16 + s:hb * 16 + s + 1], bank[:, q * 128:(q + 1) * 128],
                                ALU.mult, ALU.add, [S0.k(), decn.k(), bk], [Sn.k(hb)])
                    okey = ("out_ss", s)
                    dma(ss_d[s, :, :, :].rearrange("(hb two) p n -> (two p) hb n", two=2), Sn.ap[:, :, :],
                        [Sn.k(hb) for hb in range(16)], [okey], eng="gpsimd")
                    out_keys.append(okey)

                sload(0)
                sload(1)
                sfront(0)
                for s in range(16):
                    if s + 2 < 16:
                        sload(s + 2)
                    if s + 1 < 16:
                        sfront(s + 1)
                    sback(s)
                for gq in range(4):
                    tt(yis.ap[0:64, gq * 512:(gq + 1) * 512].rearrange("p (h q) -> p h q", q=64),
                       banks[4 + gq][0:64, 0:512].rearrange("p (h q) -> p h q", q=64),
                       ecum[0:64, 8 * gq:8 * gq + 8].unsqueeze(2).to_broadcast([64, 8, 64]), ALU.mult, [yk[gq], smt.k(4)], [yis.k(gq)])

        if not int(os.environ.get("SKIP0", "0")):
            layer0_mixer()
        A.pop()
        if not int(os.environ.get("SKIP0", "0")):
            if stage >= 2:
                ffn(0)
        if stage >= 3:
            layer1_mixer()
        if stage >= 4:
            ffn(1)
        store_out()
        if pi == 1 or True:
            pass

    ST = A.alloc("ST", [128, 2048])
    STb = A.alloc("STb", [128, 2048], BF16)
    hist32 = A.alloc("hist32", [128, 24, 4])
    if int(os.environ.get("SKIP0", "0")):
        for b_ in (S_h, S_g):
            P.op("vector", lambda e, b_=b_: e.memset(b_.ap, 0.0), writes=[b_.k()])
    epsb = A.alloc("epsb", [128, 2])
    P.op("vector", lambda e: e.memset(epsb.ap[:, :], EPS), writes=["epsb_key"])

    A.push()
    run_pass(0)
    A.pop()
    A.push()
    run_pass(1)
    A.pop()

    okey = ("out_hp",)
    dma(hp_d[:, :, :].rearrange("h k v -> k h v"), S_h.ap[:, :, :], [S_h.k()], [okey])
    out_keys.append(okey)
    okey = ("out_gp",)
    dma(gp_d[:, :, :].rearrange("h k v -> k h v"), S_g.ap[:, :, :], [S_g.k()], [okey])
    out_keys.append(okey)

    if stage >= 3:
        A.push()
        spn = A.alloc("spn", [128, 16, 128])
        for qd in range(4):
            bank, bk = PS()
            transpose_multi([(bank[:, q * 128:(q + 1) * 128], ST.ap[:, (4 * qd + q) * 128:(4 * qd + q + 1) * 128], identf)
                             for q in range(4)], [ST.k(g_) for g_ in range(4)] + [mk.k()], [bk])
            copy(spn.ap[:, 4 * qd:4 * qd + 4, :], bank[:, :].rearrange("p (q d) -> p q d", q=4), [bk], [spn.k(qd)])
        okey = ("out_sp",)
        dma(sp_d.rearrange("(hb two) p n -> (two p) hb n", two=2), spn.ap[:, :, :], [spn.k(qd) for qd in range(4)], [okey])
        out_keys.append(okey)
        cpst = A.alloc("cpst", [3, 3072])
        for b6 in range(6):
            bank, bk = PS()
            transpose_multi([(bank[0:3, q * 128:(q + 1) * 128], hist32.ap[:, 4 * b6 + q, 1:4], identf) for q in range(4)],
                            [hist32.k(4 * b6 + q) for q in range(4)] + [mk.k()], [bk])
            copy(cpst.ap[0:3, 512 * b6:512 * (b6 + 1)], bank[0:3, :], [bk], [cpst.k(b6)])
        okey = ("out_cp",)
        dma(cp_d[:, :], cpst.ap[:, :], [cpst.k(b6) for b6 in range(6)], [okey])
        out_keys.append(okey)
        A.pop()
    P.finish_wait("sync", out_keys)
    P.emit(es)
    nc._dbgP = P
    es.close()
    return nc, A.peak


def _pack_cvec(inp):
    cvv = np.zeros((128, CV_N), np.float32)

    def put(name, arr):
        o, w = CV_LAY[name]
        assert arr.shape == (128, w), (name, arr.shape, w)
        cvv[:, o:o + w] = arr

    def fm(v):
        L = v.shape[0]
        return np.ascontiguousarray(v.reshape(L, 8, 128).transpose(2, 0, 1).reshape(128, L * 8))

    put("nmpre", fm(inp["norm_mix_pre"]))
    put("nmpost", fm(inp["norm_mix_post"]))
    put("nfpre", fm(inp["norm_ffn_pre"]))
    put("nfpost", fm(inp["norm_ffn_post"]))
    put("gamma", np.ascontiguousarray(inp["hgrn_gamma"].reshape(3, 4, 128).transpose(2, 0, 1).reshape(128, 12)))
    ba = np.zeros((128, 4), np.float32)
    ba[0:64, :] = inp["ev_b_alpha"][0].reshape(4, 64).T
    put("balpha", ba)
    put("norma", inp["ev_norm_a"][0].reshape(128, 1))
    put("normb", inp["ev_norm_b"][0].reshape(128, 1))
    put("convw", np.ascontiguousarray(inp["od_conv_w"][0].reshape(4, 24, 128).transpose(2, 0, 1).reshape(128, 96)))
    put("convb", np.ascontiguousarray(inp["od_conv_b"][0].reshape(24, 128).T))
    put("dtb", np.broadcast_to(inp["od_dt_bias"][0][None, :], (128, 32)))
    put("alog", np.broadcast_to(inp["od_a_log"][0][None, :], (128, 32)))
    put("dskip", np.broadcast_to(inp["od_d_skip"][0][None, :], (128, 32)))
    put("odnorm", np.broadcast_to(inp["od_norm"][0][None, :], (128, 2048)))
    return cvv


_PROG_CACHE = {}


def kernel(**inputs):
    inp = {k: np.asarray(v) for k, v in inputs.items()}
    SEQ = inp["x_prompt"].shape[1]
    stage = int(inp.pop("_stage", 99)) if "_stage" in inp else 99
    key = (SEQ, stage)
    if key not in _PROG_CACHE:
        _PROG_CACHE[key] = build_program(SEQ, stage)
    nc, _ = _PROG_CACHE[key]
    cvec = _pack_cvec(inp)
    masks = _build_masks()
    f = lambda a: np.ascontiguousarray(a, dtype=np.float32)
    shared = {
        "meta": f(inp["meta_tokens"]), "w_in0": f(inp["ev_w_in"][0]), "w_up": f(inp["ev_w_alpha_up"][0]),
        "w_out0": f(inp["ev_w_out"][0]), "w_in1": f(inp["od_w_in"][0]), "w_out1": f(inp["od_w_out"][0]),
        "w_g": f(inp["ffn_w_gate"]), "w_u": f(inp["ffn_w_up"]), "w_d": f(inp["ffn_w_down"]),
        "cvec": cvec, "masks": masks,
    }
    in_maps = []
    for c in range(8):
        m = dict(shared)
        m["xp"] = f(inp["x_prompt"][c])
        m["xs"] = f(inp["x_sample"][16 * c:16 * c + 16].reshape(64, D))
        m["st_h"] = f(inp["state_hgrn"][0, 16 * c:16 * c + 16])
        m["st_g"] = f(inp["state_gla"][0, 16 * c:16 * c + 16])
        m["st_s"] = f(inp["state_ssm"][0, 16 * c:16 * c + 16])
        m["st_c"] = f(inp["state_conv"][0, 16 * c:16 * c + 16])
        in_maps.append(m)
    if ONECORE:
        res = run_bass_kernel_spmd(nc, in_maps[:1], core_ids=[0])
        R = [res.results[0]] * 8
    else:
        res = run_bass_kernel_spmd(nc, in_maps, core_ids=list(range(8)))
        R = res.results
    cat = lambda k: np.stack([np.asarray(r[k]) for r in R], axis=0)
    y_prompt = cat("yp")
    y_sample = np.concatenate([np.asarray(r["ys"]).reshape(16, 4, D) for r in R], axis=0)
    hgrn_p = cat("hp")[None]
    gla_p = cat("gp")[None]
    ssm_p = cat("sp")[None]
    conv_p = cat("cp")[None]
    hgrn_s = np.concatenate([np.asarray(r["hs"]) for r in R], axis=0)[None]
    gla_s = np.concatenate([np.asarray(r["gs"]) for r in R], axis=0)[None]
    ssm_s = np.concatenate([np.asarray(r["ss"]) for r in R], axis=0)[None]
    conv_s = np.concatenate([np.asarray(r["cs"]) for r in R], axis=0)[None]
    return (y_prompt, y_sample, hgrn_p, gla_p, ssm_p, conv_p, hgrn_s, gla_s, ssm_s, conv_s)
```

```python
import contextlib
import os
import numpy as np
import concourse.bass as bass
import concourse.mybir as mybir
from concourse.bass_utils import run_bass_kernel_spmd

F32 = mybir.dt.float32
BF16 = mybir.dt.bfloat16
AF = mybir.ActivationFunctionType
ALU = mybir.AluOpType

ENGINES = ("sync", "scalar", "gpsimd", "vector", "tensor")
N_DMA_SEMS = 32
EPS = 1e-6
D = 1024
DFF = 2816
IN_EVEN = 3600
IN_ODD = 5152


class _Op:
    __slots__ = ("eng", "fn", "dma", "waits", "sem", "val", "ninc")


class Prog:
    def __init__(self, nc):
        self.nc = nc
        self.ops = []
        self.cnt = {e: 0 for e in ENGINES}
        self.dma_cnt = [0] * N_DMA_SEMS
        self.dma_rr = {"hw": 0, "sw": 0}
        self.last_w = {}
        self.readers = {}
        self.known = {e: {} for e in ENGINES}
        self.op_clock = {}
        self.base_keys = {}
        self.base_deps = {}

    def _need(self, op, dep):
        if dep is None:
            return
        sk, v = dep
        if sk == ("e", "tensor") and op.eng == "tensor":
            return
        kn = self.known[op.eng]
        if kn.get(sk, 0) >= v:
            return
        kn[sk] = v
        op.waits.append((sk, v))
        clk = self.op_clock.get((sk, v))
        if clk:
            for k2, v2 in clk.items():
                if kn.get(k2, 0) < v2:
                    kn[k2] = v2

    def retire_deps(self, bases):
        deps = {}
        for b in bases:
            for k in self.base_keys.get(b, ()):
                lw = self.last_w.get(k)
                if lw is not None:
                    deps[lw[0]] = max(deps.get(lw[0], 0), lw[1])
                for r in self.readers.get(k, ()):
                    deps[r[0]] = max(deps.get(r[0], 0), r[1])
        return deps

    def set_base_deps(self, base, deps):
        if deps:
            cur = self.base_deps.setdefault(base, {})
            for sk, v in deps.items():
                cur[sk] = max(cur.get(sk, 0), v)

    def op(self, eng, fn, reads=(), writes=(), dma=False, ndma=1):
        o = _Op()
        o.eng, o.fn, o.dma, o.waits = eng, fn, dma, []
        reads, writes = list(reads), list(writes)
        for k in reads:
            if isinstance(k, tuple) and k[0] == "ps":
                for r in self.readers.get(k, ()):
                    if r[0] != ("e", eng):
                        self._need(o, r)
        for k in list(reads) + list(writes):
            b = k[0] if isinstance(k, tuple) else k
            self.base_keys.setdefault(b, set()).add(k)
            bd = self.base_deps.get(b)
            if bd:
                for sk, v in bd.items():
                    self._need(o, (sk, v))
        for k in reads:
            self._need(o, self.last_w.get(k))
        for k in writes:
            self._need(o, self.last_w.get(k))
            for r in self.readers.get(k, ()):
                self._need(o, r)
        if dma:
            half = N_DMA_SEMS // 2
            kind = "sw" if eng == "gpsimd" else "hw"
            s = self.dma_rr[kind] + (half if kind == "sw" else 0)
            self.dma_rr[kind] = (self.dma_rr[kind] + 1) % half
            if self.dma_cnt[s] > 0:
                self._need(o, (("d", s), 16 * self.dma_cnt[s]))
            self.dma_cnt[s] += ndma
            o.sem, o.val, o.ninc = ("d", s), 16 * self.dma_cnt[s], ndma
        else:
            self.cnt[eng] += 1
            o.sem, o.val, o.ninc = ("e", eng), self.cnt[eng], 1
        me = (o.sem, o.val)
        clk = dict(self.known[eng])
        if not dma:
            clk[me[0]] = me[1]
        self.op_clock[me] = clk
        for k in writes:
            self.last_w[k] = me
            self.readers[k] = []
        for k in reads:
            self.readers.setdefault(k, []).append(me)
        self.ops.append(o)
        return o

    def finish_wait(self, eng, keys):
        o = _Op()
        o.eng, o.fn, o.dma, o.waits = eng, None, False, []
        for k in keys:
            self._need(o, self.last_w.get(k))
        o.sem = None
        self.ops.append(o)

    def emit(self, es):
        nc = self.nc
        sems = {}
        for e in ENGINES:
            sems[("e", e)] = es.enter_context(nc.semaphore("se_" + e))
        for i in range(N_DMA_SEMS):
            sems[("d", i)] = es.enter_context(nc.semaphore("sd_%d" % i))
        block = es.enter_context(nc.Block())
        per = {e: [o for o in self.ops if o.eng == e] for e in ENGINES}

        def run(eh, ops):
            for o in ops:
                for sk, v in o.waits:
                    eh.wait_ge(sems[sk], v)
                if o.fn is None:
                    continue
                r = o.fn(eh)
                if o.dma:
                    if not isinstance(r, (list, tuple)):
                        r = [r]
                    assert len(r) == o.ninc, (len(r), o.ninc)
                    for ins in r:
                        ins.then_inc(sems[o.sem], 16)
                else:
                    if isinstance(r, (list, tuple)):
                        r = r[-1]
                    r.then_inc(sems[o.sem], 1)

        @block.sync
        def _(e):
            run(e, per["sync"])

        @block.scalar
        def _(e):
            run(e, per["scalar"])

        @block.gpsimd
        def _(e):
            run(e, per["gpsimd"])

        @block.vector
        def _(e):
            run(e, per["vector"])

        @block.tensor
        def _(e):
            run(e, per["tensor"])


class _Cut(Exception):
    pass


CUT = float(os.environ.get("L1CUT", "99"))
ONECORE = int(os.environ.get("K1CORE", "0"))


def cutpoint(k):
    if CUT <= k:
        raise _Cut()


class Buf:
    __slots__ = ("ap", "name")

    def __init__(self, ap, name):
        self.ap, self.name = ap, name

    def k(self, *idx):
        return (self.name,) + idx if idx else self.name


class Arena:
    def __init__(self, P, big, nbytes):
        self.P, self.big, self.nbytes = P, big, nbytes
        self.off = 0
        self.stack = []
        self.live = []
        self.retired = []
        self.uid = 0
        self.peak = 0

    def alloc(self, name, shape, dt=F32):
        self.uid += 1
        name = "%s#%d" % (name, self.uid)
        esz = 4 if dt == F32 else 2
        n = 1
        for s in shape[1:]:
            n *= s
        nb = (n * esz + 63) // 64 * 64
        st = self.off
        self.off += nb
        self.peak = max(self.peak, self.off)
        assert self.off <= self.nbytes, ("SBUF arena overflow", name, self.off)
        ap = self.big[0:shape[0], st // 4:(st + n * esz + 3) // 4]
        if dt != F32:
            ap = ap.bitcast(dt)
            if n % 2:
                ap = ap[:, 0:n]
        if len(shape) == 3:
            ap = ap.rearrange("p (a b) -> p a b", a=shape[1])
        elif len(shape) == 4:
            ap = ap.rearrange("p (a b c) -> p a b c", a=shape[1], b=shape[2])
        deps = {}
        for (rs, re, rd) in self.retired:
            if rs < st + nb and st < re:
                for sk, v in rd.items():
                    deps[sk] = max(deps.get(sk, 0), v)
        self.P.set_base_deps(name, deps)
        self.live.append((st, st + nb, name))
        return Buf(ap, name)

    def push(self):
        self.stack.append((self.off, len(self.live)))

    def pop(self):
        off, nl = self.stack.pop()
        for (st, en, name) in self.live[nl:]:
            self.retired.append((st, en, self.P.retire_deps([name])))
        del self.live[nl:]
        self.off = off

    @contextlib.contextmanager
    def scope(self):
        self.push()
        try:
            yield
        finally:
            self.pop()


def _cvec_layout():
    names = [("nmpre", 16), ("nmpost", 16), ("nfpre", 16), ("nfpost", 16), ("gamma", 12), ("balpha", 4),
             ("norma", 1), ("normb", 1), ("convw", 96), ("convb", 24), ("dtb", 32), ("alog", 32),
             ("dskip", 32), ("odnorm", 2048)]
    off, lay = 0, {}
    for n, w in names:
        lay[n] = (off, w)
        off += w
    return lay, off


CV_LAY, CV_N = _cvec_layout()


def _mask_layout():
    names = [("identf", 128), ("ones", 128), ("mask64", 512), ("maskms", 80), ("bd64", 128), ("causal", 128),
             ("cms", 80), ("seg", 16), ("ncausal", 128), ("ncms", 80), ("sseg", 80)]
    off, lay = 0, {}
    for n, w in names:
        lay[n] = (off, w)
        off += w
    return lay, off


MK_LAY, MK_N = _mask_layout()


def _build_masks():
    m = np.zeros((128, MK_N), np.float32)

    def put(name, arr):
        o, w = MK_LAY[name]
        m[:arr.shape[0], o:o + arr.shape[1]] = arr

    put("identf", np.eye(128, dtype=np.float32))
    put("ones", np.ones((128, 128), np.float32))
    r = np.ones((128, 512), np.float32)
    r[:, ::64] = 0.0
    put("mask64", r)
    r = np.ones((128, 80), np.float32)
    r[:, 0:64:4] = 0.0
    r[:, 64] = 0.0
    put("maskms", r)
    j = np.arange(128)[:, None]
    i = np.arange(128)[None, :]
    causal = (i >= j).astype(np.float32)
    put("causal", causal)
    put("bd64", causal * ((i // 64) == (j // 64)))
    seg_id = np.concatenate([np.arange(64) // 4, np.full(16, 16)])
    cms = ((seg_id[:, None] == seg_id[None, :]) & (np.arange(80)[None, :] >= np.arange(80)[:, None])).astype(np.float32)
    put("cms", cms)
    seg = np.zeros((80, 16), np.float32)
    seg[np.arange(64), np.arange(64) // 4] = 1.0
    put("seg", seg)
    put("ncausal", (causal - 1.0) * 30000.0)
    put("ncms", (cms - 1.0) * 30000.0)
    put("sseg", (seg_id[:, None] == seg_id[None, :]).astype(np.float32))
    return m


def build_program(SEQ, stage=99):
    NT = SEQ // 128
    n0 = NT // 2
    n1 = NT - n0
    nc = bass.Bass("TRN2", target_bir_lowering=False)

    def din(name, shape, dt=F32):
        return nc.dram_tensor(name, shape, dt, kind="ExternalInput").ap()

    def dout(name, shape):
        return nc.dram_tensor(name, shape, F32, kind="ExternalOutput").ap()

    xp_d = din("xp", [SEQ, D])
    xs_d = din("xs", [64, D])
    meta_d = din("meta", [16, D])
    sth_d = din("st_h", [16, 4, 128, 128])
    stg_d = din("st_g", [16, 4, 64, 128])
    sts_d = din("st_s", [16, 32, 64, 128])
    stc_d = din("st_c", [16, 3, 3072])
    win0_d = din("w_in0", [D, IN_EVEN])
    wup_d = din("w_up", [16, 256])
    wout0_d = din("w_out0", [D, D])
    win1_d = din("w_in1", [D, IN_ODD])
    wout1_d = din("w_out1", [2048, D])
    wg_d = din("w_g", [2, D, DFF])
    wu_d = din("w_u", [2, D, DFF])
    wd_d = din("w_d", [2, DFF, D])
    cvec_d = din("cvec", [128, CV_N])
    mask_d = din("masks", [128, MK_N])

    yp_d = dout("yp", [SEQ, D])
    ys_d = dout("ys", [64, D])
    hp_d = dout("hp", [4, 128, 128])
    gp_d = dout("gp", [4, 64, 128])
    sp_d = dout("sp", [32, 64, 128])
    cp_d = dout("cp", [3, 3072])
    hs_d = dout("hs", [16, 4, 128, 128])
    gs_d = dout("gs", [16, 4, 64, 128])
    ss_d = dout("ss", [16, 32, 64, 128])
    cs_d = dout("cs", [16, 3, 3072])
    out_keys = []

    es = contextlib.ExitStack()
    ARENA_BYTES = 212000
    big = es.enter_context(nc.sbuf_tensor("arena", [128, ARENA_BYTES // 4], F32))
    banks = [es.enter_context(nc.psum_tensor("bank%d" % i, [128, 512], F32)) for i in range(8)]
    P = Prog(nc)
    A = Arena(P, big, ARENA_BYTES)

    ps_state = {"i": 0}

    def PS(pool=(0, 1, 2, 3, 4, 5, 6, 7)):
        ps_state["i"] += 1
        b = pool[ps_state["i"] % len(pool)]
        return banks[b], ("ps", b)

    def bfview(bank):
        return bank[:, :].bitcast(BF16)

    rr = {"dmaq": 0, "ev": 0}

    def dma(out, in_, reads, writes, eng=None):
        if eng is None:
            eng = "sync"
        P.op(eng, lambda e: e.dma_start(out=out, in_=in_), reads=reads, writes=writes, dma=True)

    def act(out, in_, func, reads, writes, scale=1.0, bias=0.0):
        reads = list(reads)
        if not isinstance(bias, float):
            reads.append("epsb_key")
        P.op("scalar", lambda e: e.activation(out=out, in_=in_, func=func, scale=scale, bias=bias),
             reads=reads, writes=writes)

    def tt(out, in0, in1, op, reads, writes, eng="vector"):
        P.op(eng, lambda e: e.tensor_tensor(out=out, in0=in0, in1=in1, op=op), reads=reads, writes=writes)

    def ts(out, in0, s1, s2, op0, op1, reads, writes, eng="vector"):
        P.op(eng, lambda e: e.tensor_scalar(out=out, in0=in0, scalar1=s1, scalar2=s2, op0=op0, op1=op1),
             reads=reads, writes=writes)

    def stt(out, in0, scalar, in1, op0, op1, reads, writes):
        P.op("vector", lambda e: e.scalar_tensor_tensor(out=out, in0=in0, scalar=scalar, in1=in1, op0=op0, op1=op1),
             reads=reads, writes=writes)

    def copy(out, in_, reads, writes, eng=None):
        if eng is None:
            rr["ev"] += 1
            eng = "vector" if rr["ev"] % 2 else "scalar"
        if eng == "scalar":
            act(out, in_, AF.Copy, reads, writes)
        else:
            P.op(eng, lambda e: e.tensor_copy(out=out, in_=in_), reads=reads, writes=writes)

    def mm(out, pairs, reads, writes, tile_position=None, first=True, last=True):
        def fn(e):
            r = None
            n = len(pairs)
            for i, (l, rh) in enumerate(pairs):
                kw = {}
                if tile_position is not None:
                    kw["tile_position"] = tile_position
                r = e.matmul(out, lhsT=l, rhs=rh, start=(first and i == 0), stop=(last and i == n - 1), **kw)
            return r
        P.op("tensor", fn, reads=reads, writes=writes)

    def mm_multi(items, reads, writes):
        def fn(e):
            r = None
            for (o, l, rh, st, sp, tp) in items:
                kw = {}
                if tp is not None:
                    kw["tile_position"] = tp
                r = e.matmul(o, lhsT=l, rhs=rh, start=st, stop=sp, **kw)
            return r
        P.op("tensor", fn, reads=reads, writes=writes)

    def transpose_multi(items, reads, writes):
        def fn(e):
            r = None
            for (o, i_, idn) in items:
                r = e.transpose(out=o, in_=i_, identity=idn)
            return r
        P.op("tensor", fn, reads=reads, writes=writes)

    cv = A.alloc("cvec", [128, CV_N])
    mk = A.alloc("masks", [128, MK_N])
    dma(cv.ap, cvec_d[:, :], [], [cv.k()])
    dma(mk.ap, mask_d[:, :], [], [mk.k()])

    def CV(name, parts=128):
        o, w = CV_LAY[name]
        return cv.ap[0:parts, o:o + w]

    def MK(name, parts=128, cols=None):
        o, w = MK_LAY[name]
        if cols is not None:
            w = cols
        return mk.ap[0:parts, o:o + w]

    cb = A.alloc("cbf", [128, 128 * 4 + 16], BF16)
    identb = cb.ap[:, 0:128]
    onesb = cb.ap[:, 128:256]
    ncausb = cb.ap[:, 256:384]
    ncmsb = cb.ap[:, 384:464]
    segb = cb.ap[:, 512:528]
    copy(identb, MK("identf"), [mk.k()], [cb.k(0)], eng="vector")
    copy(onesb, MK("ones"), [mk.k()], [cb.k(1)], eng="vector")
    copy(ncausb, MK("ncausal"), [mk.k()], [cb.k(2)], eng="vector")
    copy(ncmsb, MK("ncms"), [mk.k()], [cb.k(3)], eng="vector")
    copy(segb, MK("seg"), [mk.k()], [cb.k(4)], eng="vector")
    CBK = [cb.k(i) for i in range(5)]
    identf = MK("identf")
    onesf = MK("ones")

    lbb = A.alloc("lb", [128, 16])
    g_o, _ = CV_LAY["gamma"]
    gam = cv.ap[:, g_o:g_o + 12].rearrange("p (l h) -> p l h", l=3)
    eg = lbb.ap[:, 0:12].rearrange("p (l h) -> p l h", l=3)
    act(lbb.ap[:, 0:12], cv.ap[:, g_o:g_o + 12], AF.Exp, [cv.k()], [lbb.k()])
    sm = A.alloc("lbtmp", [128, 8])
    tt(sm.ap[:, 0:4], eg[:, 0, :], eg[:, 1, :], ALU.add, [lbb.k()], [sm.k()])
    tt(sm.ap[:, 0:4], sm.ap[:, 0:4], eg[:, 2, :], ALU.add, [lbb.k(), sm.k()], [sm.k()])
    P.op("vector", lambda e: e.reciprocal(out=sm.ap[:, 4:8], in_=sm.ap[:, 0:4]), reads=[sm.k()], writes=[sm.k()])
    LB = lbb.ap[:, 12:16]
    tt(LB, eg[:, 0, :], sm.ap[:, 4:8], ALU.mult, [lbb.k(), sm.k()], [lbb.k()])
    OML = sm.ap[:, 0:4]
    ts(OML, LB, -1.0, 1.0, ALU.mult, ALU.add, [lbb.k(), sm.k()], [sm.k()])
    LBK = [lbb.k(), sm.k()]

    S_h = A.alloc("S_h", [128, 4, 128])
    S_g = A.alloc("S_g", [64, 4, 128])
    Sbf_h = A.alloc("Sbf_h", [128, 4, 128], BF16)
    Sbf_g = A.alloc("Sbf_g", [64, 4, 128], BF16)

    def run_pass(pi):
        npt = n0 if pi == 0 else n1
        tile0 = 0 if pi == 0 else n0
        PC = 128 * npt
        has_ms = (pi == 0)
        Tp = PC + (80 if has_ms else 0)
        MS0 = PC
        groups = []
        c = 0
        while c < PC:
            n = min(512, PC - c)
            groups.append(("P", c, n))
            c += n
        if has_ms:
            groups = [("MS", MS0, 80)] + groups

        xT = A.alloc("xT", [128, 8, Tp])
        A.push()
        W0 = A.alloc("W0", [128, 8, IN_EVEN], BF16)
        for kc in range(8):
            dma(W0.ap[:, kc, :], win0_d[kc * 128:(kc + 1) * 128, :], [], [W0.k(kc)], eng="gpsimd")
        W0K = [W0.k(kc) for kc in range(8)]
        wup = A.alloc("wup", [16, 256], BF16)
        dma(wup.ap[:, :], wup_d[:, :], [], [wup.k()], eng="gpsimd")
        XK = lambda g: xT.k(g)

        def gkey(buf, g):
            return buf.k(g[1])

        with A.scope():
            stg = [A.alloc("instage", [128, D]) for _ in range(2)]
            units = []
            if has_ms:
                units.append(("MS", MS0, 80))
            for t in range(npt):
                units.append(("T", 128 * t, 128))
            for ui, (kind, c0, nr) in enumerate(units):
                sb = stg[ui % 2]
                if kind == "MS":
                    dma(sb.ap[0:64, :], xs_d[:, :], [], [sb.k()])
                    dma(sb.ap[64:80, :], meta_d[:, :], [], [sb.k(1)], eng="scalar")
                    rk = [sb.k(), sb.k(1)]
                else:
                    r0 = (tile0 + c0 // 128) * 128
                    dma(sb.ap[:, :], xp_d[r0:r0 + 128, :], [], [sb.k()], eng=("sync" if ui % 2 else "scalar"))
                    rk = [sb.k()]
                gk = xT.k(("MS", MS0) if kind == "MS" else (c0 // 512) * 512)
                for half in range(2):
                    bank, bk = PS()
                    transpose_multi([(bank[:, q * 128:q * 128 + nr], sb.ap[0:nr, (half * 4 + q) * 128:(half * 4 + q + 1) * 128],
                                      identf[0:nr, 0:nr]) for q in range(4)], rk + [mk.k()], [bk])
                    copy(xT.ap[:, half * 4:half * 4 + 4, c0:c0 + nr],
                         bank[:, :].rearrange("p (q c) -> p q c", q=4)[:, :, 0:nr], [bk], [(xT.name, "in", ui, half)])
            XIN_KEYS = [(xT.name, "in", ui, h) for ui in range(len(units)) for h in range(2)]

        def xkeys_for(g):
            return XIN_KEYS + [xT.k(g[1])]

        def fm_rstd(src_fn, nchunks, n, scale_div, reads, sq, rstd, tagk):
            for c in range(nchunks):
                act(sq.ap[:, c, 0:n], src_fn(c), AF.Square, reads, [sq.k(c)])
            bank, bk = PS()
            mm(bank[:, 0:n], [(onesb, sq.ap[:, c, 0:n]) for c in range(nchunks)],
               [sq.k(c) for c in range(nchunks)] + CBK, [bk])
            act(rstd.ap[:, 0:n], bank[:, 0:n], AF.Ln, [bk], [rstd.k()], scale=1.0 / scale_div, bias=epsb.ap[:, 0:1])
            act(rstd.ap[:, 0:n], rstd.ap[:, 0:n], AF.Exp, [rstd.k()], [rstd.k()], scale=-0.5)

        def prenorm(g, wname, layer, hn, hn_c0, sq, rstd):
            kind, c0, n = g
            o, _ = CV_LAY[wname]
            fm_rstd(lambda c: xT.ap[:, c, c0:c0 + n], 8, n, float(D), xkeys_for(g), sq, rstd, None)
            for c in range(8):
                stt(hn.ap[:, c, hn_c0:hn_c0 + n], xT.ap[:, c, c0:c0 + n], cv.ap[:, o + layer * 8 + c:o + layer * 8 + c + 1],
                    rstd.ap[:, 0:n], ALU.mult, ALU.mult, xkeys_for(g) + [rstd.k(), cv.k()], [hn.k(g[1], c)])

        def postnorm_add(g, wname, layer, mix, sq, rstd):
            kind, c0, n = g
            o, _ = CV_LAY[wname]
            fm_rstd(lambda c: mix.ap[:, c, 0:n], 8, n, float(D), [mix.k(c) for c in range(8)], sq, rstd, None)
            for c in range(8):
                tt(mix.ap[:, c, 0:n], mix.ap[:, c, 0:n], rstd.ap[:, 0:n], ALU.mult, [mix.k(c), rstd.k()], [mix.k(c)],
                   eng=("gpsimd" if c % 2 else "vector"))
            for c in range(8):
                stt(xT.ap[:, c, c0:c0 + n], mix.ap[:, c, 0:n], cv.ap[:, o + layer * 8 + c:o + layer * 8 + c + 1],
                    xT.ap[:, c, c0:c0 + n], ALU.mult, ALU.add, xkeys_for(g) + [mix.k(c), cv.k()], [xT.k(g[1])])

        def layer0_mixer():
            for g in groups:
                layer0_group(g, W0, W0K, None, None, wup)

        def layer0_group(g, W0, W0K, WO, WOK, wup):
            kind, c0, n = g
            isms = (kind == "MS")
            ntile = 1 if isms else n // 128
            nrow = 80 if isms else 128
            with A.scope():
                hn = A.alloc("hn", [128, 8, n], BF16)
                yT = A.alloc("yT", [128, 8, n], BF16)
                with A.scope():
                    sq = A.alloc("sq", [128, 8, n], BF16)
                    rstd = A.alloc("rstd", [128, n])
                    prenorm(g, "nmpre", 0, hn, 0, sq, rstd)
                HNK = [hn.k(g[1], c) for c in range(8)]

                def proj_fm(col0, m, nn=n):
                    bank, bk = PS()
                    mm(bank[0:m, 0:nn], [(W0.ap[:, kc, col0:col0 + m], hn.ap[:, kc, 0:nn]) for kc in range(8)],
                       HNK + W0K, [bk])
                    return bank, bk

                with A.scope():
                    sg = A.alloc("sg", [128, 8, n], BF16)
                    qe = A.alloc("qe", [128, 8, n], BF16)
                    ke = A.alloc("ke", [128, 8, n], BF16)
                    Eall = A.alloc("Eall", [128, 8, 20])
                    alow = A.alloc("alow", [16, n], BF16)
                    with A.scope():
                        G1 = A.alloc("G1", [128, 8, n])
                        for h in range(8):
                            col = (1536 + 128 * h) if h < 4 else (3072 + 128 * (h - 4))
                            bank, bk = proj_fm(col, 128)
                            act(sg.ap[:, h, 0:n], bank[:, 0:n], AF.Silu, [bk], [sg.k(h)])
                        for h in range(4):
                            bank, bk = proj_fm(512 + 128 * h, 128)
                            act(G1.ap[:, h, 0:n], bank[:, 0:n], AF.Sigmoid, [bk], [G1.k(h)])
                        bank, bk = proj_fm(3584, 16)
                        copy(alow.ap[:, 0:n], bank[0:16, 0:n], [bk], [alow.k()], eng="vector")
                        bo, _ = CV_LAY["balpha"]
                        for h in range(4):
                            bank, bk = PS()
                            mm(bank[0:64, 0:n], [(wup.ap[0:16, 64 * h:64 * h + 64], alow.ap[0:16, 0:n])],
                               [wup.k(), alow.k()], [bk])
                            act(G1.ap[0:64, 4 + h, 0:n], bank[0:64, 0:n], AF.Sigmoid, [bk, cv.k()], [G1.k(4 + h)],
                                bias=cv.ap[0:64, bo + h:bo + h + 1])
                        rmask = MK("maskms", cols=80) if isms else MK("mask64", cols=n)
                        gsets = [[A.alloc(nm, [128, n]) for nm in ("lf", "cum", "eq", "ek")] for _ in range(2)]
                        for h in range(8):
                            dk = 128 if h < 4 else 64
                            if True:
                                lf, cum, eq, ek = gsets[h % 2]
                                if h < 4:
                                    ts(G1.ap[:, h, 0:n], G1.ap[:, h, 0:n], OML[:, h:h + 1], LB[:, h:h + 1], ALU.mult, ALU.add,
                                       [G1.k(h)] + LBK, [G1.k(h)])
                                    act(lf.ap[:, 0:n], G1.ap[:, h, 0:n], AF.Ln, [G1.k(h)], [lf.k()])
                                    ts(G1.ap[:, h, 0:n], G1.ap[:, h, 0:n], -1.0, 1.0, ALU.mult, ALU.add, [G1.k(h), lf.k()], [G1.k(h)])
                                    esc = 1.0
                                else:
                                    act(lf.ap[0:dk, 0:n], G1.ap[0:dk, h, 0:n], AF.Ln, [G1.k(h)], [lf.k()])
                                    esc = 1.0 / 16.0
                                if isms or h < 4:
                                    P.op("vector", lambda e, cum=cum, lf=lf, dk=dk: e.tensor_tensor_scan(
                                        out=cum.ap[0:dk, 0:n], data0=rmask[0:dk, 0:n], data1=lf.ap[0:dk, 0:n], initial=0.0,
                                        op0=ALU.mult, op1=ALU.add), reads=[lf.k(), mk.k()], writes=[cum.k()])
                                else:
                                    def scan_fn(e, cum=cum, lf=lf, dk=dk):
                                        r_ = None
                                        for t_ in range(n // 128):
                                            r_ = e.tensor_tensor_scan(out=cum.ap[0:dk, 128 * t_:128 * t_ + 128],
                                                                      data0=onesf[0:dk, 0:128], data1=lf.ap[0:dk, 128 * t_:128 * t_ + 128],
                                                                      initial=0.0, op0=ALU.mult, op1=ALU.add)
                                        return r_
                                    P.op("vector", scan_fn, reads=[lf.k(), mk.k()], writes=[cum.k()])
                                act(eq.ap[0:dk, 0:n], cum.ap[0:dk, 0:n], AF.Exp, [cum.k()], [eq.k()], scale=esc)
                                act(ek.ap[0:dk, 0:n], cum.ap[0:dk, 0:n], AF.Exp, [cum.k()], [ek.k()], scale=-esc)
                                if isms:
                                    copy(Eall.ap[0:dk, h, 0:16], eq.ap[0:dk, 0:64].rearrange("p (s j) -> p s j", j=4)[:, :, 3], [eq.k()], [Eall.k(h)], eng="vector")
                                    copy(Eall.ap[0:dk, h, 16:17], eq.ap[0:dk, 79:80], [eq.k()], [Eall.k(h)], eng="vector")
                                else:
                                    CHh = 64 if h < 4 else 128
                                    copy(Eall.ap[0:dk, h, 0:n // CHh], eq.ap[0:dk, 0:n].rearrange("p (c j) -> p c j", j=CHh)[:, :, CHh - 1], [eq.k()], [Eall.k(h)], eng="vector")
                                if h < 4:
                                    bank, bk = proj_fm(128 * h, 128)
                                    tt(qe.ap[:, h, 0:n], bank[:, 0:n], eq.ap[:, 0:n], ALU.mult, [bk, eq.k()], [qe.k(h)])
                                    tt(ke.ap[:, h, 0:n], G1.ap[:, h, 0:n], ek.ap[:, 0:n], ALU.mult, [G1.k(h), ek.k()], [ke.k(h)])
                                else:
                                    bank, bk = proj_fm(2048 + 64 * (h - 4), 64)
                                    stt(qe.ap[0:64, h, 0:n], bank[0:64, 0:n], 0.125, eq.ap[0:64, 0:n], ALU.mult, ALU.mult,
                                        [bk, eq.k()], [qe.k(h)])
                                    bank, bk = proj_fm(2304 + 64 * (h - 4), 64)
                                    tt(ke.ap[0:64, h, 0:n], bank[0:64, 0:n], ek.ap[0:64, 0:n], ALU.mult, [bk, ek.k()], [ke.k(h)])

                    WO = A.alloc("WO0", [128, 8, D], BF16)
                    for kc in range(8):
                        dma(WO.ap[:, kc, :], wout0_d[kc * 128:(kc + 1) * 128, :], [], [WO.k(kc)], eng="gpsimd")
                    WOK = [WO.k(kc) for kc in range(8)]
                    for t in range(ntile):
                        l0 = 128 * t
                        for fam in range(2):
                            dk = 128 if fam == 0 else 64
                            S = S_h if fam == 0 else S_g
                            Sbf = Sbf_h if fam == 0 else Sbf_g
                            nwname = "norma" if fam == 0 else "normb"
                            vcol = 1024 if fam == 0 else 2560
                            hs = [4 * fam + q for q in range(4)]
                            with A.scope():
                                ktok = A.alloc("ktok", [128, 4, 128], BF16)
                                vtok = A.alloc("vtok", [128, 4, 128], BF16)
                                scm = A.alloc("scm", [128, 4, 128], BF16)
                                Sst = A.alloc("Sst", [128, 4, 4, 128], BF16)
                                Ttmp = A.alloc("Ttmp", [128, 4, 128])
                                bank, bk = PS()
                                bv = bfview(bank)
                                transpose_multi([(bv[0:nrow, q * dk:(q + 1) * dk], ke.ap[0:dk, hs[q], l0:l0 + nrow],
                                                  identb[0:dk, 0:dk]) for q in range(4)],
                                                [ke.k(h) for h in hs] + CBK, [bk])
                                copy(ktok.ap[0:nrow, :, 0:dk], bv[0:nrow, 0:4 * dk].rearrange("p (q d) -> p q d", q=4),
                                     [bk], [ktok.k()])
                                bank, bk = PS()
                                mm(bank[0:nrow, 0:512], [(hn.ap[:, kc, l0:l0 + nrow], W0.ap[:, kc, vcol:vcol + 512]) for kc in range(8)],
                                   HNK + W0K, [bk])
                                copy(vtok.ap[0:nrow, :, :], bank[0:nrow, :].rearrange("p (q d) -> p q d", q=4), [bk], [vtok.k()])
                                bank, bk = PS()
                                mm_multi([(bank[0:nrow, q * 128:q * 128 + nrow], ke.ap[0:dk, hs[q], l0:l0 + nrow],
                                           qe.ap[0:dk, hs[q], l0:l0 + nrow], True, True, None) for q in range(4)],
                                         [ke.k(h) for h in hs] + [qe.k(h) for h in hs], [bk])
                                cmask = MK("cms", parts=80, cols=80) if isms else (MK("bd64") if fam == 0 else MK("causal"))
                                CH = 64 if fam == 0 else 128
                                nch = 128 // CH
                                tt(scm.ap[0:nrow, :, 0:nrow], bank[0:nrow, :].rearrange("p (q d) -> p q d", q=4)[:, :, 0:nrow],
                                   cmask.unsqueeze(1).to_broadcast([nrow, 4, nrow]), ALU.mult, [bk, mk.k()], [scm.k()])

                                if not isms:
                                    for c in range(nch):
                                        if c == 0:
                                            copy(Sst.ap[0:dk, 0, :, :], Sbf.ap[0:dk, :, :], [Sbf.k()], [Sst.k(0)], eng="gpsimd")
                                        bank, bk = PS()
                                        mm_multi([(bank[0:dk, q * 128:(q + 1) * 128], ktok.ap[CH * c:CH * c + CH, q, 0:dk],
                                                   vtok.ap[CH * c:CH * c + CH, q, :], True, True, (CH * c, 0)) for q in range(4)],
                                                 [ktok.k(), vtok.k()], [bk])
                                        tt(Ttmp.ap[0:dk, :, :], S.ap[0:dk, :, :], bank[0:dk, :].rearrange("p (q d) -> p q d", q=4),
                                           ALU.add, [S.k(), bk], [Ttmp.k()])
                                        ci = t * nch + c
                                        tt(S.ap[0:dk, :, :], Ttmp.ap[0:dk, :, :],
                                           Eall.ap[0:dk, 4 * fam:4 * fam + 4, ci:ci + 1].to_broadcast([dk, 4, 128]),
                                           ALU.mult, [Ttmp.k()] + [Eall.k(h) for h in hs], [S.k()])
                                        if c < nch - 1:
                                            copy(Sst.ap[0:dk, c + 1, :, :], S.ap[0:dk, :, :], [S.k()], [Sst.k(c + 1)], eng="scalar")
                                        else:
                                            copy(Sbf.ap[0:dk, :, :], S.ap[0:dk, :, :], [S.k()], [Sbf.k()], eng="scalar")
                                    obank, obk = PS()
                                    items = []
                                    for q in range(4):
                                        items.append((obank[:, q * 128:(q + 1) * 128], vtok.ap[:, q, :], scm.ap[:, q, :], True, False, None))
                                        for c in range(nch):
                                            items.append((obank[:, q * 128 + CH * c:q * 128 + CH * c + CH], Sst.ap[0:dk, c, q, :],
                                                          qe.ap[0:dk, hs[q], l0 + CH * c:l0 + CH * c + CH], False, c == nch - 1, None))
                                    mm_multi(items, [vtok.k(), scm.k()] + [Sst.k(c) for c in range(nch)] + [qe.k(h) for h in hs], [obk])
                                else:
                                    bank, bk = PS()
                                    mm_multi([(bank[0:dk, q * 128:(q + 1) * 128], ktok.ap[64:80, q, 0:dk], vtok.ap[64:80, q, :],
                                               True, True, (64, 0)) for q in range(4)], [ktok.k(), vtok.k()], [bk])
                                    tt(S.ap[0:dk, :, :], bank[0:dk, :].rearrange("p (q d) -> p q d", q=4),
                                       Eall.ap[0:dk, 4 * fam:4 * fam + 4, 16:17].to_broadcast([dk, 4, 128]), ALU.mult,
                                       [bk] + [Eall.k(h) for h in hs], [S.k()])
                                    copy(Sbf.ap[0:dk, :, :], S.ap[0:dk, :, :], [S.k()], [Sbf.k()], eng="scalar")
                                    obank, obk = PS(pool=(6, 7))
                                    st_d = sth_d if fam == 0 else stg_d
                                    so_d = hs_d if fam == 0 else gs_d
                                    S0s_ = [A.alloc("S0", [128, 16, 128]) for _ in range(2)]
                                    S0bs_ = [A.alloc("S0b", [128, 16, 128], BF16) for _ in range(2)]
                                    Vbds_ = [A.alloc("Vbd", [64, 16, 128], BF16)] * 2
                                    Sns_ = [A.alloc("Sn", [128, 16, 128]) for _ in range(2)]

                                    def s0_load(q_):
                                        dma(S0s_[q_ % 2].ap[0:dk, :, :], st_d[:, q_, :, :].rearrange("s k v -> k s v"), [],
                                            [S0s_[q_ % 2].k()], eng="sync")
                                    s0_load(0)
                                    for q in range(4):
                                        if True:
                                            S0, S0b, Vbd, Sn = S0s_[q % 2], S0bs_[q % 2], Vbds_[q % 2], Sns_[q % 2]
                                            if q + 1 < 4:
                                                s0_load(q + 1)
                                            copy(S0b.ap[0:dk, :, :], S0.ap[0:dk, :, :], [S0.k()], [S0b.k()], eng="gpsimd")
                                            tt(Vbd.ap[:, :, :], vtok.ap[0:64, q, :].unsqueeze(1).to_broadcast([64, 16, 128]),
                                               segb[0:64, 0:16].unsqueeze(2).to_broadcast([64, 16, 128]), ALU.mult,
                                               [vtok.k()] + CBK, [Vbd.k()])
                                            for qq in range(4):
                                                bank, bk = PS(pool=(0, 1, 2, 3, 4, 5))
                                                mm(bank[0:dk, 0:512], [(ktok.ap[0:64, q, 0:dk],
                                                                        Vbd.ap[:, 4 * qq:4 * qq + 4, :].rearrange("p s d -> p (s d)"))],
                                                   [ktok.k(), Vbd.k()], [bk])
                                                tt(Sn.ap[0:dk, 4 * qq:4 * qq + 4, :], S0.ap[0:dk, 4 * qq:4 * qq + 4, :],
                                                   bank[0:dk, :].rearrange("p (s d) -> p s d", s=4), ALU.add, [S0.k(), bk], [Sn.k(qq)])
                                                tt(Sn.ap[0:dk, 4 * qq:4 * qq + 4, :], Sn.ap[0:dk, 4 * qq:4 * qq + 4, :],
                                                   Eall.ap[0:dk, hs[q], 4 * qq:4 * qq + 4].unsqueeze(2).to_broadcast([dk, 4, 128]),
                                                   ALU.mult, [Sn.k(qq), Eall.k(hs[q])], [Sn.k(qq)])
                                            okey = ("out_s", fam, q)
                                            dma(so_d[:, q, :, :].rearrange("s k v -> k s v"), Sn.ap[0:dk, :, :],
                                                [Sn.k(qq) for qq in range(4)], [okey], eng="gpsimd")
                                            out_keys.append(okey)
                                            items = [(obank[:, q * 128:q * 128 + 80], vtok.ap[0:80, q, :], scm.ap[0:80, q, 0:80],
                                                      True, False, None)]
                                            for s in range(16):
                                                items.append((obank[:, q * 128 + 4 * s:q * 128 + 4 * s + 4], S0b.ap[0:dk, s, :],
                                                              qe.ap[0:dk, hs[q], 4 * s:4 * s + 4], False, s == 15, None))
                                            mm_multi(items, [vtok.k(), scm.k(), S0b.k(), qe.k(hs[q])], [obk])
                                with A.scope():
                                    sqb = A.alloc("sqb", [128, 4, 128], BF16)
                                    rs = A.alloc("rs", [128, 4, 128])
                                    y1 = A.alloc("y1", [128, 4, 128])
                                    o3 = obank[:, :].rearrange("p (q d) -> p q d", q=4)[:, :, 0:nrow]
                                    act(sqb.ap[:, :, 0:nrow], o3, AF.Square, [obk], [sqb.k()])
                                    bank, bk = PS(pool=(0, 1, 2, 3, 4, 5))
                                    mm_multi([(bank[:, q * 128:q * 128 + nrow], onesb, sqb.ap[:, q, 0:nrow], True, True, None)
                                              for q in range(4)], [sqb.k()] + CBK, [bk])
                                    b3 = bank[:, :].rearrange("p (q d) -> p q d", q=4)[:, :, 0:nrow]
                                    act(rs.ap[:, :, 0:nrow], b3, AF.Ln, [bk], [rs.k()], scale=1.0 / 128.0, bias=epsb.ap[:, 0:1])
                                    act(rs.ap[:, :, 0:nrow], rs.ap[:, :, 0:nrow], AF.Exp, [rs.k()], [rs.k()], scale=-0.5)
                                    no, _ = CV_LAY[nwname]
                                    for q in range(4):
                                        stt(y1.ap[:, q, 0:nrow], obank[:, q * 128:q * 128 + nrow], cv.ap[:, no:no + 1],
                                            rs.ap[:, q, 0:nrow], ALU.mult, ALU.mult, [obk, rs.k(), cv.k()], [y1.k(q)])
                                    tt(yT.ap[:, 4 * fam:4 * fam + 4, l0:l0 + nrow], y1.ap[:, :, 0:nrow],
                                       sg.ap[:, 4 * fam:4 * fam + 4, l0:l0 + nrow], ALU.mult,
                                       [y1.k(q) for q in range(4)] + [sg.k(h) for h in hs], [yT.k(fam, t)], eng="gpsimd")
                    YK = [yT.k(fam, t) for fam in range(2) for t in range(ntile)]
                    with A.scope():
                        mix = A.alloc("mix", [128, 8, n])

                        class _HnAlias:
                            ap = hn.ap
                            name = hn.name

                            @staticmethod
                            def k(c):
                                return hn.k(g[1], c)
                        sq = _HnAlias
                        rstd = A.alloc("rstd", [128, n])
                        for oc in range(8):
                            bank, bk = PS()
                            mm(bank[:, 0:n], [(WO.ap[:, hc, oc * 128:(oc + 1) * 128], yT.ap[:, hc, 0:n]) for hc in range(8)],
                               YK + WOK, [bk])
                            copy(mix.ap[:, oc, 0:n], bank[:, 0:n], [bk], [mix.k(oc)])
                        postnorm_add(g, "nmpost", 0, mix, sq, rstd)

        def ffn(layer):
            with A.scope():
                hnF = A.alloc("hnF", [128, 8, Tp], BF16)
                h1 = A.alloc("h1", [128, 22, Tp], BF16)
                with A.scope():
                    sq = A.alloc("sq", [128, 8, 512], BF16)
                    rstd = A.alloc("rstd", [128, 512])
                    for g in groups:
                        prenorm(g, "nfpre", layer, hnF, g[1], sq, rstd)
                with A.scope():
                    wgb = [A.alloc("wgb", [128, 8, 256], BF16) for _ in range(3)]
                    wub = [A.alloc("wub", [128, 8, 256], BF16) for _ in range(3)]
                    sgt = [A.alloc("sgt", [128, 512]) for _ in range(2)]
                    it = 0
                    for jb in range(11):
                        wb, ub = wgb[jb % 3], wub[jb % 3]
                        dma(wb.ap[:, :, :], wg_d[layer, :, jb * 256:(jb + 1) * 256].rearrange("(k p) n -> p k n", p=128),
                            [], [wb.k()], eng="gpsimd")
                        dma(ub.ap[:, :, :], wu_d[layer, :, jb * 256:(jb + 1) * 256].rearrange("(k p) n -> p k n", p=128),
                            [], [ub.k()], eng="gpsimd")
                        for jj in range(2):
                            j = jb * 2 + jj
                            for g in groups:
                                _, c0, n = g
                                hk = [hnF.k(g[1], c) for c in range(8)]
                                gb_, gk = PS()
                                mm(gb_[:, 0:n], [(wb.ap[:, kc, jj * 128:(jj + 1) * 128], hnF.ap[:, kc, c0:c0 + n]) for kc in range(8)],
                                   hk + [wb.k()], [gk])
                                ub_, uk = PS()
                                mm(ub_[:, 0:n], [(ub.ap[:, kc, jj * 128:(jj + 1) * 128], hnF.ap[:, kc, c0:c0 + n]) for kc in range(8)],
                                   hk + [ub.k()], [uk])
                                st_ = sgt[it % 2]
                                it += 1
                                act(st_.ap[:, 0:n], gb_[:, 0:n], AF.Silu, [gk], [st_.k()])
                                tt(h1.ap[:, j, c0:c0 + n], st_.ap[:, 0:n], ub_[:, 0:n], ALU.mult, [st_.k(), uk], [h1.k(j, g[1])])
                with A.scope():
                    mixF = A.alloc("mixF", [128, 8, Tp])
                    wdb = [A.alloc("wdb", [128, 22, 128], BF16) for _ in range(3)]
                    for oc in range(8):
                        wd = wdb[oc % 3]
                        dma(wd.ap[:, :, :], wd_d[layer, :, oc * 128:(oc + 1) * 128].rearrange("(j p) n -> p j n", p=128),
                            [], [wd.k()], eng="gpsimd")
                        for g in groups:
                            _, c0, n = g
                            bank, bk = PS()
                            mm(bank[:, 0:n], [(wd.ap[:, j, :], h1.ap[:, j, c0:c0 + n]) for j in range(22)],
                               [h1.k(j, g[1]) for j in range(22)] + [wd.k()], [bk])
                            copy(mixF.ap[:, oc, c0:c0 + n], bank[:, 0:n], [bk], [mixF.k(g[1], oc)])
                    with A.scope():
                        sq = A.alloc("sq", [128, 8, 512], BF16)
                        rstd = A.alloc("rstd", [128, 512])
                        for g in groups:
                            _, c0, n = g
                            o, _ = CV_LAY["nfpost"]
                            fm_rstd(lambda c: mixF.ap[:, c, c0:c0 + n], 8, n, float(D), [mixF.k(g[1], c) for c in range(8)],
                                    sq, rstd, None)
                            for c in range(8):
                                tt(mixF.ap[:, c, c0:c0 + n], mixF.ap[:, c, c0:c0 + n], rstd.ap[:, 0:n], ALU.mult,
                                   [mixF.k(g[1], c), rstd.k()], [mixF.k(g[1], c)], eng=("gpsimd" if c % 2 else "vector"))
                            for c in range(8):
                                stt(xT.ap[:, c, c0:c0 + n], mixF.ap[:, c, c0:c0 + n],
                                    cv.ap[:, o + layer * 8 + c:o + layer * 8 + c + 1], xT.ap[:, c, c0:c0 + n], ALU.mult, ALU.add,
                                    xkeys_for(g) + [mixF.k(g[1], c), cv.k()], [xT.k(g[1])])

        def store_out():
            with A.scope():
                ost = [A.alloc("ostage", [128, D]) for _ in range(2)]
                units = []
                if has_ms:
                    units.append(("MS", MS0, 80))
                for t in range(npt):
                    units.append(("T", 128 * t, 128))
                for ui, (kind, c0, nr) in enumerate(units):
                    ob = ost[ui % 2]
                    gk = xT.k(MS0) if kind == "MS" else xT.k((c0 // 512) * 512)
                    for half in range(2):
                        bank, bk = PS()
                        transpose_multi([(bank[0:nr, q * 128:(q + 1) * 128], xT.ap[:, half * 4 + q, c0:c0 + nr], identf)
                                         for q in range(4)], XIN_KEYS + [gk, mk.k()], [bk])
                        copy(ob.ap[0:nr, half * 512:(half + 1) * 512], bank[0:nr, :], [bk], [ob.k(half)])
                    if kind == "MS":
                        okey = ("out_ys",)
                        dma(ys_d[:, :], ob.ap[0:64, :], [ob.k(0), ob.k(1)], [okey], eng="sync")
                    else:
                        r0 = (tile0 + c0 // 128) * 128
                        okey = ("out_yp", r0)
                        dma(yp_d[r0:r0 + 128, :], ob.ap[:, :], [ob.k(0), ob.k(1)], [okey], eng=("sync" if ui % 2 else "scalar"))
                    out_keys.append(okey)

        def act_acc(out, in_, func, accum_out, reads, writes):
            P.op("scalar", lambda e: e.activation(out=out, in_=in_, func=func, accum_out=accum_out),
                 reads=reads, writes=writes)

        def layer1_mixer():
            with A.scope():
                negA = A.alloc("negA", [128, 32])
                act(negA.ap[:, :], CV("alog"), AF.Exp, [cv.k()], [negA.k()])
                diagD = A.alloc("diagD", [128, 32, 128], BF16)
                dso_, _ = CV_LAY["dskip"]
                for h_ in range(32):
                    ts(diagD.ap[:, h_, :], identb, cv.ap[:, dso_ + h_:dso_ + h_ + 1], 1.0, ALU.mult, ALU.mult, CBK + [cv.k()],
                       [diagD.k(h_)], eng=("vector" if h_ % 2 else "gpsimd"))
                for g in groups:
                    try:
                        layer1_group(g, negA, diagD)
                    except _Cut:
                        pass

        def layer1_group(g, negA, diagD):
            kind, c0, n = g
            isms = (kind == "MS")
            ntile = 1 if isms else n // 128
            R = 80 if isms else 128
            cwo, _ = CV_LAY["convw"]
            cbo, _ = CV_LAY["convb"]
            onecol = MK("ones")[:, 0:1]
            with A.scope():
                hn = A.alloc("hn1", [128, 8, n], BF16)
                sq = A.alloc("sq1", [128, 8, n], BF16)
                rstd = A.alloc("rstd1", [128, n])
                cutpoint(0.3)
                prenorm(g, "nmpre", 1, hn, 0, sq, rstd)
                cutpoint(0.5)
                HNK = [hn.k(g[1], c) for c in range(8)]
                BT = A.alloc("BT", [128, 4, n], BF16)
                CT = A.alloc("CT", [128, 4, n], BF16)
                xtok = A.alloc("xtok", [128, ntile, 2048], BF16)
                btok = A.alloc("btok", [128, ntile, 512], BF16)
                zs = A.alloc("zs", [128, ntile, 2048], BF16)
                y3T = A.alloc("y3T", [128, 16, n], BF16)
                dtall = A.alloc("dtall", [128, ntile, 32])
                if isms:
                    hsT = A.alloc("hs4", [128, 24, 64], BF16)
                    rawSf = A.alloc("rawSf", [128, 24, 64])
                    with A.scope():
                        stc = A.alloc("stc", [64, 3072])
                        P.op("vector", lambda e: e.memset(stc.ap[:, :], 0.0), writes=[stc.k()])
                        for s_ in range(16):
                            dma(stc.ap[4 * s_ + 1:4 * s_ + 4, :], stc_d[s_, :, :], [stc.k()], [stc.k(1, s_)],
                                eng=("sync" if s_ % 2 else "scalar"))
                        STCK = [stc.k()] + [stc.k(1, s_) for s_ in range(16)]
                        for b6 in range(6):
                            bank, bk = PS()
                            transpose_multi([(bank[:, q * 64:(q + 1) * 64], stc.ap[0:64, (4 * b6 + q) * 128:(4 * b6 + q + 1) * 128],
                                              identf[0:64, 0:64]) for q in range(4)], STCK + [mk.k()], [bk])
                            copy(hsT.ap[:, 4 * b6:4 * b6 + 4, :], bank[:, 0:256].rearrange("p (q r) -> p q r", q=4), [bk],
                                 [hsT.k(b6)])
                cutpoint(1)
                with A.scope():
                    wblk = [A.alloc("wblk", [128, 8, 512], BF16) for _ in range(2)]
                    wdt = A.alloc("wdt", [128, 8, 32], BF16)
                    dma(wdt.ap[:, :, :], win1_d[:, 5120:5152].rearrange("(k p) n -> p k n", p=128), [], [wdt.k()], eng="gpsimd")
                    rawb = [A.alloc("rawb", [128, n + 4], BF16) for _ in range(3)]
                    xcb = [A.alloc("xcb", [128, n], BF16) for _ in range(3)]
                    dgs = [A.alloc("dg", [128, 4, 128], BF16) for _ in range(3)]
                    if isms:
                        rawS = [A.alloc("rawS", [128, 16, 8], BF16) for _ in range(3)]
                        rawM = [A.alloc("rawM", [128, 20], BF16) for _ in range(3)]
                        for rm in rawM:
                            P.op("vector", lambda e, rm=rm: e.memset(rm.ap[:, :], 0.0), writes=[rm.k()])
                    cutpoint(1.2)

                    def load_wblk(b):
                        wb = wblk[b % 2]
                        dma(wb.ap[:, :, :], win1_d[:, 2048 + 512 * b:2048 + 512 * (b + 1)].rearrange("(k p) n -> p k n", p=128),
                            [], [wb.k()], eng="gpsimd")

                    st1 = {}

                    def s1(cc):
                        b, q = cc // 4, cc % 4
                        wb = wblk[b % 2]
                        if q == 0 and b + 1 < 6 and b >= 1:
                            load_wblk(b + 1)
                        bank, bk = PS()
                        mm(bank[:, 0:n], [(wb.ap[:, kc, q * 128:(q + 1) * 128], hn.ap[:, kc, 0:n]) for kc in range(8)],
                           HNK + [wb.k()], [bk])
                        dg = dgs[cc % 3]
                        for k in range(4):
                            ts(dg.ap[:, k, :], identb, cv.ap[:, cwo + k * 24 + cc:cwo + k * 24 + cc + 1], 1.0, ALU.mult, ALU.mult,
                               CBK + [cv.k()], [dg.k(k)], eng="vector")
                        if not isms:
                            rb = rawb[cc % 3]
                            copy(rb.ap[:, 0:4], hist32.ap[:, cc, :], [hist32.k(cc)], [rb.k(0)], eng="vector")
                            copy(rb.ap[:, 4:4 + n], bank[:, 0:n], [bk], [rb.k(1)])
                            copy(hist32.ap[:, cc, :], bank[:, n - 4:n], [bk, rb.k(0)], [hist32.k(cc)], eng="vector")
                        else:
                            rS, rM = rawS[cc % 3], rawM[cc % 3]
                            copy(rS.ap[:, :, 0:4], hsT.ap[:, cc, :].rearrange("p (s k) -> p s k", k=4), [hsT.k(cc // 4)], [rS.k(0)],
                                 eng="vector")
                            copy(rS.ap[:, :, 4:8], bank[:, 0:64].rearrange("p (s j) -> p s j", j=4), [bk], [rS.k(1)], eng="vector")
                            copy(rawSf.ap[:, cc, :], bank[:, 0:64], [bk], [rawSf.k(cc)], eng="scalar")
                            copy(rM.ap[:, 4:20], bank[:, 64:80], [bk], [rM.k()], eng="vector")
                            copy(hist32.ap[:, cc, :], bank[:, 76:80], [bk], [hist32.k(cc)], eng="vector")

                    def s2(cc):
                        dg = dgs[cc % 3]
                        DGK = [dg.k(k) for k in range(4)]
                        cbank, cbk = PS()
                        if not isms:
                            rb = rawb[cc % 3]
                            mm(cbank[:, 0:n], [(dg.ap[:, k, :], rb.ap[:, 1 + k:1 + k + n]) for k in range(4)],
                               DGK + [rb.k(0), rb.k(1)], [cbk])
                        else:
                            rS, rM = rawS[cc % 3], rawM[cc % 3]
                            items = []
                            for k in range(4):
                                items.append((cbank[:, 0:124], dg.ap[:, k, :],
                                              rS.ap[:, :, :].rearrange("p s r -> p (s r)")[:, 1 + k:1 + k + 124],
                                              k == 0, k == 3, None))
                            for k in range(4):
                                items.append((cbank[:, 128:144], dg.ap[:, k, :], rM.ap[:, 1 + k:1 + k + 16], k == 0, k == 3, None))
                            mm_multi(items, DGK + [rS.k(0), rS.k(1), rM.k()], [cbk])
                        if cc < 16:
                            dst, dk_ = xcb[cc % 3].ap[:, 0:n], xcb[cc % 3].k()
                        elif cc < 20:
                            dst, dk_ = BT.ap[:, cc - 16, 0:n], BT.k(cc - 16)
                        else:
                            dst, dk_ = CT.ap[:, cc - 20, 0:n], CT.k(cc - 20)
                        if not isms:
                            act(dst, cbank[:, 0:n], AF.Silu, [cbk, cv.k()], [dk_], bias=cv.ap[:, cbo + cc:cbo + cc + 1])
                        else:
                            act(dst[:, 0:64].rearrange("p (s j) -> p s j", j=4),
                                cbank[:, 0:128].rearrange("p (s r) -> p s r", r=8)[:, :, 0:4], AF.Silu, [cbk, cv.k()], [dk_],
                                bias=cv.ap[:, cbo + cc:cbo + cc + 1])
                            act(dst[:, 64:80], cbank[:, 128:144], AF.Silu, [cbk, cv.k(), dk_], [dk_],
                                bias=cv.ap[:, cbo + cc:cbo + cc + 1])
                        st1[cc] = (dst, dk_)

                    def s3(cc):
                        dst, dk_ = st1.pop(cc)
                        if cc >= 20:
                            return
                        tbank, tbk = PS()
                        tv = bfview(tbank)
                        transpose_multi([(tv[0:R, t * 128:(t + 1) * 128], dst[:, 128 * t:128 * t + R], identb)
                                         for t in range(ntile)], [dk_] + CBK, [tbk])
                        if cc < 16:
                            copy(xtok.ap[0:R, :, cc * 128:(cc + 1) * 128],
                                 tv[0:R, 0:ntile * 128].rearrange("p (t d) -> p t d", t=ntile), [tbk], [xtok.k(cc)])
                        else:
                            copy(btok.ap[0:R, :, (cc - 16) * 128:(cc - 15) * 128],
                                 tv[0:R, 0:ntile * 128].rearrange("p (t d) -> p t d", t=ntile), [tbk], [btok.k(cc - 16)])

                    load_wblk(0)
                    load_wblk(1)
                    for step in range(24 + 2):
                        if step < 24:
                            s1(step)
                        if 0 <= step - 1 < 24:
                            s2(step - 1)
                        if 0 <= step - 2 < 24:
                            s3(step - 2)
                    cutpoint(2)
                    dto, _ = CV_LAY["dtb"]
                    for t in range(ntile):
                        bank, bk = PS()
                        mm(bank[0:R, 0:32], [(hn.ap[:, kc, 128 * t:128 * t + R], wdt.ap[:, kc, :]) for kc in range(8)],
                           HNK + [wdt.k()], [bk])
                        tt(dtall.ap[0:R, t, :], bank[0:R, 0:32], cv.ap[0:R, dto:dto + 32], ALU.add, [bk, cv.k()], [dtall.k(t)])
                    for b in range(4):
                        wb = wblk[b % 2]
                        dma(wb.ap[:, :, :], win1_d[:, 512 * b:512 * (b + 1)].rearrange("(k p) n -> p k n", p=128),
                            [], [wb.k()], eng="gpsimd")
                        for t in range(ntile):
                            bank, bk = PS()
                            mm(bank[0:R, 0:512], [(hn.ap[:, kc, 128 * t:128 * t + R], wb.ap[:, kc, :]) for kc in range(8)],
                               HNK + [wb.k()], [bk])
                            act(zs.ap[0:R, t, 512 * b:512 * (b + 1)], bank[0:R, 0:512], AF.Silu, [bk], [zs.k(t, b)])
                cutpoint(3)
                XTK = [xtok.k(cc) for cc in range(16)]
                BTK = [btok.k(q) for q in range(4)]
                with A.scope():
                    nmaskb = ncmsb if isms else ncausb
                    dso, _ = CV_LAY["dskip"]
                    ono, _ = CV_LAY["odnorm"]
                    ybk = [("ps", 4 + gq) for gq in range(4)]
                    smts = [A.alloc("smt", [128, 8, 32]) for _ in range(2)]
                    xws = [A.alloc("xw", [128, 2048], BF16) for _ in range(2)]
                    cbms = [A.alloc("cbm", [128, 4, 128], BF16) for _ in range(2)]
                    yis = A.alloc("yis", [128, 2048])
                    Dgs = [A.alloc("Dg", [128, 128]) for _ in range(4)]
                    Ls = [A.alloc("L", [128, 128]) for _ in range(4)]
                    Ms = [A.alloc("M", [128, 128], BF16) for _ in range(4)]
                    y = A.alloc("y", [128, 2048])
                    ssq = A.alloc("ssq", [128, 8])
                    junk = A.alloc("junk", [128, 512], BF16)
                    y3 = A.alloc("y3", [128, 2048], BF16)

                    def prologue(t):
                        l0 = 128 * t
                        smt, xw, cbm = smts[t % 2], xws[t % 2], cbms[t % 2]
                        dtS, aS, cumS, ncum = smt.ap[0:R, 0, :], smt.ap[0:R, 1, :], smt.ap[0:R, 2, :], smt.ap[0:R, 3, :]
                        ecum, wj, tmp = smt.ap[0:R, 4, :], smt.ap[0:R, 5, :], smt.ap[0:R, 7, :]
                        dec = smt.ap[:, 6, :]
                        act(tmp, dtall.ap[0:R, t, :], AF.Exp, [dtall.k(t)], [smt.k(7)])
                        act(dtS, tmp, AF.Ln, [smt.k(7), mk.k()], [smt.k(0)], bias=onecol[0:R, :])
                        stt(aS, dtS, -1.0, negA.ap[0:R, :], ALU.mult, ALU.mult, [smt.k(0), negA.k()], [smt.k(1)])
                        tri = MK("cms", parts=80, cols=80) if isms else MK("causal")
                        segm = MK("sseg", parts=80, cols=80) if isms else onesf
                        TR = R if isms else 128
                        bank, bk = PS(pool=(0, 1))
                        mm_multi([(bank[0:R, 0:32], tri[0:R, 0:R], aS, True, True, None),
                                  (bank[0:TR, 32:64], segm[0:R, 0:TR], aS, True, True, None)], [smt.k(1), mk.k()], [bk])
                        copy(cumS, bank[0:R, 0:32], [bk], [smt.k(2)], eng="vector")
                        act(ncum, dtS, AF.Ln, [smt.k(0)], [smt.k(3)])
                        tt(ncum, ncum, bank[0:R, 0:32], ALU.subtract, [smt.k(3), bk], [smt.k(3)])
                        act(ecum, cumS, AF.Exp, [smt.k(2)], [smt.k(4)])
                        tt(tmp, bank[0:R, 32:64], cumS, ALU.subtract, [bk, smt.k(2), smt.k(0)], [smt.k(7)])
                        if not isms:
                            act(dec, bank[:, 32:64], AF.Exp, [bk], [smt.k(6)])
                        act(tmp, tmp, AF.Exp, [smt.k(7)], [smt.k(7)])
                        tt(wj, tmp, dtS, ALU.mult, [smt.k(7), smt.k(0)], [smt.k(5)])
                        tt(xw.ap[0:R, :].rearrange("p (h q) -> p h q", q=64),
                           xtok.ap[0:R, t, :].rearrange("p (h q) -> p h q", q=64),
                           wj.unsqueeze(2).to_broadcast([R, 32, 64]), ALU.mult, XTK + [smt.k(5)], [xw.k()])
                        bank, bk = PS(pool=(0, 1))
                        mm_multi([(bank[0:R, gq * 128:gq * 128 + R], BT.ap[:, gq, l0:l0 + R], CT.ap[:, gq, l0:l0 + R], True, True, None)
                                  for gq in range(4)], [BT.k(q) for q in range(4)] + [CT.k(q) for q in range(4)], [bk])
                        copy(cbm.ap[0:R, :, 0:R], bank[0:R, :].rearrange("p (q d) -> p q d", q=4)[:, :, 0:R], [bk], [cbm.k()])

                    def middle(t, part):
                        l0 = 128 * t
                        smt, xw, cbm = smts[t % 2], xws[t % 2], cbms[t % 2]
                        dtS, aS, cumS, ncum = smt.ap[0:R, 0, :], smt.ap[0:R, 1, :], smt.ap[0:R, 2, :], smt.ap[0:R, 3, :]
                        ecum = smt.ap[0:R, 4, :]
                        dec = smt.ap[:, 6, :]
                        if part == "pre":
                            cutpoint(4)
                            if isms:
                                sample_ssd(CT, btok, xw, smt, aS, ecum, yis)
                                for gq in range(4):
                                    bank, bk = PS(pool=(0, 1))
                                    mm(bank[:, 0:512], [(btok.ap[64:80, 0, gq * 128:(gq + 1) * 128], xw.ap[64:80, gq * 512:(gq + 1) * 512])],
                                       BTK + [xw.k()], [bk], tile_position=(64, 0))
                                    copy(ST.ap[:, gq * 512:(gq + 1) * 512], bank[:, 0:512], [bk], [ST.k(gq)], eng="vector")
                                    copy(STb.ap[:, gq * 512:(gq + 1) * 512], bank[:, 0:512], [bk], [STb.k(gq)], eng="scalar")
                            else:
                                for gq in range(4):
                                    cs_ = slice(gq * 512, (gq + 1) * 512)
                                    bank, bk = PS(pool=(0, 1))
                                    mm(bank[0:R, 0:512], [(CT.ap[:, gq, l0:l0 + R], STb.ap[:, cs_])], [CT.k(gq), STb.k(gq)], [bk])
                                    tt(yis.ap[0:R, cs_].rearrange("p (h q) -> p h q", q=64),
                                       bank[0:R, 0:512].rearrange("p (h q) -> p h q", q=64),
                                       ecum[:, 8 * gq:8 * gq + 8].unsqueeze(2).to_broadcast([R, 8, 64]), ALU.mult, [bk, smt.k(4)],
                                       [yis.k(gq)])
                                for gq in range(4):
                                    cs_ = slice(gq * 512, (gq + 1) * 512)
                                    bank, bk = PS(pool=(0, 1))
                                    mm(bank[:, 0:512], [(btok.ap[0:R, t, gq * 128:(gq + 1) * 128], xw.ap[0:R, cs_])], BTK + [xw.k()], [bk])
                                    tt(ST.ap[:, cs_].rearrange("p (h q) -> p h q", q=64), ST.ap[:, cs_].rearrange("p (h q) -> p h q", q=64),
                                       dec[:, 8 * gq:8 * gq + 8].unsqueeze(2).to_broadcast([128, 8, 64]), ALU.mult, [ST.k(gq), smt.k(6)],
                                       [ST.k(gq)])
                                    tt(ST.ap[:, cs_], ST.ap[:, cs_], bank[:, 0:512], ALU.add, [ST.k(gq), bk], [ST.k(gq)])
                                    copy(STb.ap[:, cs_], ST.ap[:, cs_], [ST.k(gq)], [STb.k(gq)], eng="scalar")
                            return
                        cutpoint(5)

                        def stageA(h):
                            Dg = Dgs[h % 4]
                            ts(Dg.ap[0:R, 0:R], identf[0:R, 0:R], cumS[:, h:h + 1], 1.0, ALU.mult, ALU.mult, [smt.k(2), mk.k()],
                               [Dg.k()], eng="gpsimd")
                            rbank, rbk = PS(pool=(1, 2, 3))
                            mm_multi([(rbank[0:R, 0:R], onesf[0:R, 0:R], Dg.ap[0:R, 0:R], True, False, None),
                                      (rbank[0:R, 0:R], identb[0:R, 0:R], nmaskb[0:R, 0:R], False, True, None)],
                                     [Dg.k(), mk.k()] + CBK, [rbk])
                            return rbank, rbk

                        def stageB(h, rbank, rbk):
                            gq = h // 8
                            L, M = Ls[h % 4], Ms[h % 4]
                            act(L.ap[0:R, 0:R], rbank[0:R, 0:R], AF.Exp, [rbk, smt.k(3)], [L.k()], bias=ncum[:, h:h + 1])
                            tt(M.ap[0:R, 0:R], L.ap[0:R, 0:R], cbm.ap[0:R, gq, 0:R], ALU.mult, [L.k(), cbm.k()], [M.k()])

                        def stageC(h):
                            gq = h // 8
                            M = Ms[h % 4]
                            mm(banks[4 + gq][0:R, (h % 8) * 64:(h % 8) * 64 + 64],
                               [(M.ap[0:R, 0:R], xtok.ap[0:R, t, h * 64:(h + 1) * 64]),
                                (diagD.ap[0:R, h, 0:R], xtok.ap[0:R, t, h * 64:(h + 1) * 64])],
                               [M.k(), diagD.k(h)] + XTK, [ybk[gq]])

                        DLY = 3
                        rbs = {}
                        for step in range(32 + DLY):
                            if step < 32:
                                rbs[step] = stageA(step)
                            if 0 <= step - 1 < 32:
                                stageB(step - 1, *rbs.pop(step - 1))
                            if 0 <= step - DLY < 32:
                                stageC(step - DLY)
                        cutpoint(6)

                    def epiA(t):
                        for gq in range(4):
                            cs_ = slice(gq * 512, (gq + 1) * 512)
                            if not isms:
                                tt(y.ap[0:R, cs_], banks[4 + gq][0:R, 0:512], yis.ap[0:R, cs_], ALU.add, [ybk[gq], yis.k(gq)], [y.k(gq)])
                            else:
                                tt(y.ap[0:64, cs_], banks[4 + gq][0:64, 0:512], yis.ap[0:64, cs_], ALU.add, [ybk[gq], yis.k(gq)],
                                   [y.k(gq)])
                                copy(y.ap[64:80, cs_], banks[4 + gq][64:80, 0:512], [ybk[gq]], [y.k(gq, 1)], eng="vector")

                    def epiB1(t):
                        for gq in range(4):
                            cs_ = slice(gq * 512, (gq + 1) * 512)
                            yk = [y.k(gq)] + ([y.k(gq, 1)] if isms else [])
                            tt(y.ap[0:R, cs_], y.ap[0:R, cs_], zs.ap[0:R, t, cs_], ALU.mult, yk + [zs.k(t, gq)], yk, eng="gpsimd")
                            act_acc(junk.ap[0:R, :], y.ap[0:R, cs_], AF.Square, ssq.ap[0:R, gq:gq + 1], yk, [junk.k(), ssq.k(gq)])
                        SSK = [ssq.k(gq) for gq in range(4)]
                        act(ssq.ap[0:R, 4:8], ssq.ap[0:R, 0:4], AF.Ln, SSK, [ssq.k(9)], scale=1.0 / 512.0, bias=epsb.ap[0:R, 0:1])
                        act(ssq.ap[0:R, 4:8], ssq.ap[0:R, 4:8], AF.Exp, [ssq.k(9)], [ssq.k(9)], scale=-0.5)
                        for gq in range(4):
                            cs_ = slice(gq * 512, (gq + 1) * 512)
                            yk = [y.k(gq)] + ([y.k(gq, 1)] if isms else [])
                            stt(y3.ap[0:R, cs_], y.ap[0:R, cs_], ssq.ap[0:R, 4 + gq:5 + gq], cv.ap[0:R, ono + gq * 512:ono + (gq + 1) * 512],
                                ALU.mult, ALU.mult, yk + [ssq.k(9), cv.k()], [y3.k(gq)])

                    def epiB2(t):
                        l0 = 128 * t
                        for half in range(2):
                            bank, bk = PS(pool=(0, 1, 2, 3))
                            bv = bfview(bank)
                            transpose_multi([(bv[:, q * R:(q + 1) * R], y3.ap[0:R, (8 * half + q) * 128:(8 * half + q + 1) * 128],
                                              identb[0:R, 0:R]) for q in range(8)], [y3.k(2 * half), y3.k(2 * half + 1)] + CBK, [bk])
                            copy(y3T.ap[:, 8 * half:8 * half + 8, l0:l0 + R], bv[:, 0:8 * R].rearrange("p (q r) -> p q r", q=8), [bk],
                                 [y3T.k(t, half)])
                        cutpoint(7)

                    prologue(0)
                    for t in range(ntile):
                        middle(t, "pre")
                        if t > 0:
                            epiB1(t - 1)
                        middle(t, "loop")
                        if t > 0:
                            epiB2(t - 1)
                        epiA(t)
                        if t + 1 < ntile:
                            prologue(t + 1)
                    epiB1(ntile - 1)
                    epiB2(ntile - 1)
                cutpoint(8)
                if isms:
                    with A.scope():
                        tokS = A.alloc("tokS", [64, 3072])
                        for b6 in range(6):
                            bank, bk = PS()
                            transpose_multi([(bank[0:64, q * 128:(q + 1) * 128], rawSf.ap[:, 4 * b6 + q, :], identf) for q in range(4)],
                                            [rawSf.k(4 * b6 + q) for q in range(4)] + [mk.k()], [bk])
                            copy(tokS.ap[0:64, 512 * b6:512 * (b6 + 1)], bank[0:64, :], [bk], [tokS.k(b6)])
                        for s in range(16):
                            okey = ("out_cs", s)
                            dma(cs_d[s, :, :], tokS.ap[4 * s + 1:4 * s + 4, :], [tokS.k(b6) for b6 in range(6)], [okey],
                                eng=("sync" if s % 2 else "scalar"))
                            out_keys.append(okey)
                cutpoint(9)
                Y3K = [y3T.k(t, half) for t in range(ntile) for half in range(2)]
                with A.scope():
                    mix = A.alloc("mix1", [128, 8, n])
                    wob = [A.alloc("wob", [128, 16, 128], BF16) for _ in range(3)]
                    for oc in range(8):
                        wo = wob[oc % 3]
                        dma(wo.ap[:, :, :], wout1_d[:, oc * 128:(oc + 1) * 128].rearrange("(k p) n -> p k n", p=128), [], [wo.k()],
                            eng="gpsimd")
                        bank, bk = PS()
                        mm(bank[:, 0:n], [(wo.ap[:, kc, :], y3T.ap[:, kc, 0:n]) for kc in range(16)], Y3K + [wo.k()], [bk])
                        copy(mix.ap[:, oc, 0:n], bank[:, 0:n], [bk], [mix.k(oc)])
                    postnorm_add(g, "nmpost", 1, mix, sq, rstd)

        def sample_ssd(CT, btok, xw, smt, aS, ecum, yis):
            with A.scope():
                CmT = A.alloc("CmT", [128, 4, 1088], BF16)
                P.op("gpsimd", lambda e: e.memset(CmT.ap[:, :, :], 0.0), writes=[CmT.k()])
                for gq in range(4):
                    copy(CmT.ap[:, gq, :].rearrange("p (s r) -> p s r", r=68)[:, :, 0:4],
                         CT.ap[:, gq, 0:64].rearrange("p (s j) -> p s j", j=4), [CT.k(gq), CmT.k()], [CmT.k(gq)], eng="vector")
                CMK = [CmT.k(gq) for gq in range(4)]
                decn = A.alloc("decn", [128, 256])
                with A.scope():
                    aexp = A.alloc("aexp", [64, 2048])
                    copy(aexp.ap[:, :].rearrange("p (h q) -> p h q", q=64), aS[0:64, :].unsqueeze(2).to_broadcast([64, 32, 64]),
                         [smt.k(1)], [aexp.k()], eng="vector")
                    bank, bk = PS(pool=(0, 1))
                    mm_multi([(bank[:, hb * 16:(hb + 1) * 16], aexp.ap[0:64, hb * 128:(hb + 1) * 128], MK("seg", parts=64, cols=16),
                               True, True, None) for hb in range(16)], [aexp.k(), mk.k()], [bk])
                    act(decn.ap[:, :], bank[:, 0:256], AF.Exp, [bk], [decn.k()])
                S0s = [A.alloc("S0s", [128, 16, 128]) for _ in range(3)]
                S0Ts = [A.alloc("S0T", [128, 2048], BF16) for _ in range(2)]
                Sns = [A.alloc("Sns", [128, 16, 128]) for _ in range(2)]
                Bms = [A.alloc("Bm", [64, 512], BF16) for _ in range(2)]
                yk = [("ps", 4 + gq) for gq in range(4)]
                def sload(s):
                    S0 = S0s[s % 3]
                    dma(S0.ap[:, :, :], sts_d[s, :, :, :].rearrange("(hb two) p n -> (two p) hb n", two=2), [], [S0.k()], eng="sync")

                def sfront(s):
                    S0, S0T = S0s[s % 3], S0Ts[s % 2]
                    for qd in range(4):
                        bank, bk = PS(pool=(0, 1))
                        transpose_multi([(bank[:, q * 128:(q + 1) * 128], S0.ap[:, 4 * qd + q, :], identf) for q in range(4)],
                                        [S0.k(), mk.k()], [bk])
                        copy(S0T.ap[:, qd * 512:(qd + 1) * 512], bank[:, :], [bk], [S0T.k(qd)])

                def sback(s):
                    S0, S0T, Sn, Bm = S0s[s % 3], S0Ts[s % 2], Sns[s % 2], Bms[s % 2]
                    mm_multi([(banks[4 + gq][0:64, 0:512], CmT.ap[:, gq, s * 64:(s + 1) * 64], S0T.ap[:, gq * 512:(gq + 1) * 512],
                               s == 0, s == 15, None) for gq in range(4)], CMK + [S0T.k(qd) for qd in range(4)], yk)
                    ts(Bm.ap[:, :], btok.ap[0:64, 0, :], MK("seg", parts=64, cols=16)[:, s:s + 1], 1.0, ALU.mult, ALU.mult,
                       [btok.k(q) for q in range(4)] + [mk.k()], [Bm.k()], eng="gpsimd")
                    for qd in range(4):
                        bank, bk = PS(pool=(2, 3))
                        mm_multi([(bank[:, q * 128:(q + 1) * 128], xw.ap[0:64, (4 * qd + q) * 128:(4 * qd + q + 1) * 128],
                                   Bm.ap[0:64, qd * 128:(qd + 1) * 128], True, True, None) for q in range(4)], [xw.k(), Bm.k()], [bk])
                        for q in range(4):
                            hb = 4 * qd + q
                            stt(Sn.ap[:, hb, :], S0.ap[:, hb, :], decn.ap[:, hb * 16 + s:hb * 16 + s + 1], bank[:, q * 128:(q + 1) * 128],
                                ALU.mult, ALU.add, [S0.k(), decn.k(), bk], [Sn.k(hb)])
                    okey = ("out_ss", s)
                    dma(ss_d[s, :, :, :].rearrange("(hb two) p n -> (two p) hb n", two=2), Sn.ap[:, :, :],
                        [Sn.k(hb) for hb in range(16)], [okey], eng="gpsimd")
                    out_keys.append(okey)

                sload(0)
                sload(1)
                sfront(0)
                for s in range(16):
                    if s + 2 < 16:
                        sload(s + 2)
                    if s + 1 < 16:
                        sfront(s + 1)
                    sback(s)
                for gq in range(4):
                    tt(yis.ap[0:64, gq * 512:(gq + 1) * 512].rearrange("p (h q) -> p h q", q=64),
                       banks[4 + gq][0:64, 0:512].rearrange("p (h q) -> p h q", q=64),
                       ecum[0:64, 8 * gq:8 * gq + 8].unsqueeze(2).to_broadcast([64, 8, 64]), ALU.mult, [yk[gq], smt.k(4)], [yis.k(gq)])

        if not int(os.environ.get("SKIP0", "0")):
            layer0_mixer()
        A.pop()
        if not int(os.environ.get("SKIP0", "0")):
            if stage >= 2:
                ffn(0)
        if stage >= 3:
            layer1_mixer()
        if stage >= 4:
            ffn(1)
        store_out()
        if pi == 1 or True:
            pass

    ST = A.alloc("ST", [128, 2048])
    STb = A.alloc("STb", [128, 2048], BF16)
    hist32 = A.alloc("hist32", [128, 24, 4])
    if int(os.environ.get("SKIP0", "0")):
        for b_ in (S_h, S_g):
            P.op("vector", lambda e, b_=b_: e.memset(b_.ap, 0.0), writes=[b_.k()])
    epsb = A.alloc("epsb", [128, 2])
    P.op("vector", lambda e: e.memset(epsb.ap[:, :], EPS), writes=["epsb_key"])

    A.push()
    run_pass(0)
    A.pop()
    A.push()
    run_pass(1)
    A.pop()

    okey = ("out_hp",)
    dma(hp_d[:, :, :].rearrange("h k v -> k h v"), S_h.ap[:, :, :], [S_h.k()], [okey])
    out_keys.append(okey)
    okey = ("out_gp",)
    dma(gp_d[:, :, :].rearrange("h k v -> k h v"), S_g.ap[:, :, :], [S_g.k()], [okey])
    out_keys.append(okey)

    if stage >= 3:
        A.push()
        spn = A.alloc("spn", [128, 16, 128])
        for qd in range(4):
            bank, bk = PS()
            transpose_multi([(bank[:, q * 128:(q + 1) * 128], ST.ap[:, (4 * qd + q) * 128:(4 * qd + q + 1) * 128], identf)
                             for q in range(4)], [ST.k(g_) for g_ in range(4)] + [mk.k()], [bk])
            copy(spn.ap[:, 4 * qd:4 * qd + 4, :], bank[:, :].rearrange("p (q d) -> p q d", q=4), [bk], [spn.k(qd)])
        okey = ("out_sp",)
        dma(sp_d.rearrange("(hb two) p n -> (two p) hb n", two=2), spn.ap[:, :, :], [spn.k(qd) for qd in range(4)], [okey])
        out_keys.append(okey)
        cpst = A.alloc("cpst", [3, 3072])
        for b6 in range(6):
            bank, bk = PS()
            transpose_multi([(bank[0:3, q * 128:(q + 1) * 128], hist32.ap[:, 4 * b6 + q, 1:4], identf) for q in range(4)],
                            [hist32.k(4 * b6 + q) for q in range(4)] + [mk.k()], [bk])
            copy(cpst.ap[0:3, 512 * b6:512 * (b6 + 1)], bank[0:3, :], [bk], [cpst.k(b6)])
        okey = ("out_cp",)
        dma(cp_d[:, :], cpst.ap[:, :], [cpst.k(b6) for b6 in range(6)], [okey])
        out_keys.append(okey)
        A.pop()
    P.finish_wait("sync", out_keys)
    P.emit(es)
    nc._dbgP = P
    es.close()
    return nc, A.peak


def _pack_cvec(inp):
    cvv = np.zeros((128, CV_N), np.float32)

    def put(name, arr):
        o, w = CV_LAY[name]
        assert arr.shape == (128, w), (name, arr.shape, w)
        cvv[:, o:o + w] = arr

    def fm(v):
        L = v.shape[0]
        return np.ascontiguousarray(v.reshape(L, 8, 128).transpose(2, 0, 1).reshape(128, L * 8))

    put("nmpre", fm(inp["norm_mix_pre"]))
    put("nmpost", fm(inp["norm_mix_post"]))
    put("nfpre", fm(inp["norm_ffn_pre"]))
    put("nfpost", fm(inp["norm_ffn_post"]))
    put("gamma", np.ascontiguousarray(inp["hgrn_gamma"].reshape(3, 4, 128).transpose(2, 0, 1).reshape(128, 12)))
    ba = np.zeros((128, 4), np.float32)
    ba[0:64, :] = inp["ev_b_alpha"][0].reshape(4, 64).T
    put("balpha", ba)
    put("norma", inp["ev_norm_a"][0].reshape(128, 1))
    put("normb", inp["ev_norm_b"][0].reshape(128, 1))
    put("convw", np.ascontiguousarray(inp["od_conv_w"][0].reshape(4, 24, 128).transpose(2, 0, 1).reshape(128, 96)))
    put("convb", np.ascontiguousarray(inp["od_conv_b"][0].reshape(24, 128).T))
    put("dtb", np.broadcast_to(inp["od_dt_bias"][0][None, :], (128, 32)))
    put("alog", np.broadcast_to(inp["od_a_log"][0][None, :], (128, 32)))
    put("dskip", np.broadcast_to(inp["od_d_skip"][0][None, :], (128, 32)))
    put("odnorm", np.broadcast_to(inp["od_norm"][0][None, :], (128, 2048)))
    return cvv


_PROG_CACHE = {}


def kernel(**inputs):
    inp = {k: np.asarray(v) for k, v in inputs.items()}
    SEQ = inp["x_prompt"].shape[1]
    stage = int(inp.pop("_stage", 99)) if "_stage" in inp else 99
    key = (SEQ, stage)
    if key not in _PROG_CACHE:
        _PROG_CACHE[key] = build_program(SEQ, stage)
    nc, _ = _PROG_CACHE[key]
    cvec = _pack_cvec(inp)
    masks = _build_masks()
    f = lambda a: np.ascontiguousarray(a, dtype=np.float32)
    shared = {
        "meta": f(inp["meta_tokens"]), "w_in0": f(inp["ev_w_in"][0]), "w_up": f(inp["ev_w_alpha_up"][0]),
        "w_out0": f(inp["ev_w_out"][0]), "w_in1": f(inp["od_w_in"][0]), "w_out1": f(inp["od_w_out"][0]),
        "w_g": f(inp["ffn_w_gate"]), "w_u": f(inp["ffn_w_up"]), "w_d": f(inp["ffn_w_down"]),
        "cvec": cvec, "masks": masks,
    }
    in_maps = []
    for c in range(8):
        m = dict(shared)
        m["xp"] = f(inp["x_prompt"][c])
        m["xs"] = f(inp["x_sample"][16 * c:16 * c + 16].reshape(64, D))
        m["st_h"] = f(inp["state_hgrn"][0, 16 * c:16 * c + 16])
        m["st_g"] = f(inp["state_gla"][0, 16 * c:16 * c + 16])
        m["st_s"] = f(inp["state_ssm"][0, 16 * c:16 * c + 16])
        m["st_c"] = f(inp["state_conv"][0, 16 * c:16 * c + 16])
        in_maps.append(m)
    if ONECORE:
        res = run_bass_kernel_spmd(nc, in_maps[:1], core_ids=[0])
        R = [res.results[0]] * 8
    else:
        res = run_bass_kernel_spmd(nc, in_maps, core_ids=list(range(8)))
        R = res.results
    cat = lambda k: np.stack([np.asarray(r[k]) for r in R], axis=0)
    y_prompt = cat("yp")
    y_sample = np.concatenate([np.asarray(r["ys"]).reshape(16, 4, D) for r in R], axis=0)
    hgrn_p = cat("hp")[None]
    gla_p = cat("gp")[None]
    ssm_p = cat("sp")[None]
    conv_p = cat("cp")[None]
    hgrn_s = np.concatenate([np.asarray(r["hs"]) for r in R], axis=0)[None]
    gla_s = np.concatenate([np.asarray(r["gs"]) for r in R], axis=0)[None]
    ssm_s = np.concatenate([np.asarray(r["ss"]) for r in R], axis=0)[None]
    conv_s = np.concatenate([np.asarray(r["cs"]) for r in R], axis=0)[None]
    return (y_prompt, y_sample, hgrn_p, gla_p, ssm_p, conv_p, hgrn_s, gla_s, ssm_s, conv_s)
```

```python
import contextlib
import os
import numpy as np
import concourse.bass as bass
import concourse.mybir as mybir
from concourse.bass_utils import run_bass_kernel_spmd

F32 = mybir.dt.float32
BF16 = mybir.dt.bfloat16
AF = mybir.ActivationFunctionType
ALU = mybir.AluOpType

ENGINES = ("sync", "scalar", "gpsimd", "vector", "tensor")
N_DMA_SEMS = 32
EPS = 1e-6
D = 1024
DFF = 2816
IN_EVEN = 3600
IN_ODD = 5152


class _Op:
    __slots__ = ("eng", "fn", "dma", "waits", "sem", "val", "ninc")


class Prog:
    def __init__(self, nc):
        self.nc = nc
        self.ops = []
        self.cnt = {e: 0 for e in ENGINES}
        self.dma_cnt = [0] * N_DMA_SEMS
        self.dma_rr = {"hw": 0, "sw": 0}
        self.last_w = {}
        self.readers = {}
        self.known = {e: {} for e in ENGINES}
        self.op_clock = {}
        self.base_keys = {}
        self.base_deps = {}

    def _need(self, op, dep):
        if dep is None:
            return
        sk, v = dep
        if sk == ("e", "tensor") and op.eng == "tensor":
            return
        kn = self.known[op.eng]
        if kn.get(sk, 0) >= v:
            return
        kn[sk] = v
        op.waits.append((sk, v))
        clk = self.op_clock.get((sk, v))
        if clk:
            for k2, v2 in clk.items():
                if kn.get(k2, 0) < v2:
                    kn[k2] = v2

    def retire_deps(self, bases):
        deps = {}
        for b in bases:
            for k in self.base_keys.get(b, ()):
                lw = self.last_w.get(k)
                if lw is not None:
                    deps[lw[0]] = max(deps.get(lw[0], 0), lw[1])
                for r in self.readers.get(k, ()):
                    deps[r[0]] = max(deps.get(r[0], 0), r[1])
        return deps

    def set_base_deps(self, base, deps):
        if deps:
            cur = self.base_deps.setdefault(base, {})
            for sk, v in deps.items():
                cur[sk] = max(cur.get(sk, 0), v)

    def op(self, eng, fn, reads=(), writes=(), dma=False, ndma=1):
        o = _Op()
        o.eng, o.fn, o.dma, o.waits = eng, fn, dma, []
        reads, writes = list(reads), list(writes)
        for k in reads:
            if isinstance(k, tuple) and k[0] == "ps":
                for r in self.readers.get(k, ()):
                    if r[0] != ("e", eng):
                        self._need(o, r)
        for k in list(reads) + list(writes):
            b = k[0] if isinstance(k, tuple) else k
            self.base_keys.setdefault(b, set()).add(k)
            bd = self.base_deps.get(b)
            if bd:
                for sk, v in bd.items():
                    self._need(o, (sk, v))
        for k in reads:
            self._need(o, self.last_w.get(k))
        own = ("e", eng)
        for k in writes:
            lw = self.last_w.get(k)
            if lw is not None and not (lw[0] == own and not dma):
                self._need(o, lw)
            for r in self.readers.get(k, ()):
                if r[0] == own and not dma:
                    continue
                self._need(o, r)
        if dma:
            half = N_DMA_SEMS // 2
            kind = "sw" if eng == "gpsimd" else "hw"
            s = self.dma_rr[kind] + (half if kind == "sw" else 0)
            self.dma_rr[kind] = (self.dma_rr[kind] + 1) % half
            if self.dma_cnt[s] > 0:
                self._need(o, (("d", s), 16 * self.dma_cnt[s]))
            self.dma_cnt[s] += ndma
            o.sem, o.val, o.ninc = ("d", s), 16 * self.dma_cnt[s], ndma
        else:
            self.cnt[eng] += 1
            o.sem, o.val, o.ninc = ("e", eng), self.cnt[eng], 1
        me = (o.sem, o.val)
        clk = dict(self.known[eng])
        if not dma:
            clk[me[0]] = me[1]
        self.op_clock[me] = clk
        for k in writes:
            self.last_w[k] = me
            self.readers[k] = []
        for k in reads:
            self.readers.setdefault(k, []).append(me)
        self.ops.append(o)
        return o

    def finish_wait(self, eng, keys):
        o = _Op()
        o.eng, o.fn, o.dma, o.waits = eng, None, False, []
        for k in keys:
            self._need(o, self.last_w.get(k))
        o.sem = None
        self.ops.append(o)

    def emit(self, es):
        nc = self.nc
        sems = {}
        for e in ENGINES:
            sems[("e", e)] = es.enter_context(nc.semaphore("se_" + e))
        for i in range(N_DMA_SEMS):
            sems[("d", i)] = es.enter_context(nc.semaphore("sd_%d" % i))
        block = es.enter_context(nc.Block())
        per = {e: [o for o in self.ops if o.eng == e] for e in ENGINES}

        def run(eh, ops):
            for o in ops:
                for sk, v in o.waits:
                    eh.wait_ge(sems[sk], v)
                if o.fn is None:
                    continue
                r = o.fn(eh)
                if o.dma:
                    if not isinstance(r, (list, tuple)):
                        r = [r]
                    assert len(r) == o.ninc, (len(r), o.ninc)
                    for ins in r:
                        ins.then_inc(sems[o.sem], 16)
                else:
                    if isinstance(r, (list, tuple)):
                        r = r[-1]
                    r.then_inc(sems[o.sem], 1)

        @block.sync
        def _(e):
            run(e, per["sync"])

        @block.scalar
        def _(e):
            run(e, per["scalar"])

        @block.gpsimd
        def _(e):
            run(e, per["gpsimd"])

        @block.vector
        def _(e):
            run(e, per["vector"])

        @block.tensor
        def _(e):
            run(e, per["tensor"])


class _Cut(Exception):
    pass


CUT = float(os.environ.get("L1CUT", "99"))
ONECORE = int(os.environ.get("K1CORE", "0"))


def cutpoint(k):
    if CUT <= k:
        raise _Cut()


class Buf:
    __slots__ = ("ap", "name")

    def __init__(self, ap, name):
        self.ap, self.name = ap, name

    def k(self, *idx):
        return (self.name,) + idx if idx else self.name


class Arena:
    def __init__(self, P, big, nbytes):
        self.P, self.big, self.nbytes = P, big, nbytes
        self.off = 0
        self.stack = []
        self.live = []
        self.retired = []
        self.uid = 0
        self.peak = 0

    def alloc(self, name, shape, dt=F32):
        self.uid += 1
        name = "%s#%d" % (name, self.uid)
        esz = 4 if dt == F32 else 2
        n = 1
        for s in shape[1:]:
            n *= s
        nb = (n * esz + 63) // 64 * 64
        st = self.off
        self.off += nb
        self.peak = max(self.peak, self.off)
        assert self.off <= self.nbytes, ("SBUF arena overflow", name, self.off)
        ap = self.big[0:shape[0], st // 4:(st + n * esz + 3) // 4]
        if dt != F32:
            ap = ap.bitcast(dt)
            if n % 2:
                ap = ap[:, 0:n]
        if len(shape) == 3:
            ap = ap.rearrange("p (a b) -> p a b", a=shape[1])
        elif len(shape) == 4:
            ap = ap.rearrange("p (a b c) -> p a b c", a=shape[1], b=shape[2])
        deps = {}
        for (rs, re, rd) in self.retired:
            if rs < st + nb and st < re:
                for sk, v in rd.items():
                    deps[sk] = max(deps.get(sk, 0), v)
        self.P.set_base_deps(name, deps)
        self.live.append((st, st + nb, name))
        return Buf(ap, name)

    def push(self):
        self.stack.append((self.off, len(self.live)))

    def pop(self):
        off, nl = self.stack.pop()
        for (st, en, name) in self.live[nl:]:
            self.retired.append((st, en, self.P.retire_deps([name])))
        del self.live[nl:]
        self.off = off

    @contextlib.contextmanager
    def scope(self):
        self.push()
        try:
            yield
        finally:
            self.pop()


def _cvec_layout():
    names = [("nmpre", 16), ("nmpost", 16), ("nfpre", 16), ("nfpost", 16), ("gamma", 12), ("balpha", 4),
             ("norma", 1), ("normb", 1), ("convw", 96), ("convb", 24), ("dtb", 32), ("alog", 32),
             ("dskip", 32), ("odnorm", 2048)]
    off, lay = 0, {}
    for n, w in names:
        lay[n] = (off, w)
        off += w
    return lay, off


CV_LAY, CV_N = _cvec_layout()


def _mask_layout():
    names = [("identf", 128), ("ones", 128), ("mask64", 512), ("maskms", 80), ("bd64", 128), ("causal", 128),
             ("cms", 80), ("seg", 16), ("ncausal", 128), ("ncms", 80), ("sseg", 80)]
    off, lay = 0, {}
    for n, w in names:
        lay[n] = (off, w)
        off += w
    return lay, off


MK_LAY, MK_N = _mask_layout()


def _build_masks():
    m = np.zeros((128, MK_N), np.float32)

    def put(name, arr):
        o, w = MK_LAY[name]
        m[:arr.shape[0], o:o + arr.shape[1]] = arr

    put("identf", np.eye(128, dtype=np.float32))
    put("ones", np.ones((128, 128), np.float32))
    r = np.ones((128, 512), np.float32)
    r[:, ::64] = 0.0
    put("mask64", r)
    r = np.ones((128, 80), np.float32)
    r[:, 0:64:4] = 0.0
    r[:, 64] = 0.0
    put("maskms", r)
    j = np.arange(128)[:, None]
    i = np.arange(128)[None, :]
    causal = (i >= j).astype(np.float32)
    put("causal", causal)
    put("bd64", causal * ((i // 64) == (j // 64)))
    seg_id = np.concatenate([np.arange(64) // 4, np.full(16, 16)])
    cms = ((seg_id[:, None] == seg_id[None, :]) & (np.arange(80)[None, :] >= np.arange(80)[:, None])).astype(np.float32)
    put("cms", cms)
    seg = np.zeros((80, 16), np.float32)
    seg[np.arange(64), np.arange(64) // 4] = 1.0
    put("seg", seg)
    put("ncausal", (causal - 1.0) * 30000.0)
    put("ncms", (cms - 1.0) * 30000.0)
    put("sseg", (seg_id[:, None] == seg_id[None, :]).astype(np.float32))
    return m


def build_program(SEQ, stage=99):
    NT = SEQ // 128
    n0 = NT // 2
    n1 = NT - n0
    nc = bass.Bass("TRN2", target_bir_lowering=False)

    def din(name, shape, dt=F32):
        return nc.dram_tensor(name, shape, dt, kind="ExternalInput").ap()

    def dout(name, shape):
        return nc.dram_tensor(name, shape, F32, kind="ExternalOutput").ap()

    xp_d = din("xp", [SEQ, D])
    xs_d = din("xs", [64, D])
    meta_d = din("meta", [16, D])
    sth_d = din("st_h", [16, 4, 128, 128])
    stg_d = din("st_g", [16, 4, 64, 128])
    sts_d = din("st_s", [16, 32, 64, 128])
    stc_d = din("st_c", [16, 3, 3072])
    win0_d = din("w_in0", [D, IN_EVEN])
    wup_d = din("w_up", [16, 256])
    wout0_d = din("w_out0", [D, D])
    win1_d = din("w_in1", [D, IN_ODD])
    wout1_d = din("w_out1", [2048, D])
    wg_d = din("w_g", [2, D, DFF])
    wu_d = din("w_u", [2, D, DFF])
    wd_d = din("w_d", [2, DFF, D])
    cvec_d = din("cvec", [128, CV_N])
    mask_d = din("masks", [128, MK_N])

    yp_d = dout("yp", [SEQ, D])
    ys_d = dout("ys", [64, D])
    hp_d = dout("hp", [4, 128, 128])
    gp_d = dout("gp", [4, 64, 128])
    sp_d = dout("sp", [32, 64, 128])
    cp_d = dout("cp", [3, 3072])
    hs_d = dout("hs", [16, 4, 128, 128])
    gs_d = dout("gs", [16, 4, 64, 128])
    ss_d = dout("ss", [16, 32, 64, 128])
    cs_d = dout("cs", [16, 3, 3072])
    out_keys = []

    es = contextlib.ExitStack()
    ARENA_BYTES = 212000
    big = es.enter_context(nc.sbuf_tensor("arena", [128, ARENA_BYTES // 4], F32))
    banks = [es.enter_context(nc.psum_tensor("bank%d" % i, [128, 512], F32)) for i in range(8)]
    P = Prog(nc)
    A = Arena(P, big, ARENA_BYTES)

    ps_state = {"i": 0}

    def PS(pool=(0, 1, 2, 3, 4, 5, 6, 7)):
        ps_state["i"] += 1
        b = pool[ps_state["i"] % len(pool)]
        return banks[b], ("ps", b)

    def bfview(bank):
        return bank[:, :].bitcast(BF16)

    rr = {"dmaq": 0, "ev": 0}

    def dma(out, in_, reads, writes, eng=None):
        if eng is None:
            eng = "sync"
        P.op(eng, lambda e: e.dma_start(out=out, in_=in_), reads=reads, writes=writes, dma=True)

    def act(out, in_, func, reads, writes, scale=1.0, bias=0.0):
        reads = list(reads)
        if not isinstance(bias, float):
            reads.append("epsb_key")
        P.op("scalar", lambda e: e.activation(out=out, in_=in_, func=func, scale=scale, bias=bias),
             reads=reads, writes=writes)

    def tt(out, in0, in1, op, reads, writes, eng="vector"):
        P.op(eng, lambda e: e.tensor_tensor(out=out, in0=in0, in1=in1, op=op), reads=reads, writes=writes)

    def ts(out, in0, s1, s2, op0, op1, reads, writes, eng="vector"):
        P.op(eng, lambda e: e.tensor_scalar(out=out, in0=in0, scalar1=s1, scalar2=s2, op0=op0, op1=op1),
             reads=reads, writes=writes)

    def stt(out, in0, scalar, in1, op0, op1, reads, writes):
        P.op("vector", lambda e: e.scalar_tensor_tensor(out=out, in0=in0, scalar=scalar, in1=in1, op0=op0, op1=op1),
             reads=reads, writes=writes)

    def copy(out, in_, reads, writes, eng=None):
        if eng is None:
            rr["ev"] += 1
            eng = "vector" if rr["ev"] % 2 else "scalar"
        if eng == "scalar":
            act(out, in_, AF.Copy, reads, writes)
        else:
            P.op(eng, lambda e: e.tensor_copy(out=out, in_=in_), reads=reads, writes=writes)

    def mm(out, pairs, reads, writes, tile_position=None, first=True, last=True):
        def fn(e):
            r = None
            n = len(pairs)
            for i, (l, rh) in enumerate(pairs):
                kw = {}
                if tile_position is not None:
                    kw["tile_position"] = tile_position
                r = e.matmul(out, lhsT=l, rhs=rh, start=(first and i == 0), stop=(last and i == n - 1), **kw)
            return r
        P.op("tensor", fn, reads=reads, writes=writes)

    def mm_multi(items, reads, writes):
        def fn(e):
            r = None
            for (o, l, rh, st, sp, tp) in items:
                kw = {}
                if tp is not None:
                    kw["tile_position"] = tp
                r = e.matmul(o, lhsT=l, rhs=rh, start=st, stop=sp, **kw)
            return r
        P.op("tensor", fn, reads=reads, writes=writes)

    def transpose_multi(items, reads, writes):
        def fn(e):
            r = None
            for (o, i_, idn) in items:
                r = e.transpose(out=o, in_=i_, identity=idn)
            return r
        P.op("tensor", fn, reads=reads, writes=writes)

    cv = A.alloc("cvec", [128, CV_N])
    mk = A.alloc("masks", [128, MK_N])
    dma(cv.ap, cvec_d[:, :], [], [cv.k()])
    dma(mk.ap, mask_d[:, :], [], [mk.k()])

    def CV(name, parts=128):
        o, w = CV_LAY[name]
        return cv.ap[0:parts, o:o + w]

    def MK(name, parts=128, cols=None):
        o, w = MK_LAY[name]
        if cols is not None:
            w = cols
        return mk.ap[0:parts, o:o + w]

    cb = A.alloc("cbf", [128, 128 * 4 + 16], BF16)
    identb = cb.ap[:, 0:128]
    onesb = cb.ap[:, 128:256]
    ncausb = cb.ap[:, 256:384]
    ncmsb = cb.ap[:, 384:464]
    segb = cb.ap[:, 512:528]
    copy(identb, MK("identf"), [mk.k()], [cb.k(0)], eng="vector")
    copy(onesb, MK("ones"), [mk.k()], [cb.k(1)], eng="vector")
    copy(ncausb, MK("ncausal"), [mk.k()], [cb.k(2)], eng="vector")
    copy(ncmsb, MK("ncms"), [mk.k()], [cb.k(3)], eng="vector")
    copy(segb, MK("seg"), [mk.k()], [cb.k(4)], eng="vector")
    CBK = [cb.k(i) for i in range(5)]
    identf = MK("identf")
    onesf = MK("ones")

    lbb = A.alloc("lb", [128, 16])
    g_o, _ = CV_LAY["gamma"]
    gam = cv.ap[:, g_o:g_o + 12].rearrange("p (l h) -> p l h", l=3)
    eg = lbb.ap[:, 0:12].rearrange("p (l h) -> p l h", l=3)
    act(lbb.ap[:, 0:12], cv.ap[:, g_o:g_o + 12], AF.Exp, [cv.k()], [lbb.k()])
    sm = A.alloc("lbtmp", [128, 8])
    tt(sm.ap[:, 0:4], eg[:, 0, :], eg[:, 1, :], ALU.add, [lbb.k()], [sm.k()])
    tt(sm.ap[:, 0:4], sm.ap[:, 0:4], eg[:, 2, :], ALU.add, [lbb.k(), sm.k()], [sm.k()])
    P.op("vector", lambda e: e.reciprocal(out=sm.ap[:, 4:8], in_=sm.ap[:, 0:4]), reads=[sm.k()], writes=[sm.k()])
    LB = lbb.ap[:, 12:16]
    tt(LB, eg[:, 0, :], sm.ap[:, 4:8], ALU.mult, [lbb.k(), sm.k()], [lbb.k()])
    OML = sm.ap[:, 0:4]
    ts(OML, LB, -1.0, 1.0, ALU.mult, ALU.add, [lbb.k(), sm.k()], [sm.k()])
    LBK = [lbb.k(), sm.k()]

    S_h = A.alloc("S_h", [128, 4, 128])
    S_g = A.alloc("S_g", [64, 4, 128])
    Sbf_h = A.alloc("Sbf_h", [128, 4, 128], BF16)
    Sbf_g = A.alloc("Sbf_g", [64, 4, 128], BF16)

    def run_pass(pi):
        npt = n0 if pi == 0 else n1
        tile0 = 0 if pi == 0 else n0
        PC = 128 * npt
        has_ms = (pi == 0)
        Tp = PC + (80 if has_ms else 0)
        MS0 = PC
        groups = []
        c = 0
        while c < PC:
            n = min(512, PC - c)
            groups.append(("P", c, n))
            c += n
        if has_ms:
            groups = [("MS", MS0, 80)] + groups

        xT = A.alloc("xT", [128, 8, Tp])
        A.push()
        W0 = A.alloc("W0", [128, 8, IN_EVEN], BF16)
        for kc in range(8):
            dma(W0.ap[:, kc, :], win0_d[kc * 128:(kc + 1) * 128, :], [], [W0.k(kc)], eng="gpsimd")
        W0K = [W0.k(kc) for kc in range(8)]
        wup = A.alloc("wup", [16, 256], BF16)
        dma(wup.ap[:, :], wup_d[:, :], [], [wup.k()], eng="gpsimd")
        XK = lambda g: xT.k(g)

        def gkey(buf, g):
            return buf.k(g[1])

        with A.scope():
            stg = [A.alloc("instage", [128, D]) for _ in range(2)]
            units = []
            if has_ms:
                units.append(("MS", MS0, 80))
            for t in range(npt):
                units.append(("T", 128 * t, 128))
            for ui, (kind, c0, nr) in enumerate(units):
                sb = stg[ui % 2]
                if kind == "MS":
                    dma(sb.ap[0:64, :], xs_d[:, :], [], [sb.k()])
                    dma(sb.ap[64:80, :], meta_d[:, :], [], [sb.k(1)], eng="scalar")
                    rk = [sb.k(), sb.k(1)]
                else:
                    r0 = (tile0 + c0 // 128) * 128
                    dma(sb.ap[:, :], xp_d[r0:r0 + 128, :], [], [sb.k()], eng=("sync" if ui % 2 else "scalar"))
                    rk = [sb.k()]
                gk = xT.k(("MS", MS0) if kind == "MS" else (c0 // 512) * 512)
                for half in range(2):
                    bank, bk = PS()
                    transpose_multi([(bank[:, q * 128:q * 128 + nr], sb.ap[0:nr, (half * 4 + q) * 128:(half * 4 + q + 1) * 128],
                                      identf[0:nr, 0:nr]) for q in range(4)], rk + [mk.k()], [bk])
                    copy(xT.ap[:, half * 4:half * 4 + 4, c0:c0 + nr],
                         bank[:, :].rearrange("p (q c) -> p q c", q=4)[:, :, 0:nr], [bk], [(xT.name, "in", ui, half)])
            XIN_KEYS = [(xT.name, "in", ui, h) for ui in range(len(units)) for h in range(2)]

        def xkeys_for(g):
            return XIN_KEYS + [xT.k(g[1])]

        def fm_rstd(src_fn, nchunks, n, scale_div, reads, sq, rstd, tagk):
            for c in range(nchunks):
                act(sq.ap[:, c, 0:n], src_fn(c), AF.Square, reads, [sq.k(c)])
            bank, bk = PS()
            mm(bank[:, 0:n], [(onesb, sq.ap[:, c, 0:n]) for c in range(nchunks)],
               [sq.k(c) for c in range(nchunks)] + CBK, [bk])
            act(rstd.ap[:, 0:n], bank[:, 0:n], AF.Ln, [bk], [rstd.k()], scale=1.0 / scale_div, bias=epsb.ap[:, 0:1])
            act(rstd.ap[:, 0:n], rstd.ap[:, 0:n], AF.Exp, [rstd.k()], [rstd.k()], scale=-0.5)

        def prenorm(g, wname, layer, hn, hn_c0, sq, rstd):
            kind, c0, n = g
            o, _ = CV_LAY[wname]
            fm_rstd(lambda c: xT.ap[:, c, c0:c0 + n], 8, n, float(D), xkeys_for(g), sq, rstd, None)
            for c in range(8):
                stt(hn.ap[:, c, hn_c0:hn_c0 + n], xT.ap[:, c, c0:c0 + n], cv.ap[:, o + layer * 8 + c:o + layer * 8 + c + 1],
                    rstd.ap[:, 0:n], ALU.mult, ALU.mult, xkeys_for(g) + [rstd.k(), cv.k()], [hn.k(g[1], c)])

        def postnorm_add(g, wname, layer, mix, sq, rstd):
            kind, c0, n = g
            o, _ = CV_LAY[wname]
            fm_rstd(lambda c: mix.ap[:, c, 0:n], 8, n, float(D), [mix.k(c) for c in range(8)], sq, rstd, None)
            for c in range(8):
                tt(mix.ap[:, c, 0:n], mix.ap[:, c, 0:n], rstd.ap[:, 0:n], ALU.mult, [mix.k(c), rstd.k()], [mix.k(c)],
                   eng=("gpsimd" if c % 2 else "vector"))
            for c in range(8):
                stt(xT.ap[:, c, c0:c0 + n], mix.ap[:, c, 0:n], cv.ap[:, o + layer * 8 + c:o + layer * 8 + c + 1],
                    xT.ap[:, c, c0:c0 + n], ALU.mult, ALU.add, xkeys_for(g) + [mix.k(c), cv.k()], [xT.k(g[1])])

        def layer0_mixer():
            for g in groups:
                layer0_group(g, W0, W0K, None, None, wup)

        def layer0_group(g, W0, W0K, WO, WOK, wup):
            kind, c0, n = g
            isms = (kind == "MS")
            ntile = 1 if isms else n // 128
            nrow = 80 if isms else 128
            with A.scope():
                hn = A.alloc("hn", [128, 8, n], BF16)
                yT = A.alloc("yT", [128, 8, n], BF16)
                with A.scope():
                    sq = A.alloc("sq", [128, 8, n], BF16)
                    rstd = A.alloc("rstd", [128, n])
                    prenorm(g, "nmpre", 0, hn, 0, sq, rstd)
                HNK = [hn.k(g[1], c) for c in range(8)]

                def proj_fm(col0, m, nn=n):
                    bank, bk = PS()
                    mm(bank[0:m, 0:nn], [(W0.ap[:, kc, col0:col0 + m], hn.ap[:, kc, 0:nn]) for kc in range(8)],
                       HNK + W0K, [bk])
                    return bank, bk

                with A.scope():
                    sg = A.alloc("sg", [128, 8, n], BF16)
                    qe = A.alloc("qe", [128, 8, n], BF16)
                    ke = A.alloc("ke", [128, 8, n], BF16)
                    Eall = A.alloc("Eall", [128, 8, 20])
                    alow = A.alloc("alow", [16, n], BF16)
                    with A.scope():
                        G1 = A.alloc("G1", [128, 8, n])
                        for h in range(8):
                            col = (1536 + 128 * h) if h < 4 else (3072 + 128 * (h - 4))
                            bank, bk = proj_fm(col, 128)
                            act(sg.ap[:, h, 0:n], bank[:, 0:n], AF.Silu, [bk], [sg.k(h)])
                        for h in range(4):
                            bank, bk = proj_fm(512 + 128 * h, 128)
                            act(G1.ap[:, h, 0:n], bank[:, 0:n], AF.Sigmoid, [bk], [G1.k(h)])
                        bank, bk = proj_fm(3584, 16)
                        copy(alow.ap[:, 0:n], bank[0:16, 0:n], [bk], [alow.k()], eng="vector")
                        bo, _ = CV_LAY["balpha"]
                        for h in range(4):
                            bank, bk = PS()
                            mm(bank[0:64, 0:n], [(wup.ap[0:16, 64 * h:64 * h + 64], alow.ap[0:16, 0:n])],
                               [wup.k(), alow.k()], [bk])
                            act(G1.ap[0:64, 4 + h, 0:n], bank[0:64, 0:n], AF.Sigmoid, [bk, cv.k()], [G1.k(4 + h)],
                                bias=cv.ap[0:64, bo + h:bo + h + 1])
                        rmask = MK("maskms", cols=80) if isms else MK("mask64", cols=n)
                        gsets = [[A.alloc(nm, [128, n]) for nm in ("lf", "cum", "eq", "ek")] for _ in range(2)]
                        for h in range(8):
                            dk = 128 if h < 4 else 64
                            if True:
                                lf, cum, eq, ek = gsets[h % 2]
                                if h < 4:
                                    ts(G1.ap[:, h, 0:n], G1.ap[:, h, 0:n], OML[:, h:h + 1], LB[:, h:h + 1], ALU.mult, ALU.add,
                                       [G1.k(h)] + LBK, [G1.k(h)])
                                    act(lf.ap[:, 0:n], G1.ap[:, h, 0:n], AF.Ln, [G1.k(h)], [lf.k()])
                                    ts(G1.ap[:, h, 0:n], G1.ap[:, h, 0:n], -1.0, 1.0, ALU.mult, ALU.add, [G1.k(h), lf.k()], [G1.k(h)])
                                    esc = 1.0
                                else:
                                    act(lf.ap[0:dk, 0:n], G1.ap[0:dk, h, 0:n], AF.Ln, [G1.k(h)], [lf.k()])
                                    esc = 1.0 / 16.0
                                if isms or h < 4:
                                    P.op("vector", lambda e, cum=cum, lf=lf, dk=dk: e.tensor_tensor_scan(
                                        out=cum.ap[0:dk, 0:n], data0=rmask[0:dk, 0:n], data1=lf.ap[0:dk, 0:n], initial=0.0,
                                        op0=ALU.mult, op1=ALU.add), reads=[lf.k(), mk.k()], writes=[cum.k()])
                                else:
                                    def scan_fn(e, cum=cum, lf=lf, dk=dk):
                                        r_ = None
                                        for t_ in range(n // 128):
                                            r_ = e.tensor_tensor_scan(out=cum.ap[0:dk, 128 * t_:128 * t_ + 128],
                                                                      data0=onesf[0:dk, 0:128], data1=lf.ap[0:dk, 128 * t_:128 * t_ + 128],
                                                                      initial=0.0, op0=ALU.mult, op1=ALU.add)
                                        return r_
                                    P.op("vector", scan_fn, reads=[lf.k(), mk.k()], writes=[cum.k()])
                                act(eq.ap[0:dk, 0:n], cum.ap[0:dk, 0:n], AF.Exp, [cum.k()], [eq.k()], scale=esc)
                                act(ek.ap[0:dk, 0:n], cum.ap[0:dk, 0:n], AF.Exp, [cum.k()], [ek.k()], scale=-esc)
                                if isms:
                                    copy(Eall.ap[0:dk, h, 0:16], eq.ap[0:dk, 0:64].rearrange("p (s j) -> p s j", j=4)[:, :, 3], [eq.k()], [Eall.k(h)], eng="vector")
                                    copy(Eall.ap[0:dk, h, 16:17], eq.ap[0:dk, 79:80], [eq.k()], [Eall.k(h)], eng="vector")
                                else:
                                    CHh = 64 if h < 4 else 128
                                    copy(Eall.ap[0:dk, h, 0:n // CHh], eq.ap[0:dk, 0:n].rearrange("p (c j) -> p c j", j=CHh)[:, :, CHh - 1], [eq.k()], [Eall.k(h)], eng="vector")
                                if h < 4:
                                    bank, bk = proj_fm(128 * h, 128)
                                    tt(qe.ap[:, h, 0:n], bank[:, 0:n], eq.ap[:, 0:n], ALU.mult, [bk, eq.k()], [qe.k(h)])
                                    tt(ke.ap[:, h, 0:n], G1.ap[:, h, 0:n], ek.ap[:, 0:n], ALU.mult, [G1.k(h), ek.k()], [ke.k(h)])
                                else:
                                    bank, bk = proj_fm(2048 + 64 * (h - 4), 64)
                                    stt(qe.ap[0:64, h, 0:n], bank[0:64, 0:n], 0.125, eq.ap[0:64, 0:n], ALU.mult, ALU.mult,
                                        [bk, eq.k()], [qe.k(h)])
                                    bank, bk = proj_fm(2304 + 64 * (h - 4), 64)
                                    tt(ke.ap[0:64, h, 0:n], bank[0:64, 0:n], ek.ap[0:64, 0:n], ALU.mult, [bk, ek.k()], [ke.k(h)])

                    WO = A.alloc("WO0", [128, 8, D], BF16)
                    for kc in range(8):
                        dma(WO.ap[:, kc, :], wout0_d[kc * 128:(kc + 1) * 128, :], [], [WO.k(kc)], eng="gpsimd")
                    WOK = [WO.k(kc) for kc in range(8)]
                    for t in range(ntile):
                        l0 = 128 * t
                        for fam in range(2):
                            dk = 128 if fam == 0 else 64
                            S = S_h if fam == 0 else S_g
                            Sbf = Sbf_h if fam == 0 else Sbf_g
                            nwname = "norma" if fam == 0 else "normb"
                            vcol = 1024 if fam == 0 else 2560
                            hs = [4 * fam + q for q in range(4)]
                            with A.scope():
                                ktok = A.alloc("ktok", [128, 4, 128], BF16)
                                vtok = A.alloc("vtok", [128, 4, 128], BF16)
                                scm = A.alloc("scm", [128, 4, 128], BF16)
                                Sst = A.alloc("Sst", [128, 4, 4, 128], BF16)
                                Ttmp = A.alloc("Ttmp", [128, 4, 128])
                                bank, bk = PS()
                                bv = bfview(bank)
                                transpose_multi([(bv[0:nrow, q * dk:(q + 1) * dk], ke.ap[0:dk, hs[q], l0:l0 + nrow],
                                                  identb[0:dk, 0:dk]) for q in range(4)],
                                                [ke.k(h) for h in hs] + CBK, [bk])
                                copy(ktok.ap[0:nrow, :, 0:dk], bv[0:nrow, 0:4 * dk].rearrange("p (q d) -> p q d", q=4),
                                     [bk], [ktok.k()])
                                bank, bk = PS()
                                mm(bank[0:nrow, 0:512], [(hn.ap[:, kc, l0:l0 + nrow], W0.ap[:, kc, vcol:vcol + 512]) for kc in range(8)],
                                   HNK + W0K, [bk])
                                copy(vtok.ap[0:nrow, :, :], bank[0:nrow, :].rearrange("p (q d) -> p q d", q=4), [bk], [vtok.k()])
                                bank, bk = PS()
                                mm_multi([(bank[0:nrow, q * 128:q * 128 + nrow], ke.ap[0:dk, hs[q], l0:l0 + nrow],
                                           qe.ap[0:dk, hs[q], l0:l0 + nrow], True, True, None) for q in range(4)],
                                         [ke.k(h) for h in hs] + [qe.k(h) for h in hs], [bk])
                                cmask = MK("cms", parts=80, cols=80) if isms else (MK("bd64") if fam == 0 else MK("causal"))
                                CH = 64 if fam == 0 else 128
                                nch = 128 // CH
                                tt(scm.ap[0:nrow, :, 0:nrow], bank[0:nrow, :].rearrange("p (q d) -> p q d", q=4)[:, :, 0:nrow],
                                   cmask.unsqueeze(1).to_broadcast([nrow, 4, nrow]), ALU.mult, [bk, mk.k()], [scm.k()])

                                if not isms:
                                    for c in range(nch):
                                        if c == 0:
                                            copy(Sst.ap[0:dk, 0, :, :], Sbf.ap[0:dk, :, :], [Sbf.k()], [Sst.k(0)], eng="gpsimd")
                                        bank, bk = PS()
                                        mm_multi([(bank[0:dk, q * 128:(q + 1) * 128], ktok.ap[CH * c:CH * c + CH, q, 0:dk],
                                                   vtok.ap[CH * c:CH * c + CH, q, :], True, True, (CH * c, 0)) for q in range(4)],
                                                 [ktok.k(), vtok.k()], [bk])
                                        tt(Ttmp.ap[0:dk, :, :], S.ap[0:dk, :, :], bank[0:dk, :].rearrange("p (q d) -> p q d", q=4),
                                           ALU.add, [S.k(), bk], [Ttmp.k()])
                                        ci = t * nch + c
                                        tt(S.ap[0:dk, :, :], Ttmp.ap[0:dk, :, :],
                                           Eall.ap[0:dk, 4 * fam:4 * fam + 4, ci:ci + 1].to_broadcast([dk, 4, 128]),
                                           ALU.mult, [Ttmp.k()] + [Eall.k(h) for h in hs], [S.k()])
                                        if c < nch - 1:
                                            copy(Sst.ap[0:dk, c + 1, :, :], S.ap[0:dk, :, :], [S.k()], [Sst.k(c + 1)], eng="scalar")
                                        else:
                                            copy(Sbf.ap[0:dk, :, :], S.ap[0:dk, :, :], [S.k()], [Sbf.k()], eng="scalar")
                                    obank, obk = PS()
                                    items = []
                                    for q in range(4):
                                        items.append((obank[:, q * 128:(q + 1) * 128], vtok.ap[:, q, :], scm.ap[:, q, :], True, False, None))
                                        for c in range(nch):
                                            items.append((obank[:, q * 128 + CH * c:q * 128 + CH * c + CH], Sst.ap[0:dk, c, q, :],
                                                          qe.ap[0:dk, hs[q], l0 + CH * c:l0 + CH * c + CH], False, c == nch - 1, None))
                                    mm_multi(items, [vtok.k(), scm.k()] + [Sst.k(c) for c in range(nch)] + [qe.k(h) for h in hs], [obk])
                                else:
                                    bank, bk = PS()
                                    mm_multi([(bank[0:dk, q * 128:(q + 1) * 128], ktok.ap[64:80, q, 0:dk], vtok.ap[64:80, q, :],
                                               True, True, (64, 0)) for q in range(4)], [ktok.k(), vtok.k()], [bk])
                                    tt(S.ap[0:dk, :, :], bank[0:dk, :].rearrange("p (q d) -> p q d", q=4),
                                       Eall.ap[0:dk, 4 * fam:4 * fam + 4, 16:17].to_broadcast([dk, 4, 128]), ALU.mult,
                                       [bk] + [Eall.k(h) for h in hs], [S.k()])
                                    copy(Sbf.ap[0:dk, :, :], S.ap[0:dk, :, :], [S.k()], [Sbf.k()], eng="scalar")
                                    obank, obk = PS(pool=(6, 7))
                                    st_d = sth_d if fam == 0 else stg_d
                                    so_d = hs_d if fam == 0 else gs_d
                                    S0s_ = [A.alloc("S0", [128, 16, 128]) for _ in range(2)]
                                    S0bs_ = [A.alloc("S0b", [128, 16, 128], BF16) for _ in range(2)]
                                    Vbds_ = [A.alloc("Vbd", [64, 16, 128], BF16)] * 2
                                    Sns_ = [A.alloc("Sn", [128, 16, 128]) for _ in range(2)]

                                    def s0_load(q_):
                                        dma(S0s_[q_ % 2].ap[0:dk, :, :], st_d[:, q_, :, :].rearrange("s k v -> k s v"), [],
                                            [S0s_[q_ % 2].k()], eng="sync")
                                    s0_load(0)
                                    for q in range(4):
                                        if True:
                                            S0, S0b, Vbd, Sn = S0s_[q % 2], S0bs_[q % 2], Vbds_[q % 2], Sns_[q % 2]
                                            if q + 1 < 4:
                                                s0_load(q + 1)
                                            copy(S0b.ap[0:dk, :, :], S0.ap[0:dk, :, :], [S0.k()], [S0b.k()], eng="gpsimd")
                                            tt(Vbd.ap[:, :, :], vtok.ap[0:64, q, :].unsqueeze(1).to_broadcast([64, 16, 128]),
                                               segb[0:64, 0:16].unsqueeze(2).to_broadcast([64, 16, 128]), ALU.mult,
                                               [vtok.k()] + CBK, [Vbd.k()])
                                            for qq in range(4):
                                                bank, bk = PS(pool=(0, 1, 2, 3, 4, 5))
                                                mm(bank[0:dk, 0:512], [(ktok.ap[0:64, q, 0:dk],
                                                                        Vbd.ap[:, 4 * qq:4 * qq + 4, :].rearrange("p s d -> p (s d)"))],
                                                   [ktok.k(), Vbd.k()], [bk])
                                                tt(Sn.ap[0:dk, 4 * qq:4 * qq + 4, :], S0.ap[0:dk, 4 * qq:4 * qq + 4, :],
                                                   bank[0:dk, :].rearrange("p (s d) -> p s d", s=4), ALU.add, [S0.k(), bk], [Sn.k(qq)])
                                                tt(Sn.ap[0:dk, 4 * qq:4 * qq + 4, :], Sn.ap[0:dk, 4 * qq:4 * qq + 4, :],
                                                   Eall.ap[0:dk, hs[q], 4 * qq:4 * qq + 4].unsqueeze(2).to_broadcast([dk, 4, 128]),
                                                   ALU.mult, [Sn.k(qq), Eall.k(hs[q])], [Sn.k(qq)])
                                            okey = ("out_s", fam, q)
                                            dma(so_d[:, q, :, :].rearrange("s k v -> k s v"), Sn.ap[0:dk, :, :],
                                                [Sn.k(qq) for qq in range(4)], [okey], eng="gpsimd")
                                            out_keys.append(okey)
                                            items = [(obank[:, q * 128:q * 128 + 80], vtok.ap[0:80, q, :], scm.ap[0:80, q, 0:80],
                                                      True, False, None)]
                                            for s in range(16):
                                                items.append((obank[:, q * 128 + 4 * s:q * 128 + 4 * s + 4], S0b.ap[0:dk, s, :],
                                                              qe.ap[0:dk, hs[q], 4 * s:4 * s + 4], False, s == 15, None))
                                            mm_multi(items, [vtok.k(), scm.k(), S0b.k(), qe.k(hs[q])], [obk])
                                with A.scope():
                                    sqb = A.alloc("sqb", [128, 4, 128], BF16)
                                    rs = A.alloc("rs", [128, 4, 128])
                                    y1 = A.alloc("y1", [128, 4, 128])
                                    o3 = obank[:, :].rearrange("p (q d) -> p q d", q=4)[:, :, 0:nrow]
                                    act(sqb.ap[:, :, 0:nrow], o3, AF.Square, [obk], [sqb.k()])
                                    bank, bk = PS(pool=(0, 1, 2, 3, 4, 5))
                                    mm_multi([(bank[:, q * 128:q * 128 + nrow], onesb, sqb.ap[:, q, 0:nrow], True, True, None)
                                              for q in range(4)], [sqb.k()] + CBK, [bk])
                                    b3 = bank[:, :].rearrange("p (q d) -> p q d", q=4)[:, :, 0:nrow]
                                    act(rs.ap[:, :, 0:nrow], b3, AF.Ln, [bk], [rs.k()], scale=1.0 / 128.0, bias=epsb.ap[:, 0:1])
                                    act(rs.ap[:, :, 0:nrow], rs.ap[:, :, 0:nrow], AF.Exp, [rs.k()], [rs.k()], scale=-0.5)
                                    no, _ = CV_LAY[nwname]
                                    for q in range(4):
                                        stt(y1.ap[:, q, 0:nrow], obank[:, q * 128:q * 128 + nrow], cv.ap[:, no:no + 1],
                                            rs.ap[:, q, 0:nrow], ALU.mult, ALU.mult, [obk, rs.k(), cv.k()], [y1.k(q)])
                                    tt(yT.ap[:, 4 * fam:4 * fam + 4, l0:l0 + nrow], y1.ap[:, :, 0:nrow],
                                       sg.ap[:, 4 * fam:4 * fam + 4, l0:l0 + nrow], ALU.mult,
                                       [y1.k(q) for q in range(4)] + [sg.k(h) for h in hs], [yT.k(fam, t)], eng="gpsimd")
                    YK = [yT.k(fam, t) for fam in range(2) for t in range(ntile)]
                    with A.scope():
                        mix = A.alloc("mix", [128, 8, n])

                        class _HnAlias:
                            ap = hn.ap
                            name = hn.name

                            @staticmethod
                            def k(c):
                                return hn.k(g[1], c)
                        sq = _HnAlias
                        rstd = A.alloc("rstd", [128, n])
                        for oc in range(8):
                            bank, bk = PS()
                            mm(bank[:, 0:n], [(WO.ap[:, hc, oc * 128:(oc + 1) * 128], yT.ap[:, hc, 0:n]) for hc in range(8)],
                               YK + WOK, [bk])
                            copy(mix.ap[:, oc, 0:n], bank[:, 0:n], [bk], [mix.k(oc)])
                        postnorm_add(g, "nmpost", 0, mix, sq, rstd)

        def ffn(layer):
            with A.scope():
                hnF = A.alloc("hnF", [128, 8, Tp], BF16)
                h1 = A.alloc("h1", [128, 22, Tp], BF16)
                with A.scope():
                    sq = A.alloc("sq", [128, 8, 512], BF16)
                    rstd = A.alloc("rstd", [128, 512])
                    for g in groups:
                        prenorm(g, "nfpre", layer, hnF, g[1], sq, rstd)
                with A.scope():
                    wgb = [A.alloc("wgb", [128, 8, 256], BF16) for _ in range(3)]
                    wub = [A.alloc("wub", [128, 8, 256], BF16) for _ in range(3)]
                    sgt = [A.alloc("sgt", [128, 512]) for _ in range(2)]
                    it = 0
                    for jb in range(11):
                        wb, ub = wgb[jb % 3], wub[jb % 3]
                        dma(wb.ap[:, :, :], wg_d[layer, :, jb * 256:(jb + 1) * 256].rearrange("(k p) n -> p k n", p=128),
                            [], [wb.k()], eng="gpsimd")
                        dma(ub.ap[:, :, :], wu_d[layer, :, jb * 256:(jb + 1) * 256].rearrange("(k p) n -> p k n", p=128),
                            [], [ub.k()], eng="gpsimd")
                        for jj in range(2):
                            j = jb * 2 + jj
                            for g in groups:
                                _, c0, n = g
                                hk = [hnF.k(g[1], c) for c in range(8)]
                                gb_, gk = PS()
                                mm(gb_[:, 0:n], [(wb.ap[:, kc, jj * 128:(jj + 1) * 128], hnF.ap[:, kc, c0:c0 + n]) for kc in range(8)],
                                   hk + [wb.k()], [gk])
                                ub_, uk = PS()
                                mm(ub_[:, 0:n], [(ub.ap[:, kc, jj * 128:(jj + 1) * 128], hnF.ap[:, kc, c0:c0 + n]) for kc in range(8)],
                                   hk + [ub.k()], [uk])
                                st_ = sgt[it % 2]
                                it += 1
                                act(st_.ap[:, 0:n], gb_[:, 0:n], AF.Silu, [gk], [st_.k()])
                                tt(h1.ap[:, j, c0:c0 + n], st_.ap[:, 0:n], ub_[:, 0:n], ALU.mult, [st_.k(), uk], [h1.k(j, g[1])])
                with A.scope():
                    mixF = A.alloc("mixF", [128, 8, Tp])
                    wdb = [A.alloc("wdb", [128, 22, 128], BF16) for _ in range(3)]
                    for oc in range(8):
                        wd = wdb[oc % 3]
                        dma(wd.ap[:, :, :], wd_d[layer, :, oc * 128:(oc + 1) * 128].rearrange("(j p) n -> p j n", p=128),
                            [], [wd.k()], eng="gpsimd")
                        for g in groups:
                            _, c0, n = g
                            bank, bk = PS()
                            mm(bank[:, 0:n], [(wd.ap[:, j, :], h1.ap[:, j, c0:c0 + n]) for j in range(22)],
                               [h1.k(j, g[1]) for j in range(22)] + [wd.k()], [bk])
                            copy(mixF.ap[:, oc, c0:c0 + n], bank[:, 0:n], [bk], [mixF.k(g[1], oc)])
                    with A.scope():
                        sq = A.alloc("sq", [128, 8, 512], BF16)
                        rstd = A.alloc("rstd", [128, 512])
                        for g in groups:
                            _, c0, n = g
                            o, _ = CV_LAY["nfpost"]
                            fm_rstd(lambda c: mixF.ap[:, c, c0:c0 + n], 8, n, float(D), [mixF.k(g[1], c) for c in range(8)],
                                    sq, rstd, None)
                            for c in range(8):
                                tt(mixF.ap[:, c, c0:c0 + n], mixF.ap[:, c, c0:c0 + n], rstd.ap[:, 0:n], ALU.mult,
                                   [mixF.k(g[1], c), rstd.k()], [mixF.k(g[1], c)], eng=("gpsimd" if c % 2 else "vector"))
                            for c in range(8):
                                stt(xT.ap[:, c, c0:c0 + n], mixF.ap[:, c, c0:c0 + n],
                                    cv.ap[:, o + layer * 8 + c:o + layer * 8 + c + 1], xT.ap[:, c, c0:c0 + n], ALU.mult, ALU.add,
                                    xkeys_for(g) + [mixF.k(g[1], c), cv.k()], [xT.k(g[1])])

        def store_out():
            with A.scope():
                ost = [A.alloc("ostage", [128, D]) for _ in range(2)]
                units = []
                if has_ms:
                    units.append(("MS", MS0, 80))
                for t in range(npt):
                    units.append(("T", 128 * t, 128))
                for ui, (kind, c0, nr) in enumerate(units):
                    ob = ost[ui % 2]
                    gk = xT.k(MS0) if kind == "MS" else xT.k((c0 // 512) * 512)
                    for half in range(2):
                        bank, bk = PS()
                        transpose_multi([(bank[0:nr, q * 128:(q + 1) * 128], xT.ap[:, half * 4 + q, c0:c0 + nr], identf)
                                         for q in range(4)], XIN_KEYS + [gk, mk.k()], [bk])
                        copy(ob.ap[0:nr, half * 512:(half + 1) * 512], bank[0:nr, :], [bk], [ob.k(half)])
                    if kind == "MS":
                        okey = ("out_ys",)
                        dma(ys_d[:, :], ob.ap[0:64, :], [ob.k(0), ob.k(1)], [okey], eng="sync")
                    else:
                        r0 = (tile0 + c0 // 128) * 128
                        okey = ("out_yp", r0)
                        dma(yp_d[r0:r0 + 128, :], ob.ap[:, :], [ob.k(0), ob.k(1)], [okey], eng=("sync" if ui % 2 else "scalar"))
                    out_keys.append(okey)

        def act_acc(out, in_, func, accum_out, reads, writes):
            P.op("scalar", lambda e: e.activation(out=out, in_=in_, func=func, accum_out=accum_out),
                 reads=reads, writes=writes)

        def layer1_mixer():
            with A.scope():
                negA = A.alloc("negA", [128, 32])
                act(negA.ap[:, :], CV("alog"), AF.Exp, [cv.k()], [negA.k()])
                diagD = A.alloc("diagD", [128, 32, 128], BF16)
                dso_, _ = CV_LAY["dskip"]
                for h_ in range(32):
                    ts(diagD.ap[:, h_, :], identb, cv.ap[:, dso_ + h_:dso_ + h_ + 1], 1.0, ALU.mult, ALU.mult, CBK + [cv.k()],
                       [diagD.k(h_)], eng=("vector" if h_ % 2 else "gpsimd"))
                for g in groups:
                    try:
                        layer1_group(g, negA, diagD)
                    except _Cut:
                        pass

        def layer1_group(g, negA, diagD):
            kind, c0, n = g
            isms = (kind == "MS")
            ntile = 1 if isms else n // 128
            R = 80 if isms else 128
            cwo, _ = CV_LAY["convw"]
            cbo, _ = CV_LAY["convb"]
            onecol = MK("ones")[:, 0:1]
            with A.scope():
                hn = A.alloc("hn1", [128, 8, n], BF16)
                sq = A.alloc("sq1", [128, 8, n], BF16)
                rstd = A.alloc("rstd1", [128, n])
                cutpoint(0.3)
                prenorm(g, "nmpre", 1, hn, 0, sq, rstd)
                cutpoint(0.5)
                HNK = [hn.k(g[1], c) for c in range(8)]
                BT = A.alloc("BT", [128, 4, n], BF16)
                CT = A.alloc("CT", [128, 4, n], BF16)
                xtok = A.alloc("xtok", [128, ntile, 2048], BF16)
                btok = A.alloc("btok", [128, ntile, 512], BF16)
                zs = A.alloc("zs", [128, ntile, 2048], BF16)
                y3T = A.alloc("y3T", [128, 16, n], BF16)
                dtall = A.alloc("dtall", [128, ntile, 32])
                if isms:
                    hsT = A.alloc("hs4", [128, 24, 64], BF16)
                    rawSf = A.alloc("rawSf", [128, 24, 64])
                    with A.scope():
                        stc = A.alloc("stc", [64, 3072])
                        P.op("vector", lambda e: e.memset(stc.ap[:, :], 0.0), writes=[stc.k()])
                        for s_ in range(16):
                            dma(stc.ap[4 * s_ + 1:4 * s_ + 4, :], stc_d[s_, :, :], [stc.k()], [stc.k(1, s_)],
                                eng=("sync" if s_ % 2 else "scalar"))
                        STCK = [stc.k()] + [stc.k(1, s_) for s_ in range(16)]
                        for b6 in range(6):
                            bank, bk = PS()
                            transpose_multi([(bank[:, q * 64:(q + 1) * 64], stc.ap[0:64, (4 * b6 + q) * 128:(4 * b6 + q + 1) * 128],
                                              identf[0:64, 0:64]) for q in range(4)], STCK + [mk.k()], [bk])
                            copy(hsT.ap[:, 4 * b6:4 * b6 + 4, :], bank[:, 0:256].rearrange("p (q r) -> p q r", q=4), [bk],
                                 [hsT.k(b6)])
                cutpoint(1)
                with A.scope():
                    wblk = [A.alloc("wblk", [128, 8, 512], BF16) for _ in range(2)]
                    wdt = A.alloc("wdt", [128, 8, 32], BF16)
                    dma(wdt.ap[:, :, :], win1_d[:, 5120:5152].rearrange("(k p) n -> p k n", p=128), [], [wdt.k()], eng="gpsimd")
                    rawb = [A.alloc("rawb", [128, n + 4], BF16) for _ in range(3)]
                    xcb = [A.alloc("xcb", [128, n], BF16) for _ in range(3)]
                    dgs = [A.alloc("dg", [128, 4, 128], BF16) for _ in range(3)]
                    if isms:
                        rawS = [A.alloc("rawS", [128, 16, 8], BF16) for _ in range(3)]
                        rawM = [A.alloc("rawM", [128, 20], BF16) for _ in range(3)]
                        for rm in rawM:
                            P.op("vector", lambda e, rm=rm: e.memset(rm.ap[:, :], 0.0), writes=[rm.k()])
                    cutpoint(1.2)

                    def load_wblk(b):
                        wb = wblk[b % 2]
                        dma(wb.ap[:, :, :], win1_d[:, 2048 + 512 * b:2048 + 512 * (b + 1)].rearrange("(k p) n -> p k n", p=128),
                            [], [wb.k()], eng="gpsimd")

                    st1 = {}

                    def s1(cc):
                        b, q = cc // 4, cc % 4
                        wb = wblk[b % 2]
                        if q == 0 and b + 1 < 6 and b >= 1:
                            load_wblk(b + 1)
                        bank, bk = PS()
                        mm(bank[:, 0:n], [(wb.ap[:, kc, q * 128:(q + 1) * 128], hn.ap[:, kc, 0:n]) for kc in range(8)],
                           HNK + [wb.k()], [bk])
                        dg = dgs[cc % 3]
                        for k in range(4):
                            ts(dg.ap[:, k, :], identb, cv.ap[:, cwo + k * 24 + cc:cwo + k * 24 + cc + 1], 1.0, ALU.mult, ALU.mult,
                               CBK + [cv.k()], [dg.k(k)], eng="vector")
                        if not isms:
                            rb = rawb[cc % 3]
                            copy(rb.ap[:, 0:4], hist32.ap[:, cc, :], [hist32.k(cc)], [rb.k(0)], eng="vector")
                            copy(rb.ap[:, 4:4 + n], bank[:, 0:n], [bk], [rb.k(1)])
                            copy(hist32.ap[:, cc, :], bank[:, n - 4:n], [bk, rb.k(0)], [hist32.k(cc)], eng="vector")
                        else:
                            rS, rM = rawS[cc % 3], rawM[cc % 3]
                            copy(rS.ap[:, :, 0:4], hsT.ap[:, cc, :].rearrange("p (s k) -> p s k", k=4), [hsT.k(cc // 4)], [rS.k(0)],
                                 eng="vector")
                            copy(rS.ap[:, :, 4:8], bank[:, 0:64].rearrange("p (s j) -> p s j", j=4), [bk], [rS.k(1)], eng="vector")
                            copy(rawSf.ap[:, cc, :], bank[:, 0:64], [bk], [rawSf.k(cc)], eng="scalar")
                            copy(rM.ap[:, 4:20], bank[:, 64:80], [bk], [rM.k()], eng="vector")
                            copy(hist32.ap[:, cc, :], bank[:, 76:80], [bk], [hist32.k(cc)], eng="vector")

                    def s2(cc):
                        dg = dgs[cc % 3]
                        DGK = [dg.k(k) for k in range(4)]
                        cbank, cbk = PS()
                        if not isms:
                            rb = rawb[cc % 3]
                            mm(cbank[:, 0:n], [(dg.ap[:, k, :], rb.ap[:, 1 + k:1 + k + n]) for k in range(4)],
                               DGK + [rb.k(0), rb.k(1)], [cbk])
                        else:
                            rS, rM = rawS[cc % 3], rawM[cc % 3]
                            items = []
                            for k in range(4):
                                items.append((cbank[:, 0:124], dg.ap[:, k, :],
                                              rS.ap[:, :, :].rearrange("p s r -> p (s r)")[:, 1 + k:1 + k + 124],
                                              k == 0, k == 3, None))
                            for k in range(4):
                                items.append((cbank[:, 128:144], dg.ap[:, k, :], rM.ap[:, 1 + k:1 + k + 16], k == 0, k == 3, None))
                            mm_multi(items, DGK + [rS.k(0), rS.k(1), rM.k()], [cbk])
                        if cc < 16:
                            dst, dk_ = xcb[cc % 3].ap[:, 0:n], xcb[cc % 3].k()
                        elif cc < 20:
                            dst, dk_ = BT.ap[:, cc - 16, 0:n], BT.k(cc - 16)
                        else:
                            dst, dk_ = CT.ap[:, cc - 20, 0:n], CT.k(cc - 20)
                        if not isms:
                            act(dst, cbank[:, 0:n], AF.Silu, [cbk, cv.k()], [dk_], bias=cv.ap[:, cbo + cc:cbo + cc + 1])
                        else:
                            act(dst[:, 0:64].rearrange("p (s j) -> p s j", j=4),
                                cbank[:, 0:128].rearrange("p (s r) -> p s r", r=8)[:, :, 0:4], AF.Silu, [cbk, cv.k()], [dk_],
                                bias=cv.ap[:, cbo + cc:cbo + cc + 1])
                            act(dst[:, 64:80], cbank[:, 128:144], AF.Silu, [cbk, cv.k(), dk_], [dk_],
                                bias=cv.ap[:, cbo + cc:cbo + cc + 1])
                        st1[cc] = (dst, dk_)

                    def s3(cc):
                        dst, dk_ = st1.pop(cc)
                        if cc >= 20:
                            return
                        tbank, tbk = PS()
                        tv = bfview(tbank)
                        transpose_multi([(tv[0:R, t * 128:(t + 1) * 128], dst[:, 128 * t:128 * t + R], identb)
                                         for t in range(ntile)], [dk_] + CBK, [tbk])
                        if cc < 16:
                            copy(xtok.ap[0:R, :, cc * 128:(cc + 1) * 128],
                                 tv[0:R, 0:ntile * 128].rearrange("p (t d) -> p t d", t=ntile), [tbk], [xtok.k(cc)])
                        else:
                            copy(btok.ap[0:R, :, (cc - 16) * 128:(cc - 15) * 128],
                                 tv[0:R, 0:ntile * 128].rearrange("p (t d) -> p t d", t=ntile), [tbk], [btok.k(cc - 16)])

                    load_wblk(0)
                    load_wblk(1)
                    for step in range(24 + 2):
                        if step < 24:
                            s1(step)
                        if 0 <= step - 1 < 24:
                            s2(step - 1)
                        if 0 <= step - 2 < 24:
                            s3(step - 2)
                    cutpoint(2)
                    dto, _ = CV_LAY["dtb"]
                    for t in range(ntile):
                        bank, bk = PS()
                        mm(bank[0:R, 0:32], [(hn.ap[:, kc, 128 * t:128 * t + R], wdt.ap[:, kc, :]) for kc in range(8)],
                           HNK + [wdt.k()], [bk])
                        tt(dtall.ap[0:R, t, :], bank[0:R, 0:32], cv.ap[0:R, dto:dto + 32], ALU.add, [bk, cv.k()], [dtall.k(t)])
                    for b in range(4):
                        wb = wblk[b % 2]
                        dma(wb.ap[:, :, :], win1_d[:, 512 * b:512 * (b + 1)].rearrange("(k p) n -> p k n", p=128),
                            [], [wb.k()], eng="gpsimd")
                        for t in range(ntile):
                            bank, bk = PS()
                            mm(bank[0:R, 0:512], [(hn.ap[:, kc, 128 * t:128 * t + R], wb.ap[:, kc, :]) for kc in range(8)],
                               HNK + [wb.k()], [bk])
                            act(zs.ap[0:R, t, 512 * b:512 * (b + 1)], bank[0:R, 0:512], AF.Silu, [bk], [zs.k(t, b)])
                cutpoint(3)
                XTK = [xtok.k(cc) for cc in range(16)]
                BTK = [btok.k(q) for q in range(4)]
                with A.scope():
                    nmaskb = ncmsb if isms else ncausb
                    dso, _ = CV_LAY["dskip"]
                    ono, _ = CV_LAY["odnorm"]
                    ybk = [("ps", 4 + gq) for gq in range(4)]
                    smts = [A.alloc("smt", [128, 8, 32]) for _ in range(2)]
                    xws = [A.alloc("xw", [128, 2048], BF16) for _ in range(2)]
                    cbms = [A.alloc("cbm", [128, 4, 128], BF16) for _ in range(2)]
                    yis = A.alloc("yis", [128, 2048])
                    Dgs = [A.alloc("Dg", [128, 128]) for _ in range(4)]
                    Ls = [A.alloc("L", [128, 128]) for _ in range(4)]
                    Ms = [A.alloc("M", [128, 128], BF16) for _ in range(4)]
                    y = A.alloc("y", [128, 2048])
                    ssq = A.alloc("ssq", [128, 8])
                    junk = A.alloc("junk", [128, 512], BF16)
                    y3 = A.alloc("y3", [128, 2048], BF16)

                    def prologue(t):
                        l0 = 128 * t
                        smt, xw, cbm = smts[t % 2], xws[t % 2], cbms[t % 2]
                        dtS, aS, cumS, ncum = smt.ap[0:R, 0, :], smt.ap[0:R, 1, :], smt.ap[0:R, 2, :], smt.ap[0:R, 3, :]
                        ecum, wj, tmp = smt.ap[0:R, 4, :], smt.ap[0:R, 5, :], smt.ap[0:R, 7, :]
                        dec = smt.ap[:, 6, :]
                        act(tmp, dtall.ap[0:R, t, :], AF.Exp, [dtall.k(t)], [smt.k(7)])
                        act(dtS, tmp, AF.Ln, [smt.k(7), mk.k()], [smt.k(0)], bias=onecol[0:R, :])
                        stt(aS, dtS, -1.0, negA.ap[0:R, :], ALU.mult, ALU.mult, [smt.k(0), negA.k()], [smt.k(1)])
                        tri = MK("cms", parts=80, cols=80) if isms else MK("causal")
                        segm = MK("sseg", parts=80, cols=80) if isms else onesf
                        TR = R if isms else 128
                        bank, bk = PS(pool=(0, 1))
                        mm_multi([(bank[0:R, 0:32], tri[0:R, 0:R], aS, True, True, None),
                                  (bank[0:TR, 32:64], segm[0:R, 0:TR], aS, True, True, None)], [smt.k(1), mk.k()], [bk])
                        copy(cumS, bank[0:R, 0:32], [bk], [smt.k(2)], eng="vector")
                        act(ncum, dtS, AF.Ln, [smt.k(0)], [smt.k(3)])
                        tt(ncum, ncum, bank[0:R, 0:32], ALU.subtract, [smt.k(3), bk], [smt.k(3)])
                        act(ecum, cumS, AF.Exp, [smt.k(2)], [smt.k(4)])
                        tt(tmp, bank[0:R, 32:64], cumS, ALU.subtract, [bk, smt.k(2), smt.k(0)], [smt.k(7)])
                        if not isms:
                            act(dec, bank[:, 32:64], AF.Exp, [bk], [smt.k(6)])
                        act(tmp, tmp, AF.Exp, [smt.k(7)], [smt.k(7)])
                        tt(wj, tmp, dtS, ALU.mult, [smt.k(7), smt.k(0)], [smt.k(5)])
                        tt(xw.ap[0:R, :].rearrange("p (h q) -> p h q", q=64),
                           xtok.ap[0:R, t, :].rearrange("p (h q) -> p h q", q=64),
                           wj.unsqueeze(2).to_broadcast([R, 32, 64]), ALU.mult, XTK + [smt.k(5)], [xw.k()])
                        bank, bk = PS(pool=(0, 1))
                        mm_multi([(bank[0:R, gq * 128:gq * 128 + R], BT.ap[:, gq, l0:l0 + R], CT.ap[:, gq, l0:l0 + R], True, True, None)
                                  for gq in range(4)], [BT.k(q) for q in range(4)] + [CT.k(q) for q in range(4)], [bk])
                        copy(cbm.ap[0:R, :, 0:R], bank[0:R, :].rearrange("p (q d) -> p q d", q=4)[:, :, 0:R], [bk], [cbm.k()])

                    def middle(t, part):
                        l0 = 128 * t
                        smt, xw, cbm = smts[t % 2], xws[t % 2], cbms[t % 2]
                        dtS, aS, cumS, ncum = smt.ap[0:R, 0, :], smt.ap[0:R, 1, :], smt.ap[0:R, 2, :], smt.ap[0:R, 3, :]
                        ecum = smt.ap[0:R, 4, :]
                        dec = smt.ap[:, 6, :]
                        if part == "pre":
                            cutpoint(4)
                            if isms:
                                sample_ssd(CT, btok, xw, smt, aS, ecum, yis)
                                for gq in range(4):
                                    bank, bk = PS(pool=(0, 1))
                                    mm(bank[:, 0:512], [(btok.ap[64:80, 0, gq * 128:(gq + 1) * 128], xw.ap[64:80, gq * 512:(gq + 1) * 512])],
                                       BTK + [xw.k()], [bk], tile_position=(64, 0))
                                    copy(ST.ap[:, gq * 512:(gq + 1) * 512], bank[:, 0:512], [bk], [ST.k(gq)], eng="vector")
                                    copy(STb.ap[:, gq * 512:(gq + 1) * 512], bank[:, 0:512], [bk], [STb.k(gq)], eng="scalar")
                            else:
                                for gq in range(4):
                                    cs_ = slice(gq * 512, (gq + 1) * 512)
                                    bank, bk = PS(pool=(0, 1))
                                    mm(bank[0:R, 0:512], [(CT.ap[:, gq, l0:l0 + R], STb.ap[:, cs_])], [CT.k(gq), STb.k(gq)], [bk])
                                    tt(yis.ap[0:R, cs_].rearrange("p (h q) -> p h q", q=64),
                                       bank[0:R, 0:512].rearrange("p (h q) -> p h q", q=64),
                                       ecum[:, 8 * gq:8 * gq + 8].unsqueeze(2).to_broadcast([R, 8, 64]), ALU.mult, [bk, smt.k(4)],
                                       [yis.k(gq)])
                                for gq in range(4):
                                    cs_ = slice(gq * 512, (gq + 1) * 512)
                                    bank, bk = PS(pool=(0, 1))
                                    mm(bank[:, 0:512], [(btok.ap[0:R, t, gq * 128:(gq + 1) * 128], xw.ap[0:R, cs_])], BTK + [xw.k()], [bk])
                                    tt(ST.ap[:, cs_].rearrange("p (h q) -> p h q", q=64), ST.ap[:, cs_].rearrange("p (h q) -> p h q", q=64),
                                       dec[:, 8 * gq:8 * gq + 8].unsqueeze(2).to_broadcast([128, 8, 64]), ALU.mult, [ST.k(gq), smt.k(6)],
                                       [ST.k(gq)])
                                    tt(ST.ap[:, cs_], ST.ap[:, cs_], bank[:, 0:512], ALU.add, [ST.k(gq), bk], [ST.k(gq)])
                                    copy(STb.ap[:, cs_], ST.ap[:, cs_], [ST.k(gq)], [STb.k(gq)], eng="scalar")
                            return
                        cutpoint(5)

                        def stageA(h):
                            Dg = Dgs[h % 4]
                            ts(Dg.ap[0:R, 0:R], identf[0:R, 0:R], cumS[:, h:h + 1], 1.0, ALU.mult, ALU.mult, [smt.k(2), mk.k()],
                               [Dg.k()], eng="gpsimd")
                            rbank, rbk = PS(pool=(1, 2, 3))
                            mm_multi([(rbank[0:R, 0:R], onesf[0:R, 0:R], Dg.ap[0:R, 0:R], True, False, None),
                                      (rbank[0:R, 0:R], identb[0:R, 0:R], nmaskb[0:R, 0:R], False, True, None)],
                                     [Dg.k(), mk.k()] + CBK, [rbk])
                            return rbank, rbk

                        def stageB(h, rbank, rbk):
                            gq = h // 8
                            L, M = Ls[h % 4], Ms[h % 4]
                            act(L.ap[0:R, 0:R], rbank[0:R, 0:R], AF.Exp, [rbk, smt.k(3)], [L.k()], bias=ncum[:, h:h + 1])
                            tt(M.ap[0:R, 0:R], L.ap[0:R, 0:R], cbm.ap[0:R, gq, 0:R], ALU.mult, [L.k(), cbm.k()], [M.k()])

                        def stageC(h):
                            gq = h // 8
                            M = Ms[h % 4]
                            mm(banks[4 + gq][0:R, (h % 8) * 64:(h % 8) * 64 + 64],
                               [(M.ap[0:R, 0:R], xtok.ap[0:R, t, h * 64:(h + 1) * 64]),
                                (diagD.ap[0:R, h, 0:R], xtok.ap[0:R, t, h * 64:(h + 1) * 64])],
                               [M.k(), diagD.k(h)] + XTK, [ybk[gq]])

                        DLY = 3
                        rbs = {}
                        for step in range(32 + DLY):
                            if step < 32:
                                rbs[step] = stageA(step)
                            if 0 <= step - 1 < 32:
                                stageB(step - 1, *rbs.pop(step - 1))
                            if 0 <= step - DLY < 32:
                                stageC(step - DLY)
                        cutpoint(6)

                    def epiA(t):
                        for gq in range(4):
                            cs_ = slice(gq * 512, (gq + 1) * 512)
                            if not isms:
                                tt(y.ap[0:R, cs_], banks[4 + gq][0:R, 0:512], yis.ap[0:R, cs_], ALU.add, [ybk[gq], yis.k(gq)], [y.k(gq)])
                            else:
                                tt(y.ap[0:64, cs_], banks[4 + gq][0:64, 0:512], yis.ap[0:64, cs_], ALU.add, [ybk[gq], yis.k(gq)],
                                   [y.k(gq)])
                                copy(y.ap[64:80, cs_], banks[4 + gq][64:80, 0:512], [ybk[gq]], [y.k(gq, 1)], eng="vector")

                    def epiB1(t):
                        for gq in range(4):
                            cs_ = slice(gq * 512, (gq + 1) * 512)
                            yk = [y.k(gq)] + ([y.k(gq, 1)] if isms else [])
                            tt(y.ap[0:R, cs_], y.ap[0:R, cs_], zs.ap[0:R, t, cs_], ALU.mult, yk + [zs.k(t, gq)], yk, eng="gpsimd")
                            act_acc(junk.ap[0:R, :], y.ap[0:R, cs_], AF.Square, ssq.ap[0:R, gq:gq + 1], yk, [junk.k(), ssq.k(gq)])
                        SSK = [ssq.k(gq) for gq in range(4)]
                        act(ssq.ap[0:R, 4:8], ssq.ap[0:R, 0:4], AF.Ln, SSK, [ssq.k(9)], scale=1.0 / 512.0, bias=epsb.ap[0:R, 0:1])
                        act(ssq.ap[0:R, 4:8], ssq.ap[0:R, 4:8], AF.Exp, [ssq.k(9)], [ssq.k(9)], scale=-0.5)
                        for gq in range(4):
                            cs_ = slice(gq * 512, (gq + 1) * 512)
                            yk = [y.k(gq)] + ([y.k(gq, 1)] if isms else [])
                            stt(y3.ap[0:R, cs_], y.ap[0:R, cs_], ssq.ap[0:R, 4 + gq:5 + gq], cv.ap[0:R, ono + gq * 512:ono + (gq + 1) * 512],
                                ALU.mult, ALU.mult, yk + [ssq.k(9), cv.k()], [y3.k(gq)])

                    def epiB2(t):
                        l0 = 128 * t
                        for half in range(2):
                            bank, bk = PS(pool=(0, 1, 2, 3))
                            bv = bfview(bank)
                            transpose_multi([(bv[:, q * R:(q + 1) * R], y3.ap[0:R, (8 * half + q) * 128:(8 * half + q + 1) * 128],
                                              identb[0:R, 0:R]) for q in range(8)], [y3.k(2 * half), y3.k(2 * half + 1)] + CBK, [bk])
                            copy(y3T.ap[:, 8 * half:8 * half + 8, l0:l0 + R], bv[:, 0:8 * R].rearrange("p (q r) -> p q r", q=8), [bk],
                                 [y3T.k(t, half)])
                        cutpoint(7)

                    prologue(0)
                    for t in range(ntile):
                        middle(t, "pre")
                        if t > 0:
                            epiB1(t - 1)
                        middle(t, "loop")
                        if t > 0:
                            epiB2(t - 1)
                        epiA(t)
                        if t + 1 < ntile:
                            prologue(t + 1)
                    epiB1(ntile - 1)
                    epiB2(ntile - 1)
                cutpoint(8)
                if isms:
                    with A.scope():
                        tokS = A.alloc("tokS", [64, 3072])
                        for b6 in range(6):
                            bank, bk = PS()
                            transpose_multi([(bank[0:64, q * 128:(q + 1) * 128], rawSf.ap[:, 4 * b6 + q, :], identf) for q in range(4)],
                                            [rawSf.k(4 * b6 + q) for q in range(4)] + [mk.k()], [bk])
                            copy(tokS.ap[0:64, 512 * b6:512 * (b6 + 1)], bank[0:64, :], [bk], [tokS.k(b6)])
                        for s in range(16):
                            okey = ("out_cs", s)
                            dma(cs_d[s, :, :], tokS.ap[4 * s + 1:4 * s + 4, :], [tokS.k(b6) for b6 in range(6)], [okey],
                                eng=("sync" if s % 2 else "scalar"))
                            out_keys.append(okey)
                cutpoint(9)
                Y3K = [y3T.k(t, half) for t in range(ntile) for half in range(2)]
                with A.scope():
                    mix = A.alloc("mix1", [128, 8, n])
                    wob = [A.alloc("wob", [128, 16, 128], BF16) for _ in range(3)]
                    for oc in range(8):
                        wo = wob[oc % 3]
                        dma(wo.ap[:, :, :], wout1_d[:, oc * 128:(oc + 1) * 128].rearrange("(k p) n -> p k n", p=128), [], [wo.k()],
                            eng="gpsimd")
                        bank, bk = PS()
                        mm(bank[:, 0:n], [(wo.ap[:, kc, :], y3T.ap[:, kc, 0:n]) for kc in range(16)], Y3K + [wo.k()], [bk])
                        copy(mix.ap[:, oc, 0:n], bank[:, 0:n], [bk], [mix.k(oc)])
                    postnorm_add(g, "nmpost", 1, mix, sq, rstd)

        def sample_ssd(CT, btok, xw, smt, aS, ecum, yis):
            with A.scope():
                CmT = A.alloc("CmT", [128, 4, 1088], BF16)
                P.op("gpsimd", lambda e: e.memset(CmT.ap[:, :, :], 0.0), writes=[CmT.k()])
                for gq in range(4):
                    copy(CmT.ap[:, gq, :].rearrange("p (s r) -> p s r", r=68)[:, :, 0:4],
                         CT.ap[:, gq, 0:64].rearrange("p (s j) -> p s j", j=4), [CT.k(gq), CmT.k()], [CmT.k(gq)], eng="vector")
                CMK = [CmT.k(gq) for gq in range(4)]
                decn = A.alloc("decn", [128, 256])
                with A.scope():
                    aexp = A.alloc("aexp", [64, 2048])
                    copy(aexp.ap[:, :].rearrange("p (h q) -> p h q", q=64), aS[0:64, :].unsqueeze(2).to_broadcast([64, 32, 64]),
                         [smt.k(1)], [aexp.k()], eng="vector")
                    bank, bk = PS(pool=(0, 1))
                    mm_multi([(bank[:, hb * 16:(hb + 1) * 16], aexp.ap[0:64, hb * 128:(hb + 1) * 128], MK("seg", parts=64, cols=16),
                               True, True, None) for hb in range(16)], [aexp.k(), mk.k()], [bk])
                    act(decn.ap[:, :], bank[:, 0:256], AF.Exp, [bk], [decn.k()])
                S0s = [A.alloc("S0s", [128, 16, 128]) for _ in range(3)]
                S0Ts = [A.alloc("S0T", [128, 2048], BF16) for _ in range(2)]
                Sns = [A.alloc("Sns", [128, 16, 128]) for _ in range(2)]
                Bms = [A.alloc("Bm", [64, 512], BF16) for _ in range(2)]
                yk = [("ps", 4 + gq) for gq in range(4)]
                def sload(s):
                    S0 = S0s[s % 3]
                    dma(S0.ap[:, :, :], sts_d[s, :, :, :].rearrange("(hb two) p n -> (two p) hb n", two=2), [], [S0.k()], eng="sync")

                def sfront(s):
                    S0, S0T = S0s[s % 3], S0Ts[s % 2]
                    for qd in range(4):
                        bank, bk = PS(pool=(0, 1))
                        transpose_multi([(bank[:, q * 128:(q + 1) * 128], S0.ap[:, 4 * qd + q, :], identf) for q in range(4)],
                                        [S0.k(), mk.k()], [bk])
                        copy(S0T.ap[:, qd * 512:(qd + 1) * 512], bank[:, :], [bk], [S0T.k(qd)])

                def sback(s):
                    S0, S0T, Sn, Bm = S0s[s % 3], S0Ts[s % 2], Sns[s % 2], Bms[s % 2]
                    mm_multi([(banks[4 + gq][0:64, 0:512], CmT.ap[:, gq, s * 64:(s + 1) * 64], S0T.ap[:, gq * 512:(gq + 1) * 512],
                               s == 0, s == 15, None) for gq in range(4)], CMK + [S0T.k(qd) for qd in range(4)], yk)
                    ts(Bm.ap[:, :], btok.ap[0:64, 0, :], MK("seg", parts=64, cols=16)[:, s:s + 1], 1.0, ALU.mult, ALU.mult,
                       [btok.k(q) for q in range(4)] + [mk.k()], [Bm.k()], eng="gpsimd")
                    for qd in range(4):
                        bank, bk = PS(pool=(2, 3))
                        mm_multi([(bank[:, q * 128:(q + 1) * 128], xw.ap[0:64, (4 * qd + q) * 128:(4 * qd + q + 1) * 128],
                                   Bm.ap[0:64, qd * 128:(qd + 1) * 128], True, True, None) for q in range(4)], [xw.k(), Bm.k()], [bk])
                        for q in range(4):
                            hb = 4 * qd + q
                            stt(Sn.ap[:, hb, :], S0.ap[:, hb, :], decn.ap[:, hb * 16 + s:hb * 16 + s + 1], bank[:, q * 128:(q + 1) * 128],
                                ALU.mult, ALU.add, [S0.k(), decn.k(), bk], [Sn.k(hb)])
                    okey = ("out_ss", s)
                    dma(ss_d[s, :, :, :].rearrange("(hb two) p n -> (two p) hb n", two=2), Sn.ap[:, :, :],
                        [Sn.k(hb) for hb in range(16)], [okey], eng="gpsimd")
                    out_keys.append(okey)

                sload(0)
                sload(1)
                sfront(0)
                for s in range(16):
                    if s + 2 < 16:
                        sload(s + 2)
                    if s + 1 < 16:
                        sfront(s + 1)
                    sback(s)
                for gq in range(4):
                    tt(yis.ap[0:64, gq * 512:(gq + 1) * 512].rearrange("p (h q) -> p h q", q=64),
                       banks[4 + gq][0:64, 0:512].rearrange("p (h q) -> p h q", q=64),
                       ecum[0:64, 8 * gq:8 * gq + 8].unsqueeze(2).to_broadcast([64, 8, 64]), ALU.mult, [yk[gq], smt.k(4)], [yis.k(gq)])

        if not int(os.environ.get("SKIP0", "0")):
            layer0_mixer()
        A.pop()
        if not int(os.environ.get("SKIP0", "0")):
            if stage >= 2:
                ffn(0)
        if stage >= 3:
            layer1_mixer()
        if stage >= 4:
            ffn(1)
        store_out()
        if pi == 1 or True:
            pass

    ST = A.alloc("ST", [128, 2048])
    STb = A.alloc("STb", [128, 2048], BF16)
    hist32 = A.alloc("hist32", [128, 24, 4])
    if int(os.environ.get("SKIP0", "0")):
        for b_ in (S_h, S_g):
            P.op("vector", lambda e, b_=b_: e.memset(b_.ap, 0.0), writes=[b_.k()])
    epsb = A.alloc("epsb", [128, 2])
    P.op("vector", lambda e: e.memset(epsb.ap[:, :], EPS), writes=["epsb_key"])

    A.push()
    run_pass(0)
    A.pop()
    A.push()
    run_pass(1)
    A.pop()

    okey = ("out_hp",)
    dma(hp_d[:, :, :].rearrange("h k v -> k h v"), S_h.ap[:, :, :], [S_h.k()], [okey])
    out_keys.append(okey)
    okey = ("out_gp",)
    dma(gp_d[:, :, :].rearrange("h k v -> k h v"), S_g.ap[:, :, :], [S_g.k()], [okey])
    out_keys.append(okey)

    if stage >= 3:
        A.push()
        spn = A.alloc("spn", [128, 16, 128])
        for qd in range(4):
            bank, bk = PS()
            transpose_multi([(bank[:, q * 128:(q + 1) * 128], ST.ap[:, (4 * qd + q) * 128:(4 * qd + q + 1) * 128], identf)
                             for q in range(4)], [ST.k(g_) for g_ in range(4)] + [mk.k()], [bk])
            copy(spn.ap[:, 4 * qd:4 * qd + 4, :], bank[:, :].rearrange("p (q d) -> p q d", q=4), [bk], [spn.k(qd)])
        okey = ("out_sp",)
        dma(sp_d.rearrange("(hb two) p n -> (two p) hb n", two=2), spn.ap[:, :, :], [spn.k(qd) for qd in range(4)], [okey])
        out_keys.append(okey)
        cpst = A.alloc("cpst", [3, 3072])
        for b6 in range(6):
            bank, bk = PS()
            transpose_multi([(bank[0:3, q * 128:(q + 1) * 128], hist32.ap[:, 4 * b6 + q, 1:4], identf) for q in range(4)],
                            [hist32.k(4 * b6 + q) for q in range(4)] + [mk.k()], [bk])
            copy(cpst.ap[0:3, 512 * b6:512 * (b6 + 1)], bank[0:3, :], [bk], [cpst.k(b6)])
        okey = ("out_cp",)
        dma(cp_d[:, :], cpst.ap[:, :], [cpst.k(b6) for b6 in range(6)], [okey])
        out_keys.append(okey)
        A.pop()
    P.finish_wait("sync", out_keys)
    P.emit(es)
    nc._dbgP = P
    es.close()
    return nc, A.peak


def _pack_cvec(inp):
    cvv = np.zeros((128, CV_N), np.float32)

    def put(name, arr):
        o, w = CV_LAY[name]
        assert arr.shape == (128, w), (name, arr.shape, w)
        cvv[:, o:o + w] = arr

    def fm(v):
        L = v.shape[0]
        return np.ascontiguousarray(v.reshape(L, 8, 128).transpose(2, 0, 1).reshape(128, L * 8))

    put("nmpre", fm(inp["norm_mix_pre"]))
    put("nmpost", fm(inp["norm_mix_post"]))
    put("nfpre", fm(inp["norm_ffn_pre"]))
    put("nfpost", fm(inp["norm_ffn_post"]))
    put("gamma", np.ascontiguousarray(inp["hgrn_gamma"].reshape(3, 4, 128).transpose(2, 0, 1).reshape(128, 12)))
    ba = np.zeros((128, 4), np.float32)
    ba[0:64, :] = inp["ev_b_alpha"][0].reshape(4, 64).T
    put("balpha", ba)
    put("norma", inp["ev_norm_a"][0].reshape(128, 1))
    put("normb", inp["ev_norm_b"][0].reshape(128, 1))
    put("convw", np.ascontiguousarray(inp["od_conv_w"][0].reshape(4, 24, 128).transpose(2, 0, 1).reshape(128, 96)))
    put("convb", np.ascontiguousarray(inp["od_conv_b"][0].reshape(24, 128).T))
    put("dtb", np.broadcast_to(inp["od_dt_bias"][0][None, :], (128, 32)))
    put("alog", np.broadcast_to(inp["od_a_log"][0][None, :], (128, 32)))
    put("dskip", np.broadcast_to(inp["od_d_skip"][0][None, :], (128, 32)))
    put("odnorm", np.broadcast_to(inp["od_norm"][0][None, :], (128, 2048)))
    return cvv


_PROG_CACHE = {}


def kernel(**inputs):
    inp = {k: np.asarray(v) for k, v in inputs.items()}
    SEQ = inp["x_prompt"].shape[1]
    stage = int(inp.pop("_stage", 99)) if "_stage" in inp else 99
    key = (SEQ, stage)
    if key not in _PROG_CACHE:
        _PROG_CACHE[key] = build_program(SEQ, stage)
    nc, _ = _PROG_CACHE[key]
    cvec = _pack_cvec(inp)
    masks = _build_masks()
    f = lambda a: np.ascontiguousarray(a, dtype=np.float32)
    shared = {
        "meta": f(inp["meta_tokens"]), "w_in0": f(inp["ev_w_in"][0]), "w_up": f(inp["ev_w_alpha_up"][0]),
        "w_out0": f(inp["ev_w_out"][0]), "w_in1": f(inp["od_w_in"][0]), "w_out1": f(inp["od_w_out"][0]),
        "w_g": f(inp["ffn_w_gate"]), "w_u": f(inp["ffn_w_up"]), "w_d": f(inp["ffn_w_down"]),
        "cvec": cvec, "masks": masks,
    }
    in_maps = []
    for c in range(8):
        m = dict(shared)
        m["xp"] = f(inp["x_prompt"][c])
        m["xs"] = f(inp["x_sample"][16 * c:16 * c + 16].reshape(64, D))
        m["st_h"] = f(inp["state_hgrn"][0, 16 * c:16 * c + 16])
        m["st_g"] = f(inp["state_gla"][0, 16 * c:16 * c + 16])
        m["st_s"] = f(inp["state_ssm"][0, 16 * c:16 * c + 16])
        m["st_c"] = f(inp["state_conv"][0, 16 * c:16 * c + 16])
        in_maps.append(m)
    if ONECORE:
        res = run_bass_kernel_spmd(nc, in_maps[:1], core_ids=[0])
        R = [res.results[0]] * 8
    else:
        res = run_bass_kernel_spmd(nc, in_maps, core_ids=list(range(8)))
        R = res.results
    cat = lambda k: np.stack([np.asarray(r[k]) for r in R], axis=0)
    y_prompt = cat("yp")
    y_sample = np.concatenate([np.asarray(r["ys"]).reshape(16, 4, D) for r in R], axis=0)
    hgrn_p = cat("hp")[None]
    gla_p = cat("gp")[None]
    ssm_p = cat("sp")[None]
    conv_p = cat("cp")[None]
    hgrn_s = np.concatenate([np.asarray(r["hs"]) for r in R], axis=0)[None]
    gla_s = np.concatenate([np.asarray(r["gs"]) for r in R], axis=0)[None]
    ssm_s = np.concatenate([np.asarray(r["ss"]) for r in R], axis=0)[None]
    conv_s = np.concatenate([np.asarray(r["cs"]) for r in R], axis=0)[None]
    return (y_prompt, y_sample, hgrn_p, gla_p, ssm_p, conv_p, hgrn_s, gla_s, ssm_s, conv_s)
```

```python
import contextlib
import os
import numpy as np
import concourse.bass as bass
import concourse.mybir as mybir
from concourse.bass_utils import run_bass_kernel_spmd

F32 = mybir.dt.float32
BF16 = mybir.dt.bfloat16
AF = mybir.ActivationFunctionType
ALU = mybir.AluOpType

ENGINES = ("sync", "scalar", "gpsimd", "vector", "tensor")
N_DMA_SEMS = 32
EPS = 1e-6
D = 1024
DFF = 2816
IN_EVEN = 3600
IN_ODD = 5152


class _Op:
    __slots__ = ("eng", "fn", "dma", "waits", "sem", "val", "ninc", "attach")


class Prog:
    def __init__(self, nc):
        self.nc = nc
        self.ops = []
        self.cnt = {e: 0 for e in ENGINES}
        self.dma_cnt = [0] * N_DMA_SEMS
        self.dma_rr = {"hw": 0, "sw": 0}
        self.last_w = {}
        self.readers = {}
        self.known = {e: {} for e in ENGINES}
        self.op_clock = {}
        self.base_keys = {}
        self.base_deps = {}

    def _need(self, op, dep):
        if dep is None:
            return
        sk, v = dep
        if sk == ("e", "tensor") and op.eng == "tensor":
            return
        kn = self.known[op.eng]
        if kn.get(sk, 0) >= v:
            return
        kn[sk] = v
        op.waits.append((sk, v))
        clk = self.op_clock.get((sk, v))
        if clk:
            for k2, v2 in clk.items():
                if kn.get(k2, 0) < v2:
                    kn[k2] = v2

    def retire_deps(self, bases):
        deps = {}
        for b in bases:
            for k in self.base_keys.get(b, ()):
                lw = self.last_w.get(k)
                if lw is not None:
                    deps[lw[0]] = max(deps.get(lw[0], 0), lw[1])
                for r in self.readers.get(k, ()):
                    deps[r[0]] = max(deps.get(r[0], 0), r[1])
        return deps

    def set_base_deps(self, base, deps):
        if deps:
            cur = self.base_deps.setdefault(base, {})
            for sk, v in deps.items():
                cur[sk] = max(cur.get(sk, 0), v)

    def op(self, eng, fn, reads=(), writes=(), dma=False, ndma=1, attach=False):
        o = _Op()
        o.eng, o.fn, o.dma, o.waits = eng, fn, dma, []
        o.attach = attach and not dma
        reads, writes = list(reads), list(writes)
        for k in reads:
            if isinstance(k, tuple) and k[0] == "ps":
                for r in self.readers.get(k, ()):
                    if r[0] != ("e", eng):
                        self._need(o, r)
        for k in list(reads) + list(writes):
            b = k[0] if isinstance(k, tuple) else k
            self.base_keys.setdefault(b, set()).add(k)
            bd = self.base_deps.get(b)
            if bd:
                for sk, v in bd.items():
                    self._need(o, (sk, v))
        for k in reads:
            self._need(o, self.last_w.get(k))
        for k in writes:
            self._need(o, self.last_w.get(k))
            for r in self.readers.get(k, ()):
                self._need(o, r)
        if dma:
            half = N_DMA_SEMS // 2
            kind = "sw" if eng == "gpsimd" else "hw"
            s = self.dma_rr[kind] + (half if kind == "sw" else 0)
            self.dma_rr[kind] = (self.dma_rr[kind] + 1) % half
            if self.dma_cnt[s] > 0:
                self._need(o, (("d", s), 16 * self.dma_cnt[s]))
            self.dma_cnt[s] += ndma
            o.sem, o.val, o.ninc = ("d", s), 16 * self.dma_cnt[s], ndma
        else:
            self.cnt[eng] += 1
            o.sem, o.val, o.ninc = ("e", eng), self.cnt[eng], 1
        me = (o.sem, o.val)
        clk = dict(self.known[eng])
        if not dma:
            clk[me[0]] = me[1]
        self.op_clock[me] = clk
        for k in writes:
            self.last_w[k] = me
            self.readers[k] = []
        for k in reads:
            self.readers.setdefault(k, []).append(me)
        self.ops.append(o)
        return o

    def finish_wait(self, eng, keys):
        o = _Op()
        o.eng, o.fn, o.dma, o.waits = eng, None, False, []
        o.attach = False
        for k in keys:
            self._need(o, self.last_w.get(k))
        o.sem = None
        self.ops.append(o)

    def emit(self, es):
        nc = self.nc
        sems = {}
        for e in ENGINES:
            sems[("e", e)] = es.enter_context(nc.semaphore("se_" + e))
        for i in range(N_DMA_SEMS):
            sems[("d", i)] = es.enter_context(nc.semaphore("sd_%d" % i))
        block = es.enter_context(nc.Block())
        per = {e: [o for o in self.ops if o.eng == e] for e in ENGINES}

        def run(eh, ops):
            for o in ops:
                ws = list(o.waits)
                held = ws.pop() if (o.attach and ws and o.fn is not None) else None
                for sk, v in ws:
                    eh.wait_ge(sems[sk], v)
                if o.fn is None:
                    continue
                r = o.fn(eh)
                if held is not None:
                    first = r[0] if isinstance(r, (list, tuple)) else r
                    first._wait_ge(sems[held[0]], held[1])
                if o.dma:
                    if not isinstance(r, (list, tuple)):
                        r = [r]
                    assert len(r) == o.ninc, (len(r), o.ninc)
                    for ins in r:
                        ins.then_inc(sems[o.sem], 16)
                else:
                    if isinstance(r, (list, tuple)):
                        r = r[-1]
                    r.then_inc(sems[o.sem], 1)

        @block.sync
        def _(e):
            run(e, per["sync"])

        @block.scalar
        def _(e):
            run(e, per["scalar"])

        @block.gpsimd
        def _(e):
            run(e, per["gpsimd"])

        @block.vector
        def _(e):
            run(e, per["vector"])

        @block.tensor
        def _(e):
            run(e, per["tensor"])


class _Cut(Exception):
    pass


CUT = float(os.environ.get("L1CUT", "99"))
ONECORE = int(os.environ.get("K1CORE", "0"))


def cutpoint(k):
    if CUT <= k:
        raise _Cut()


class Buf:
    __slots__ = ("ap", "name")

    def __init__(self, ap, name):
        self.ap, self.name = ap, name

    def k(self, *idx):
        return (self.name,) + idx if idx else self.name


class Arena:
    def __init__(self, P, big, nbytes):
        self.P, self.big, self.nbytes = P, big, nbytes
        self.off = 0
        self.stack = []
        self.live = []
        self.retired = []
        self.uid = 0
        self.peak = 0

    def alloc(self, name, shape, dt=F32):
        self.uid += 1
        name = "%s#%d" % (name, self.uid)
        esz = 4 if dt == F32 else 2
        n = 1
        for s in shape[1:]:
            n *= s
        nb = (n * esz + 63) // 64 * 64
        st = self.off
        self.off += nb
        self.peak = max(self.peak, self.off)
        assert self.off <= self.nbytes, ("SBUF arena overflow", name, self.off)
        ap = self.big[0:shape[0], st // 4:(st + n * esz + 3) // 4]
        if dt != F32:
            ap = ap.bitcast(dt)
            if n % 2:
                ap = ap[:, 0:n]
        if len(shape) == 3:
            ap = ap.rearrange("p (a b) -> p a b", a=shape[1])
        elif len(shape) == 4:
            ap = ap.rearrange("p (a b c) -> p a b c", a=shape[1], b=shape[2])
        deps = {}
        for (rs, re, rd) in self.retired:
            if rs < st + nb and st < re:
                for sk, v in rd.items():
                    deps[sk] = max(deps.get(sk, 0), v)
        self.P.set_base_deps(name, deps)
        self.live.append((st, st + nb, name))
        return Buf(ap, name)

    def push(self):
        self.stack.append((self.off, len(self.live)))

    def pop(self):
        off, nl = self.stack.pop()
        for (st, en, name) in self.live[nl:]:
            self.retired.append((st, en, self.P.retire_deps([name])))
        del self.live[nl:]
        self.off = off

    @contextlib.contextmanager
    def scope(self):
        self.push()
        try:
            yield
        finally:
            self.pop()


def _cvec_layout():
    names = [("nmpre", 16), ("nmpost", 16), ("nfpre", 16), ("nfpost", 16), ("gamma", 12), ("balpha", 4),
             ("norma", 1), ("normb", 1), ("convw", 96), ("convb", 24), ("dtb", 32), ("alog", 32),
             ("dskip", 32), ("odnorm", 2048)]
    off, lay = 0, {}
    for n, w in names:
        lay[n] = (off, w)
        off += w
    return lay, off


CV_LAY, CV_N = _cvec_layout()


def _mask_layout():
    names = [("identf", 128), ("ones", 128), ("mask64", 512), ("maskms", 80), ("bd64", 128), ("causal", 128),
             ("cms", 80), ("seg", 16), ("ncausal", 128), ("ncms", 80), ("sseg", 80)]
    off, lay = 0, {}
    for n, w in names:
        lay[n] = (off, w)
        off += w
    return lay, off


MK_LAY, MK_N = _mask_layout()


def _build_masks():
    m = np.zeros((128, MK_N), np.float32)

    def put(name, arr):
        o, w = MK_LAY[name]
        m[:arr.shape[0], o:o + arr.shape[1]] = arr

    put("identf", np.eye(128, dtype=np.float32))
    put("ones", np.ones((128, 128), np.float32))
    r = np.ones((128, 512), np.float32)
    r[:, ::64] = 0.0
    put("mask64", r)
    r = np.ones((128, 80), np.float32)
    r[:, 0:64:4] = 0.0
    r[:, 64] = 0.0
    put("maskms", r)
    j = np.arange(128)[:, None]
    i = np.arange(128)[None, :]
    causal = (i >= j).astype(np.float32)
    put("causal", causal)
    put("bd64", causal * ((i // 64) == (j // 64)))
    seg_id = np.concatenate([np.arange(64) // 4, np.full(16, 16)])
    cms = ((seg_id[:, None] == seg_id[None, :]) & (np.arange(80)[None, :] >= np.arange(80)[:, None])).astype(np.float32)
    put("cms", cms)
    seg = np.zeros((80, 16), np.float32)
    seg[np.arange(64), np.arange(64) // 4] = 1.0
    put("seg", seg)
    put("ncausal", (causal - 1.0) * 30000.0)
    put("ncms", (cms - 1.0) * 30000.0)
    put("sseg", (seg_id[:, None] == seg_id[None, :]).astype(np.float32))
    return m


def build_program(SEQ, stage=99):
    NT = SEQ // 128
    n0 = NT // 2
    n1 = NT - n0
    nc = bass.Bass("TRN2", target_bir_lowering=False)

    def din(name, shape, dt=F32):
        return nc.dram_tensor(name, shape, dt, kind="ExternalInput").ap()

    def dout(name, shape):
        return nc.dram_tensor(name, shape, F32, kind="ExternalOutput").ap()

    xp_d = din("xp", [SEQ, D])
    xs_d = din("xs", [64, D])
    meta_d = din("meta", [16, D])
    sth_d = din("st_h", [16, 4, 128, 128])
    stg_d = din("st_g", [16, 4, 64, 128])
    sts_d = din("st_s", [16, 32, 64, 128])
    stc_d = din("st_c", [16, 3, 3072])
    win0_d = din("w_in0", [D, IN_EVEN])
    wup_d = din("w_up", [16, 256])
    wout0_d = din("w_out0", [D, D])
    win1_d = din("w_in1", [D, IN_ODD])
    wout1_d = din("w_out1", [2048, D])
    wg_d = din("w_g", [2, D, DFF])
    wu_d = din("w_u", [2, D, DFF])
    wd_d = din("w_d", [2, DFF, D])
    cvec_d = din("cvec", [128, CV_N])
    mask_d = din("masks", [128, MK_N])

    yp_d = dout("yp", [SEQ, D])
    ys_d = dout("ys", [64, D])
    hp_d = dout("hp", [4, 128, 128])
    gp_d = dout("gp", [4, 64, 128])
    sp_d = dout("sp", [32, 64, 128])
    cp_d = dout("cp", [3, 3072])
    hs_d = dout("hs", [16, 4, 128, 128])
    gs_d = dout("gs", [16, 4, 64, 128])
    ss_d = dout("ss", [16, 32, 64, 128])
    cs_d = dout("cs", [16, 3, 3072])
    out_keys = []

    es = contextlib.ExitStack()
    ARENA_BYTES = 212000
    big = es.enter_context(nc.sbuf_tensor("arena", [128, ARENA_BYTES // 4], F32))
    banks = [es.enter_context(nc.psum_tensor("bank%d" % i, [128, 512], F32)) for i in range(8)]
    P = Prog(nc)
    A = Arena(P, big, ARENA_BYTES)

    ps_state = {"i": 0}

    def PS(pool=(0, 1, 2, 3, 4, 5, 6, 7)):
        ps_state["i"] += 1
        b = pool[ps_state["i"] % len(pool)]
        return banks[b], ("ps", b)

    def bfview(bank):
        return bank[:, :].bitcast(BF16)

    rr = {"dmaq": 0, "ev": 0}

    def dma(out, in_, reads, writes, eng=None):
        if eng is None:
            eng = "sync"
        P.op(eng, lambda e: e.dma_start(out=out, in_=in_), reads=reads, writes=writes, dma=True)

    def act(out, in_, func, reads, writes, scale=1.0, bias=0.0):
        reads = list(reads)
        if not isinstance(bias, float):
            reads.append("epsb_key")
        P.op("scalar", lambda e: e.activation(out=out, in_=in_, func=func, scale=scale, bias=bias),
             reads=reads, writes=writes, attach=True)

    def tt(out, in0, in1, op, reads, writes, eng="vector"):
        P.op(eng, lambda e: e.tensor_tensor(out=out, in0=in0, in1=in1, op=op), reads=reads, writes=writes, attach=True)

    def ts(out, in0, s1, s2, op0, op1, reads, writes, eng="vector"):
        P.op(eng, lambda e: e.tensor_scalar(out=out, in0=in0, scalar1=s1, scalar2=s2, op0=op0, op1=op1),
             reads=reads, writes=writes, attach=True)

    def stt(out, in0, scalar, in1, op0, op1, reads, writes):
        P.op("vector", lambda e: e.scalar_tensor_tensor(out=out, in0=in0, scalar=scalar, in1=in1, op0=op0, op1=op1),
             reads=reads, writes=writes, attach=True)

    def copy(out, in_, reads, writes, eng=None):
        if eng is None:
            rr["ev"] += 1
            eng = "vector" if rr["ev"] % 2 else "scalar"
        if eng == "scalar":
            act(out, in_, AF.Copy, reads, writes)
        else:
            P.op(eng, lambda e: e.tensor_copy(out=out, in_=in_), reads=reads, writes=writes, attach=True)

    def mm(out, pairs, reads, writes, tile_position=None, first=True, last=True):
        def fn(e):
            rs = []
            n = len(pairs)
            for i, (l, rh) in enumerate(pairs):
                kw = {}
                if tile_position is not None:
                    kw["tile_position"] = tile_position
                rs.append(e.matmul(out, lhsT=l, rhs=rh, start=(first and i == 0), stop=(last and i == n - 1), **kw))
            return rs
        P.op("tensor", fn, reads=reads, writes=writes, attach=True)

    def mm_multi(items, reads, writes):
        def fn(e):
            rs = []
            for (o, l, rh, st, sp, tp) in items:
                kw = {}
                if tp is not None:
                    kw["tile_position"] = tp
                rs.append(e.matmul(o, lhsT=l, rhs=rh, start=st, stop=sp, **kw))
            return rs
        P.op("tensor", fn, reads=reads, writes=writes, attach=True)

    def transpose_multi(items, reads, writes):
        def fn(e):
            rs = []
            for (o, i_, idn) in items:
                rs.append(e.transpose(out=o, in_=i_, identity=idn))
            return rs
        P.op("tensor", fn, reads=reads, writes=writes, attach=True)

    cv = A.alloc("cvec", [128, CV_N])
    mk = A.alloc("masks", [128, MK_N])
    dma(cv.ap, cvec_d[:, :], [], [cv.k()])
    dma(mk.ap, mask_d[:, :], [], [mk.k()])

    def CV(name, parts=128):
        o, w = CV_LAY[name]
        return cv.ap[0:parts, o:o + w]

    def MK(name, parts=128, cols=None):
        o, w = MK_LAY[name]
        if cols is not None:
            w = cols
        return mk.ap[0:parts, o:o + w]

    cb = A.alloc("cbf", [128, 128 * 4 + 16], BF16)
    identb = cb.ap[:, 0:128]
    onesb = cb.ap[:, 128:256]
    ncausb = cb.ap[:, 256:384]
    ncmsb = cb.ap[:, 384:464]
    segb = cb.ap[:, 512:528]
    copy(identb, MK("identf"), [mk.k()], [cb.k(0)], eng="vector")
    copy(onesb, MK("ones"), [mk.k()], [cb.k(1)], eng="vector")
    copy(ncausb, MK("ncausal"), [mk.k()], [cb.k(2)], eng="vector")
    copy(ncmsb, MK("ncms"), [mk.k()], [cb.k(3)], eng="vector")
    copy(segb, MK("seg"), [mk.k()], [cb.k(4)], eng="vector")
    CBK = [cb.k(i) for i in range(5)]
    identf = MK("identf")
    onesf = MK("ones")

    lbb = A.alloc("lb", [128, 16])
    g_o, _ = CV_LAY["gamma"]
    gam = cv.ap[:, g_o:g_o + 12].rearrange("p (l h) -> p l h", l=3)
    eg = lbb.ap[:, 0:12].rearrange("p (l h) -> p l h", l=3)
    act(lbb.ap[:, 0:12], cv.ap[:, g_o:g_o + 12], AF.Exp, [cv.k()], [lbb.k()])
    sm = A.alloc("lbtmp", [128, 8])
    tt(sm.ap[:, 0:4], eg[:, 0, :], eg[:, 1, :], ALU.add, [lbb.k()], [sm.k()])
    tt(sm.ap[:, 0:4], sm.ap[:, 0:4], eg[:, 2, :], ALU.add, [lbb.k(), sm.k()], [sm.k()])
    P.op("vector", lambda e: e.reciprocal(out=sm.ap[:, 4:8], in_=sm.ap[:, 0:4]), reads=[sm.k()], writes=[sm.k()])
    LB = lbb.ap[:, 12:16]
    tt(LB, eg[:, 0, :], sm.ap[:, 4:8], ALU.mult, [lbb.k(), sm.k()], [lbb.k()])
    OML = sm.ap[:, 0:4]
    ts(OML, LB, -1.0, 1.0, ALU.mult, ALU.add, [lbb.k(), sm.k()], [sm.k()])
    LBK = [lbb.k(), sm.k()]

    S_h = A.alloc("S_h", [128, 4, 128])
    S_g = A.alloc("S_g", [64, 4, 128])
    Sbf_h = A.alloc("Sbf_h", [128, 4, 128], BF16)
    Sbf_g = A.alloc("Sbf_g", [64, 4, 128], BF16)

    def run_pass(pi):
        npt = n0 if pi == 0 else n1
        tile0 = 0 if pi == 0 else n0
        PC = 128 * npt
        has_ms = (pi == 0)
        Tp = PC + (80 if has_ms else 0)
        MS0 = PC
        groups = []
        c = 0
        while c < PC:
            n = min(512, PC - c)
            groups.append(("P", c, n))
            c += n
        if has_ms:
            groups = [("MS", MS0, 80)] + groups

        xT = A.alloc("xT", [128, 8, Tp])
        A.push()
        W0 = A.alloc("W0", [128, 8, IN_EVEN], BF16)
        for kc in range(8):
            dma(W0.ap[:, kc, :], win0_d[kc * 128:(kc + 1) * 128, :], [], [W0.k(kc)], eng="gpsimd")
        W0K = [W0.k(kc) for kc in range(8)]
        wup = A.alloc("wup", [16, 256], BF16)
        dma(wup.ap[:, :], wup_d[:, :], [], [wup.k()], eng="gpsimd")
        XK = lambda g: xT.k(g)

        def gkey(buf, g):
            return buf.k(g[1])

        with A.scope():
            stg = [A.alloc("instage", [128, D]) for _ in range(2)]
            units = []
            if has_ms:
                units.append(("MS", MS0, 80))
            for t in range(npt):
                units.append(("T", 128 * t, 128))
            for ui, (kind, c0, nr) in enumerate(units):
                sb = stg[ui % 2]
                if kind == "MS":
                    dma(sb.ap[0:64, :], xs_d[:, :], [], [sb.k()])
                    dma(sb.ap[64:80, :], meta_d[:, :], [], [sb.k(1)], eng="scalar")
                    rk = [sb.k(), sb.k(1)]
                else:
                    r0 = (tile0 + c0 // 128) * 128
                    dma(sb.ap[:, :], xp_d[r0:r0 + 128, :], [], [sb.k()], eng=("sync" if ui % 2 else "scalar"))
                    rk = [sb.k()]
                gk = xT.k(("MS", MS0) if kind == "MS" else (c0 // 512) * 512)
                for half in range(2):
                    bank, bk = PS()
                    transpose_multi([(bank[:, q * 128:q * 128 + nr], sb.ap[0:nr, (half * 4 + q) * 128:(half * 4 + q + 1) * 128],
                                      identf[0:nr, 0:nr]) for q in range(4)], rk + [mk.k()], [bk])
                    copy(xT.ap[:, half * 4:half * 4 + 4, c0:c0 + nr],
                         bank[:, :].rearrange("p (q c) -> p q c", q=4)[:, :, 0:nr], [bk], [(xT.name, "in", ui, half)])
            XIN_KEYS = [(xT.name, "in", ui, h) for ui in range(len(units)) for h in range(2)]

        def xkeys_for(g):
            return XIN_KEYS + [xT.k(g[1])]

        def fm_rstd(src_fn, nchunks, n, scale_div, reads, sq, rstd, tagk):
            for c in range(nchunks):
                act(sq.ap[:, c, 0:n], src_fn(c), AF.Square, reads, [sq.k(c)])
            bank, bk = PS()
            mm(bank[:, 0:n], [(onesb, sq.ap[:, c, 0:n]) for c in range(nchunks)],
               [sq.k(c) for c in range(nchunks)] + CBK, [bk])
            act(rstd.ap[:, 0:n], bank[:, 0:n], AF.Ln, [bk], [rstd.k()], scale=1.0 / scale_div, bias=epsb.ap[:, 0:1])
            act(rstd.ap[:, 0:n], rstd.ap[:, 0:n], AF.Exp, [rstd.k()], [rstd.k()], scale=-0.5)

        def prenorm(g, wname, layer, hn, hn_c0, sq, rstd):
            kind, c0, n = g
            o, _ = CV_LAY[wname]
            fm_rstd(lambda c: xT.ap[:, c, c0:c0 + n], 8, n, float(D), xkeys_for(g), sq, rstd, None)
            for c in range(8):
                stt(hn.ap[:, c, hn_c0:hn_c0 + n], xT.ap[:, c, c0:c0 + n], cv.ap[:, o + layer * 8 + c:o + layer * 8 + c + 1],
                    rstd.ap[:, 0:n], ALU.mult, ALU.mult, xkeys_for(g) + [rstd.k(), cv.k()], [hn.k(g[1], c)])

        def postnorm_add(g, wname, layer, mix, sq, rstd):
            kind, c0, n = g
            o, _ = CV_LAY[wname]
            fm_rstd(lambda c: mix.ap[:, c, 0:n], 8, n, float(D), [mix.k(c) for c in range(8)], sq, rstd, None)
            for c in range(8):
                tt(mix.ap[:, c, 0:n], mix.ap[:, c, 0:n], rstd.ap[:, 0:n], ALU.mult, [mix.k(c), rstd.k()], [mix.k(c)],
                   eng=("gpsimd" if c % 2 else "vector"))
            for c in range(8):
                stt(xT.ap[:, c, c0:c0 + n], mix.ap[:, c, 0:n], cv.ap[:, o + layer * 8 + c:o + layer * 8 + c + 1],
                    xT.ap[:, c, c0:c0 + n], ALU.mult, ALU.add, xkeys_for(g) + [mix.k(c), cv.k()], [xT.k(g[1])])

        def layer0_mixer():
            for g in groups:
                layer0_group(g, W0, W0K, None, None, wup)

        def layer0_group(g, W0, W0K, WO, WOK, wup):
            kind, c0, n = g
            isms = (kind == "MS")
            ntile = 1 if isms else n // 128
            nrow = 80 if isms else 128
            with A.scope():
                hn = A.alloc("hn", [128, 8, n], BF16)
                yT = A.alloc("yT", [128, 8, n], BF16)
                with A.scope():
                    sq = A.alloc("sq", [128, 8, n], BF16)
                    rstd = A.alloc("rstd", [128, n])
                    prenorm(g, "nmpre", 0, hn, 0, sq, rstd)
                HNK = [hn.k(g[1], c) for c in range(8)]

                def proj_fm(col0, m, nn=n):
                    bank, bk = PS()
                    mm(bank[0:m, 0:nn], [(W0.ap[:, kc, col0:col0 + m], hn.ap[:, kc, 0:nn]) for kc in range(8)],
                       HNK + W0K, [bk])
                    return bank, bk

                with A.scope():
                    sg = A.alloc("sg", [128, 8, n], BF16)
                    qe = A.alloc("qe", [128, 8, n], BF16)
                    ke = A.alloc("ke", [128, 8, n], BF16)
                    Eall = A.alloc("Eall", [128, 8, 20])
                    alow = A.alloc("alow", [16, n], BF16)
                    with A.scope():
                        G1 = A.alloc("G1", [128, 8, n])
                        for h in range(8):
                            col = (1536 + 128 * h) if h < 4 else (3072 + 128 * (h - 4))
                            bank, bk = proj_fm(col, 128)
                            act(sg.ap[:, h, 0:n], bank[:, 0:n], AF.Silu, [bk], [sg.k(h)])
                        for h in range(4):
                            bank, bk = proj_fm(512 + 128 * h, 128)
                            act(G1.ap[:, h, 0:n], bank[:, 0:n], AF.Sigmoid, [bk], [G1.k(h)])
                        bank, bk = proj_fm(3584, 16)
                        copy(alow.ap[:, 0:n], bank[0:16, 0:n], [bk], [alow.k()], eng="vector")
                        bo, _ = CV_LAY["balpha"]
                        for h in range(4):
                            bank, bk = PS()
                            mm(bank[0:64, 0:n], [(wup.ap[0:16, 64 * h:64 * h + 64], alow.ap[0:16, 0:n])],
                               [wup.k(), alow.k()], [bk])
                            act(G1.ap[0:64, 4 + h, 0:n], bank[0:64, 0:n], AF.Sigmoid, [bk, cv.k()], [G1.k(4 + h)],
                                bias=cv.ap[0:64, bo + h:bo + h + 1])
                        rmask = MK("maskms", cols=80) if isms else MK("mask64", cols=n)
                        gsets = [[A.alloc(nm, [128, n]) for nm in ("lf", "cum", "eq", "ek")] for _ in range(2)]
                        for h in range(8):
                            dk = 128 if h < 4 else 64
                            if True:
                                lf, cum, eq, ek = gsets[h % 2]
                                if h < 4:
                                    ts(G1.ap[:, h, 0:n], G1.ap[:, h, 0:n], OML[:, h:h + 1], LB[:, h:h + 1], ALU.mult, ALU.add,
                                       [G1.k(h)] + LBK, [G1.k(h)])
                                    act(lf.ap[:, 0:n], G1.ap[:, h, 0:n], AF.Ln, [G1.k(h)], [lf.k()])
                                    ts(G1.ap[:, h, 0:n], G1.ap[:, h, 0:n], -1.0, 1.0, ALU.mult, ALU.add, [G1.k(h), lf.k()], [G1.k(h)])
                                    esc = 1.0
                                else:
                                    act(lf.ap[0:dk, 0:n], G1.ap[0:dk, h, 0:n], AF.Ln, [G1.k(h)], [lf.k()])
                                    esc = 1.0 / 16.0
                                if isms or h < 4:
                                    P.op("vector", lambda e, cum=cum, lf=lf, dk=dk: e.tensor_tensor_scan(
                                        out=cum.ap[0:dk, 0:n], data0=rmask[0:dk, 0:n], data1=lf.ap[0:dk, 0:n], initial=0.0,
                                        op0=ALU.mult, op1=ALU.add), reads=[lf.k(), mk.k()], writes=[cum.k()])
                                else:
                                    def scan_fn(e, cum=cum, lf=lf, dk=dk):
                                        r_ = None
                                        for t_ in range(n // 128):
                                            r_ = e.tensor_tensor_scan(out=cum.ap[0:dk, 128 * t_:128 * t_ + 128],
                                                                      data0=onesf[0:dk, 0:128], data1=lf.ap[0:dk, 128 * t_:128 * t_ + 128],
                                                                      initial=0.0, op0=ALU.mult, op1=ALU.add)
                                        return r_
                                    P.op("vector", scan_fn, reads=[lf.k(), mk.k()], writes=[cum.k()])
                                act(eq.ap[0:dk, 0:n], cum.ap[0:dk, 0:n], AF.Exp, [cum.k()], [eq.k()], scale=esc)
                                act(ek.ap[0:dk, 0:n], cum.ap[0:dk, 0:n], AF.Exp, [cum.k()], [ek.k()], scale=-esc)
                                if isms:
                                    copy(Eall.ap[0:dk, h, 0:16], eq.ap[0:dk, 0:64].rearrange("p (s j) -> p s j", j=4)[:, :, 3], [eq.k()], [Eall.k(h)], eng="vector")
                                    copy(Eall.ap[0:dk, h, 16:17], eq.ap[0:dk, 79:80], [eq.k()], [Eall.k(h)], eng="vector")
                                else:
                                    CHh = 64 if h < 4 else 128
                                    copy(Eall.ap[0:dk, h, 0:n // CHh], eq.ap[0:dk, 0:n].rearrange("p (c j) -> p c j", j=CHh)[:, :, CHh - 1], [eq.k()], [Eall.k(h)], eng="vector")
                                if h < 4:
                                    bank, bk = proj_fm(128 * h, 128)
                                    tt(qe.ap[:, h, 0:n], bank[:, 0:n], eq.ap[:, 0:n], ALU.mult, [bk, eq.k()], [qe.k(h)])
                                    tt(ke.ap[:, h, 0:n], G1.ap[:, h, 0:n], ek.ap[:, 0:n], ALU.mult, [G1.k(h), ek.k()], [ke.k(h)])
                                else:
                                    bank, bk = proj_fm(2048 + 64 * (h - 4), 64)
                                    stt(qe.ap[0:64, h, 0:n], bank[0:64, 0:n], 0.125, eq.ap[0:64, 0:n], ALU.mult, ALU.mult,
                                        [bk, eq.k()], [qe.k(h)])
                                    bank, bk = proj_fm(2304 + 64 * (h - 4), 64)
                                    tt(ke.ap[0:64, h, 0:n], bank[0:64, 0:n], ek.ap[0:64, 0:n], ALU.mult, [bk, ek.k()], [ke.k(h)])

                    WO = A.alloc("WO0", [128, 8, D], BF16)
                    for kc in range(8):
                        dma(WO.ap[:, kc, :], wout0_d[kc * 128:(kc + 1) * 128, :], [], [WO.k(kc)], eng="gpsimd")
                    WOK = [WO.k(kc) for kc in range(8)]
                    for t in range(ntile):
                        l0 = 128 * t
                        for fam in range(2):
                            dk = 128 if fam == 0 else 64
                            S = S_h if fam == 0 else S_g
                            Sbf = Sbf_h if fam == 0 else Sbf_g
                            nwname = "norma" if fam == 0 else "normb"
                            vcol = 1024 if fam == 0 else 2560
                            hs = [4 * fam + q for q in range(4)]
                            with A.scope():
                                ktok = A.alloc("ktok", [128, 4, 128], BF16)
                                vtok = A.alloc("vtok", [128, 4, 128], BF16)
                                scm = A.alloc("scm", [128, 4, 128], BF16)
                                Sst = A.alloc("Sst", [128, 4, 4, 128], BF16)
                                Ttmp = A.alloc("Ttmp", [128, 4, 128])
                                bank, bk = PS()
                                bv = bfview(bank)
                                transpose_multi([(bv[0:nrow, q * dk:(q + 1) * dk], ke.ap[0:dk, hs[q], l0:l0 + nrow],
                                                  identb[0:dk, 0:dk]) for q in range(4)],
                                                [ke.k(h) for h in hs] + CBK, [bk])
                                copy(ktok.ap[0:nrow, :, 0:dk], bv[0:nrow, 0:4 * dk].rearrange("p (q d) -> p q d", q=4),
                                     [bk], [ktok.k()])
                                bank, bk = PS()
                                mm(bank[0:nrow, 0:512], [(hn.ap[:, kc, l0:l0 + nrow], W0.ap[:, kc, vcol:vcol + 512]) for kc in range(8)],
                                   HNK + W0K, [bk])
                                copy(vtok.ap[0:nrow, :, :], bank[0:nrow, :].rearrange("p (q d) -> p q d", q=4), [bk], [vtok.k()])
                                bank, bk = PS()
                                mm_multi([(bank[0:nrow, q * 128:q * 128 + nrow], ke.ap[0:dk, hs[q], l0:l0 + nrow],
                                           qe.ap[0:dk, hs[q], l0:l0 + nrow], True, True, None) for q in range(4)],
                                         [ke.k(h) for h in hs] + [qe.k(h) for h in hs], [bk])
                                cmask = MK("cms", parts=80, cols=80) if isms else (MK("bd64") if fam == 0 else MK("causal"))
                                CH = 64 if fam == 0 else 128
                                nch = 128 // CH
                                tt(scm.ap[0:nrow, :, 0:nrow], bank[0:nrow, :].rearrange("p (q d) -> p q d", q=4)[:, :, 0:nrow],
                                   cmask.unsqueeze(1).to_broadcast([nrow, 4, nrow]), ALU.mult, [bk, mk.k()], [scm.k()])

                                if not isms:
                                    for c in range(nch):
                                        if c == 0:
                                            copy(Sst.ap[0:dk, 0, :, :], Sbf.ap[0:dk, :, :], [Sbf.k()], [Sst.k(0)], eng="gpsimd")
                                        bank, bk = PS()
                                        mm_multi([(bank[0:dk, q * 128:(q + 1) * 128], ktok.ap[CH * c:CH * c + CH, q, 0:dk],
                                                   vtok.ap[CH * c:CH * c + CH, q, :], True, True, (CH * c, 0)) for q in range(4)],
                                                 [ktok.k(), vtok.k()], [bk])
                                        tt(Ttmp.ap[0:dk, :, :], S.ap[0:dk, :, :], bank[0:dk, :].rearrange("p (q d) -> p q d", q=4),
                                           ALU.add, [S.k(), bk], [Ttmp.k()])
                                        ci = t * nch + c
                                        tt(S.ap[0:dk, :, :], Ttmp.ap[0:dk, :, :],
                                           Eall.ap[0:dk, 4 * fam:4 * fam + 4, ci:ci + 1].to_broadcast([dk, 4, 128]),
                                           ALU.mult, [Ttmp.k()] + [Eall.k(h) for h in hs], [S.k()])
                                        if c < nch - 1:
                                            copy(Sst.ap[0:dk, c + 1, :, :], S.ap[0:dk, :, :], [S.k()], [Sst.k(c + 1)], eng="scalar")
                                        else:
                                            copy(Sbf.ap[0:dk, :, :], S.ap[0:dk, :, :], [S.k()], [Sbf.k()], eng="scalar")
                                    obank, obk = PS()
                                    items = []
                                    for q in range(4):
                                        items.append((obank[:, q * 128:(q + 1) * 128], vtok.ap[:, q, :], scm.ap[:, q, :], True, False, None))
                                        for c in range(nch):
                                            items.append((obank[:, q * 128 + CH * c:q * 128 + CH * c + CH], Sst.ap[0:dk, c, q, :],
                                                          qe.ap[0:dk, hs[q], l0 + CH * c:l0 + CH * c + CH], False, c == nch - 1, None))
                                    mm_multi(items, [vtok.k(), scm.k()] + [Sst.k(c) for c in range(nch)] + [qe.k(h) for h in hs], [obk])
                                else:
                                    bank, bk = PS()
                                    mm_multi([(bank[0:dk, q * 128:(q + 1) * 128], ktok.ap[64:80, q, 0:dk], vtok.ap[64:80, q, :],
                                               True, True, (64, 0)) for q in range(4)], [ktok.k(), vtok.k()], [bk])
                                    tt(S.ap[0:dk, :, :], bank[0:dk, :].rearrange("p (q d) -> p q d", q=4),
                                       Eall.ap[0:dk, 4 * fam:4 * fam + 4, 16:17].to_broadcast([dk, 4, 128]), ALU.mult,
                                       [bk] + [Eall.k(h) for h in hs], [S.k()])
                                    copy(Sbf.ap[0:dk, :, :], S.ap[0:dk, :, :], [S.k()], [Sbf.k()], eng="scalar")
                                    obank, obk = PS(pool=(6, 7))
                                    st_d = sth_d if fam == 0 else stg_d
                                    so_d = hs_d if fam == 0 else gs_d
                                    S0s_ = [A.alloc("S0", [128, 16, 128]) for _ in range(2)]
                                    S0bs_ = [A.alloc("S0b", [128, 16, 128], BF16) for _ in range(2)]
                                    Vbds_ = [A.alloc("Vbd", [64, 16, 128], BF16)] * 2
                                    Sns_ = [A.alloc("Sn", [128, 16, 128]) for _ in range(2)]

                                    def s0_load(q_):
                                        dma(S0s_[q_ % 2].ap[0:dk, :, :], st_d[:, q_, :, :].rearrange("s k v -> k s v"), [],
                                            [S0s_[q_ % 2].k()], eng="sync")
                                    s0_load(0)
                                    for q in range(4):
                                        if True:
                                            S0, S0b, Vbd, Sn = S0s_[q % 2], S0bs_[q % 2], Vbds_[q % 2], Sns_[q % 2]
                                            if q + 1 < 4:
                                                s0_load(q + 1)
                                            copy(S0b.ap[0:dk, :, :], S0.ap[0:dk, :, :], [S0.k()], [S0b.k()], eng="gpsimd")
                                            tt(Vbd.ap[:, :, :], vtok.ap[0:64, q, :].unsqueeze(1).to_broadcast([64, 16, 128]),
                                               segb[0:64, 0:16].unsqueeze(2).to_broadcast([64, 16, 128]), ALU.mult,
                                               [vtok.k()] + CBK, [Vbd.k()])
                                            for qq in range(4):
                                                bank, bk = PS(pool=(0, 1, 2, 3, 4, 5))
                                                mm(bank[0:dk, 0:512], [(ktok.ap[0:64, q, 0:dk],
                                                                        Vbd.ap[:, 4 * qq:4 * qq + 4, :].rearrange("p s d -> p (s d)"))],
                                                   [ktok.k(), Vbd.k()], [bk])
                                                tt(Sn.ap[0:dk, 4 * qq:4 * qq + 4, :], S0.ap[0:dk, 4 * qq:4 * qq + 4, :],
                                                   bank[0:dk, :].rearrange("p (s d) -> p s d", s=4), ALU.add, [S0.k(), bk], [Sn.k(qq)])
                                                tt(Sn.ap[0:dk, 4 * qq:4 * qq + 4, :], Sn.ap[0:dk, 4 * qq:4 * qq + 4, :],
                                                   Eall.ap[0:dk, hs[q], 4 * qq:4 * qq + 4].unsqueeze(2).to_broadcast([dk, 4, 128]),
                                                   ALU.mult, [Sn.k(qq), Eall.k(hs[q])], [Sn.k(qq)])
                                            okey = ("out_s", fam, q)
                                            dma(so_d[:, q, :, :].rearrange("s k v -> k s v"), Sn.ap[0:dk, :, :],
                                                [Sn.k(qq) for qq in range(4)], [okey], eng="gpsimd")
                                            out_keys.append(okey)
                                            items = [(obank[:, q * 128:q * 128 + 80], vtok.ap[0:80, q, :], scm.ap[0:80, q, 0:80],
                                                      True, False, None)]
                                            for s in range(16):
                                                items.append((obank[:, q * 128 + 4 * s:q * 128 + 4 * s + 4], S0b.ap[0:dk, s, :],
                                                              qe.ap[0:dk, hs[q], 4 * s:4 * s + 4], False, s == 15, None))
                                            mm_multi(items, [vtok.k(), scm.k(), S0b.k(), qe.k(hs[q])], [obk])
                                with A.scope():
                                    sqb = A.alloc("sqb", [128, 4, 128], BF16)
                                    rs = A.alloc("rs", [128, 4, 128])
                                    y1 = A.alloc("y1", [128, 4, 128])
                                    o3 = obank[:, :].rearrange("p (q d) -> p q d", q=4)[:, :, 0:nrow]
                                    act(sqb.ap[:, :, 0:nrow], o3, AF.Square, [obk], [sqb.k()])
                                    bank, bk = PS(pool=(0, 1, 2, 3, 4, 5))
                                    mm_multi([(bank[:, q * 128:q * 128 + nrow], onesb, sqb.ap[:, q, 0:nrow], True, True, None)
                                              for q in range(4)], [sqb.k()] + CBK, [bk])
                                    b3 = bank[:, :].rearrange("p (q d) -> p q d", q=4)[:, :, 0:nrow]
                                    act(rs.ap[:, :, 0:nrow], b3, AF.Ln, [bk], [rs.k()], scale=1.0 / 128.0, bias=epsb.ap[:, 0:1])
                                    act(rs.ap[:, :, 0:nrow], rs.ap[:, :, 0:nrow], AF.Exp, [rs.k()], [rs.k()], scale=-0.5)
                                    no, _ = CV_LAY[nwname]
                                    for q in range(4):
                                        stt(y1.ap[:, q, 0:nrow], obank[:, q * 128:q * 128 + nrow], cv.ap[:, no:no + 1],
                                            rs.ap[:, q, 0:nrow], ALU.mult, ALU.mult, [obk, rs.k(), cv.k()], [y1.k(q)])
                                    tt(yT.ap[:, 4 * fam:4 * fam + 4, l0:l0 + nrow], y1.ap[:, :, 0:nrow],
                                       sg.ap[:, 4 * fam:4 * fam + 4, l0:l0 + nrow], ALU.mult,
                                       [y1.k(q) for q in range(4)] + [sg.k(h) for h in hs], [yT.k(fam, t)], eng="gpsimd")
                    YK = [yT.k(fam, t) for fam in range(2) for t in range(ntile)]
                    with A.scope():
                        mix = A.alloc("mix", [128, 8, n])

                        class _HnAlias:
                            ap = hn.ap
                            name = hn.name

                            @staticmethod
                            def k(c):
                                return hn.k(g[1], c)
                        sq = _HnAlias
                        rstd = A.alloc("rstd", [128, n])
                        for oc in range(8):
                            bank, bk = PS()
                            mm(bank[:, 0:n], [(WO.ap[:, hc, oc * 128:(oc + 1) * 128], yT.ap[:, hc, 0:n]) for hc in range(8)],
                               YK + WOK, [bk])
                            copy(mix.ap[:, oc, 0:n], bank[:, 0:n], [bk], [mix.k(oc)])
                        postnorm_add(g, "nmpost", 0, mix, sq, rstd)

        def ffn(layer):
            with A.scope():
                hnF = A.alloc("hnF", [128, 8, Tp], BF16)
                h1 = A.alloc("h1", [128, 22, Tp], BF16)
                with A.scope():
                    sq = A.alloc("sq", [128, 8, 512], BF16)
                    rstd = A.alloc("rstd", [128, 512])
                    for g in groups:
                        prenorm(g, "nfpre", layer, hnF, g[1], sq, rstd)
                with A.scope():
                    wgb = [A.alloc("wgb", [128, 8, 256], BF16) for _ in range(3)]
                    wub = [A.alloc("wub", [128, 8, 256], BF16) for _ in range(3)]
                    sgt = [A.alloc("sgt", [128, 512]) for _ in range(2)]
                    it = 0
                    for jb in range(11):
                        wb, ub = wgb[jb % 3], wub[jb % 3]
                        dma(wb.ap[:, :, :], wg_d[layer, :, jb * 256:(jb + 1) * 256].rearrange("(k p) n -> p k n", p=128),
                            [], [wb.k()], eng="gpsimd")
                        dma(ub.ap[:, :, :], wu_d[layer, :, jb * 256:(jb + 1) * 256].rearrange("(k p) n -> p k n", p=128),
                            [], [ub.k()], eng="gpsimd")
                        for jj in range(2):
                            j = jb * 2 + jj
                            for g in groups:
                                _, c0, n = g
                                hk = [hnF.k(g[1], c) for c in range(8)]
                                gb_, gk = PS()
                                mm(gb_[:, 0:n], [(wb.ap[:, kc, jj * 128:(jj + 1) * 128], hnF.ap[:, kc, c0:c0 + n]) for kc in range(8)],
                                   hk + [wb.k()], [gk])
                                ub_, uk = PS()
                                mm(ub_[:, 0:n], [(ub.ap[:, kc, jj * 128:(jj + 1) * 128], hnF.ap[:, kc, c0:c0 + n]) for kc in range(8)],
                                   hk + [ub.k()], [uk])
                                st_ = sgt[it % 2]
                                it += 1
                                act(st_.ap[:, 0:n], gb_[:, 0:n], AF.Silu, [gk], [st_.k()])
                                tt(h1.ap[:, j, c0:c0 + n], st_.ap[:, 0:n], ub_[:, 0:n], ALU.mult, [st_.k(), uk], [h1.k(j, g[1])])
                with A.scope():
                    mixF = A.alloc("mixF", [128, 8, Tp])
                    wdb = [A.alloc("wdb", [128, 22, 128], BF16) for _ in range(3)]
                    for oc in range(8):
                        wd = wdb[oc % 3]
                        dma(wd.ap[:, :, :], wd_d[layer, :, oc * 128:(oc + 1) * 128].rearrange("(j p) n -> p j n", p=128),
                            [], [wd.k()], eng="gpsimd")
                        for g in groups:
                            _, c0, n = g
                            bank, bk = PS()
                            mm(bank[:, 0:n], [(wd.ap[:, j, :], h1.ap[:, j, c0:c0 + n]) for j in range(22)],
                               [h1.k(j, g[1]) for j in range(22)] + [wd.k()], [bk])
                            copy(mixF.ap[:, oc, c0:c0 + n], bank[:, 0:n], [bk], [mixF.k(g[1], oc)])
                    with A.scope():
                        sq = A.alloc("sq", [128, 8, 512], BF16)
                        rstd = A.alloc("rstd", [128, 512])
                        for g in groups:
                            _, c0, n = g
                            o, _ = CV_LAY["nfpost"]
                            fm_rstd(lambda c: mixF.ap[:, c, c0:c0 + n], 8, n, float(D), [mixF.k(g[1], c) for c in range(8)],
                                    sq, rstd, None)
                            for c in range(8):
                                tt(mixF.ap[:, c, c0:c0 + n], mixF.ap[:, c, c0:c0 + n], rstd.ap[:, 0:n], ALU.mult,
                                   [mixF.k(g[1], c), rstd.k()], [mixF.k(g[1], c)], eng=("gpsimd" if c % 2 else "vector"))
                            for c in range(8):
                                stt(xT.ap[:, c, c0:c0 + n], mixF.ap[:, c, c0:c0 + n],
                                    cv.ap[:, o + layer * 8 + c:o + layer * 8 + c + 1], xT.ap[:, c, c0:c0 + n], ALU.mult, ALU.add,
                                    xkeys_for(g) + [mixF.k(g[1], c), cv.k()], [xT.k(g[1])])

        def store_out():
            with A.scope():
                ost = [A.alloc("ostage", [128, D]) for _ in range(2)]
                units = []
                if has_ms:
                    units.append(("MS", MS0, 80))
                for t in range(npt):
                    units.append(("T", 128 * t, 128))
                for ui, (kind, c0, nr) in enumerate(units):
                    ob = ost[ui % 2]
                    gk = xT.k(MS0) if kind == "MS" else xT.k((c0 // 512) * 512)
                    for half in range(2):
                        bank, bk = PS()
                        transpose_multi([(bank[0:nr, q * 128:(q + 1) * 128], xT.ap[:, half * 4 + q, c0:c0 + nr], identf)
                                         for q in range(4)], XIN_KEYS + [gk, mk.k()], [bk])
                        copy(ob.ap[0:nr, half * 512:(half + 1) * 512], bank[0:nr, :], [bk], [ob.k(half)])
                    if kind == "MS":
                        okey = ("out_ys",)
                        dma(ys_d[:, :], ob.ap[0:64, :], [ob.k(0), ob.k(1)], [okey], eng="sync")
                    else:
                        r0 = (tile0 + c0 // 128) * 128
                        okey = ("out_yp", r0)
                        dma(yp_d[r0:r0 + 128, :], ob.ap[:, :], [ob.k(0), ob.k(1)], [okey], eng=("sync" if ui % 2 else "scalar"))
                    out_keys.append(okey)

        def act_acc(out, in_, func, accum_out, reads, writes):
            P.op("scalar", lambda e: e.activation(out=out, in_=in_, func=func, accum_out=accum_out),
                 reads=reads, writes=writes)

        def layer1_mixer():
            with A.scope():
                negA = A.alloc("negA", [128, 32])
                act(negA.ap[:, :], CV("alog"), AF.Exp, [cv.k()], [negA.k()])
                diagD = A.alloc("diagD", [128, 32, 128], BF16)
                dso_, _ = CV_LAY["dskip"]
                for h_ in range(32):
                    ts(diagD.ap[:, h_, :], identb, cv.ap[:, dso_ + h_:dso_ + h_ + 1], 1.0, ALU.mult, ALU.mult, CBK + [cv.k()],
                       [diagD.k(h_)], eng=("vector" if h_ % 2 else "gpsimd"))
                for g in groups:
                    try:
                        layer1_group(g, negA, diagD)
                    except _Cut:
                        pass

        def layer1_group(g, negA, diagD):
            kind, c0, n = g
            isms = (kind == "MS")
            ntile = 1 if isms else n // 128
            R = 80 if isms else 128
            cwo, _ = CV_LAY["convw"]
            cbo, _ = CV_LAY["convb"]
            onecol = MK("ones")[:, 0:1]
            with A.scope():
                hn = A.alloc("hn1", [128, 8, n], BF16)
                sq = A.alloc("sq1", [128, 8, n], BF16)
                rstd = A.alloc("rstd1", [128, n])
                cutpoint(0.3)
                prenorm(g, "nmpre", 1, hn, 0, sq, rstd)
                cutpoint(0.5)
                HNK = [hn.k(g[1], c) for c in range(8)]
                BT = A.alloc("BT", [128, 4, n], BF16)
                CT = A.alloc("CT", [128, 4, n], BF16)
                xtok = A.alloc("xtok", [128, ntile, 2048], BF16)
                btok = A.alloc("btok", [128, ntile, 512], BF16)
                zs = A.alloc("zs", [128, ntile, 2048], BF16)
                y3T = A.alloc("y3T", [128, 16, n], BF16)
                dtall = A.alloc("dtall", [128, ntile, 32])
                if isms:
                    hsT = A.alloc("hs4", [128, 24, 64], BF16)
                    rawSf = A.alloc("rawSf", [128, 24, 64])
                    with A.scope():
                        stc = A.alloc("stc", [64, 3072])
                        P.op("vector", lambda e: e.memset(stc.ap[:, :], 0.0), writes=[stc.k()])
                        for s_ in range(16):
                            dma(stc.ap[4 * s_ + 1:4 * s_ + 4, :], stc_d[s_, :, :], [stc.k()], [stc.k(1, s_)],
                                eng=("sync" if s_ % 2 else "scalar"))
                        STCK = [stc.k()] + [stc.k(1, s_) for s_ in range(16)]
                        for b6 in range(6):
                            bank, bk = PS()
                            transpose_multi([(bank[:, q * 64:(q + 1) * 64], stc.ap[0:64, (4 * b6 + q) * 128:(4 * b6 + q + 1) * 128],
                                              identf[0:64, 0:64]) for q in range(4)], STCK + [mk.k()], [bk])
                            copy(hsT.ap[:, 4 * b6:4 * b6 + 4, :], bank[:, 0:256].rearrange("p (q r) -> p q r", q=4), [bk],
                                 [hsT.k(b6)])
                cutpoint(1)
                with A.scope():
                    wblk = [A.alloc("wblk", [128, 8, 512], BF16) for _ in range(2)]
                    wdt = A.alloc("wdt", [128, 8, 32], BF16)
                    dma(wdt.ap[:, :, :], win1_d[:, 5120:5152].rearrange("(k p) n -> p k n", p=128), [], [wdt.k()], eng="gpsimd")
                    rawb = [A.alloc("rawb", [128, n + 4], BF16) for _ in range(3)]
                    xcb = [A.alloc("xcb", [128, n], BF16) for _ in range(3)]
                    dgs = [A.alloc("dg", [128, 4, 128], BF16) for _ in range(3)]
                    if isms:
                        rawS = [A.alloc("rawS", [128, 16, 8], BF16) for _ in range(3)]
                        rawM = [A.alloc("rawM", [128, 20], BF16) for _ in range(3)]
                        for rm in rawM:
                            P.op("vector", lambda e, rm=rm: e.memset(rm.ap[:, :], 0.0), writes=[rm.k()])
                    cutpoint(1.2)

                    def load_wblk(b):
                        wb = wblk[b % 2]
                        dma(wb.ap[:, :, :], win1_d[:, 2048 + 512 * b:2048 + 512 * (b + 1)].rearrange("(k p) n -> p k n", p=128),
                            [], [wb.k()], eng="gpsimd")

                    st1 = {}

                    def s1(cc):
                        b, q = cc // 4, cc % 4
                        wb = wblk[b % 2]
                        if q == 0 and b + 1 < 6 and b >= 1:
                            load_wblk(b + 1)
                        bank, bk = PS()
                        mm(bank[:, 0:n], [(wb.ap[:, kc, q * 128:(q + 1) * 128], hn.ap[:, kc, 0:n]) for kc in range(8)],
                           HNK + [wb.k()], [bk])
                        dg = dgs[cc % 3]
                        for k in range(4):
                            ts(dg.ap[:, k, :], identb, cv.ap[:, cwo + k * 24 + cc:cwo + k * 24 + cc + 1], 1.0, ALU.mult, ALU.mult,
                               CBK + [cv.k()], [dg.k(k)], eng="vector")
                        if not isms:
                            rb = rawb[cc % 3]
                            copy(rb.ap[:, 0:4], hist32.ap[:, cc, :], [hist32.k(cc)], [rb.k(0)], eng="vector")
                            copy(rb.ap[:, 4:4 + n], bank[:, 0:n], [bk], [rb.k(1)])
                            copy(hist32.ap[:, cc, :], bank[:, n - 4:n], [bk, rb.k(0)], [hist32.k(cc)], eng="vector")
                        else:
                            rS, rM = rawS[cc % 3], rawM[cc % 3]
                            copy(rS.ap[:, :, 0:4], hsT.ap[:, cc, :].rearrange("p (s k) -> p s k", k=4), [hsT.k(cc // 4)], [rS.k(0)],
                                 eng="vector")
                            copy(rS.ap[:, :, 4:8], bank[:, 0:64].rearrange("p (s j) -> p s j", j=4), [bk], [rS.k(1)], eng="vector")
                            copy(rawSf.ap[:, cc, :], bank[:, 0:64], [bk], [rawSf.k(cc)], eng="scalar")
                            copy(rM.ap[:, 4:20], bank[:, 64:80], [bk], [rM.k()], eng="vector")
                            copy(hist32.ap[:, cc, :], bank[:, 76:80], [bk], [hist32.k(cc)], eng="vector")

                    def s2(cc):
                        dg = dgs[cc % 3]
                        DGK = [dg.k(k) for k in range(4)]
                        cbank, cbk = PS()
                        if not isms:
                            rb = rawb[cc % 3]
                            mm(cbank[:, 0:n], [(dg.ap[:, k, :], rb.ap[:, 1 + k:1 + k + n]) for k in range(4)],
                               DGK + [rb.k(0), rb.k(1)], [cbk])
                        else:
                            rS, rM = rawS[cc % 3], rawM[cc % 3]
                            items = []
                            for k in range(4):
                                items.append((cbank[:, 0:124], dg.ap[:, k, :],
                                              rS.ap[:, :, :].rearrange("p s r -> p (s r)")[:, 1 + k:1 + k + 124],
                                              k == 0, k == 3, None))
                            for k in range(4):
                                items.append((cbank[:, 128:144], dg.ap[:, k, :], rM.ap[:, 1 + k:1 + k + 16], k == 0, k == 3, None))
                            mm_multi(items, DGK + [rS.k(0), rS.k(1), rM.k()], [cbk])
                        if cc < 16:
                            dst, dk_ = xcb[cc % 3].ap[:, 0:n], xcb[cc % 3].k()
                        elif cc < 20:
                            dst, dk_ = BT.ap[:, cc - 16, 0:n], BT.k(cc - 16)
                        else:
                            dst, dk_ = CT.ap[:, cc - 20, 0:n], CT.k(cc - 20)
                        if not isms:
                            act(dst, cbank[:, 0:n], AF.Silu, [cbk, cv.k()], [dk_], bias=cv.ap[:, cbo + cc:cbo + cc + 1])
                        else:
                            act(dst[:, 0:64].rearrange("p (s j) -> p s j", j=4),
                                cbank[:, 0:128].rearrange("p (s r) -> p s r", r=8)[:, :, 0:4], AF.Silu, [cbk, cv.k()], [dk_],
                                bias=cv.ap[:, cbo + cc:cbo + cc + 1])
                            act(dst[:, 64:80], cbank[:, 128:144], AF.Silu, [cbk, cv.k(), dk_], [dk_],
                                bias=cv.ap[:, cbo + cc:cbo + cc + 1])
                        st1[cc] = (dst, dk_)

                    def s3(cc):
                        dst, dk_ = st1.pop(cc)
                        if cc >= 20:
                            return
                        tbank, tbk = PS()
                        tv = bfview(tbank)
                        transpose_multi([(tv[0:R, t * 128:(t + 1) * 128], dst[:, 128 * t:128 * t + R], identb)
                                         for t in range(ntile)], [dk_] + CBK, [tbk])
                        if cc < 16:
                            copy(xtok.ap[0:R, :, cc * 128:(cc + 1) * 128],
                                 tv[0:R, 0:ntile * 128].rearrange("p (t d) -> p t d", t=ntile), [tbk], [xtok.k(cc)])
                        else:
                            copy(btok.ap[0:R, :, (cc - 16) * 128:(cc - 15) * 128],
                                 tv[0:R, 0:ntile * 128].rearrange("p (t d) -> p t d", t=ntile), [tbk], [btok.k(cc - 16)])

                    load_wblk(0)
                    load_wblk(1)
                    for step in range(24 + 2):
                        if step < 24:
                            s1(step)
                        if 0 <= step - 1 < 24:
                            s2(step - 1)
                        if 0 <= step - 2 < 24:
                            s3(step - 2)
                    cutpoint(2)
                    dto, _ = CV_LAY["dtb"]
                    for t in range(ntile):
                        bank, bk = PS()
                        mm(bank[0:R, 0:32], [(hn.ap[:, kc, 128 * t:128 * t + R], wdt.ap[:, kc, :]) for kc in range(8)],
                           HNK + [wdt.k()], [bk])
                        tt(dtall.ap[0:R, t, :], bank[0:R, 0:32], cv.ap[0:R, dto:dto + 32], ALU.add, [bk, cv.k()], [dtall.k(t)])
                    for b in range(4):
                        wb = wblk[b % 2]
                        dma(wb.ap[:, :, :], win1_d[:, 512 * b:512 * (b + 1)].rearrange("(k p) n -> p k n", p=128),
                            [], [wb.k()], eng="gpsimd")
                        for t in range(ntile):
                            bank, bk = PS()
                            mm(bank[0:R, 0:512], [(hn.ap[:, kc, 128 * t:128 * t + R], wb.ap[:, kc, :]) for kc in range(8)],
                               HNK + [wb.k()], [bk])
                            act(zs.ap[0:R, t, 512 * b:512 * (b + 1)], bank[0:R, 0:512], AF.Silu, [bk], [zs.k(t, b)])
                cutpoint(3)
                XTK = [xtok.k(cc) for cc in range(16)]
                BTK = [btok.k(q) for q in range(4)]
                with A.scope():
                    nmaskb = ncmsb if isms else ncausb
                    dso, _ = CV_LAY["dskip"]
                    ono, _ = CV_LAY["odnorm"]
                    ybk = [("ps", 4 + gq) for gq in range(4)]
                    smts = [A.alloc("smt", [128, 8, 32]) for _ in range(2)]
                    xws = [A.alloc("xw", [128, 2048], BF16) for _ in range(2)]
                    cbms = [A.alloc("cbm", [128, 4, 128], BF16) for _ in range(2)]
                    yis = A.alloc("yis", [128, 2048])
                    Dgs = [A.alloc("Dg", [128, 128]) for _ in range(4)]
                    Ls = [A.alloc("L", [128, 128]) for _ in range(4)]
                    Ms = [A.alloc("M", [128, 128], BF16) for _ in range(4)]
                    y = A.alloc("y", [128, 2048])
                    ssq = A.alloc("ssq", [128, 8])
                    junk = A.alloc("junk", [128, 512], BF16)
                    y3 = A.alloc("y3", [128, 2048], BF16)

                    def prologue(t):
                        l0 = 128 * t
                        smt, xw, cbm = smts[t % 2], xws[t % 2], cbms[t % 2]
                        dtS, aS, cumS, ncum = smt.ap[0:R, 0, :], smt.ap[0:R, 1, :], smt.ap[0:R, 2, :], smt.ap[0:R, 3, :]
                        ecum, wj, tmp = smt.ap[0:R, 4, :], smt.ap[0:R, 5, :], smt.ap[0:R, 7, :]
                        dec = smt.ap[:, 6, :]
                        act(tmp, dtall.ap[0:R, t, :], AF.Exp, [dtall.k(t)], [smt.k(7)])
                        act(dtS, tmp, AF.Ln, [smt.k(7), mk.k()], [smt.k(0)], bias=onecol[0:R, :])
                        stt(aS, dtS, -1.0, negA.ap[0:R, :], ALU.mult, ALU.mult, [smt.k(0), negA.k()], [smt.k(1)])
                        tri = MK("cms", parts=80, cols=80) if isms else MK("causal")
                        segm = MK("sseg", parts=80, cols=80) if isms else onesf
                        TR = R if isms else 128
                        bank, bk = PS(pool=(0, 1))
                        mm_multi([(bank[0:R, 0:32], tri[0:R, 0:R], aS, True, True, None),
                                  (bank[0:TR, 32:64], segm[0:R, 0:TR], aS, True, True, None)], [smt.k(1), mk.k()], [bk])
                        copy(cumS, bank[0:R, 0:32], [bk], [smt.k(2)], eng="vector")
                        act(ncum, dtS, AF.Ln, [smt.k(0)], [smt.k(3)])
                        tt(ncum, ncum, bank[0:R, 0:32], ALU.subtract, [smt.k(3), bk], [smt.k(3)])
                        act(ecum, cumS, AF.Exp, [smt.k(2)], [smt.k(4)])
                        tt(tmp, bank[0:R, 32:64], cumS, ALU.subtract, [bk, smt.k(2), smt.k(0)], [smt.k(7)])
                        if not isms:
                            act(dec, bank[:, 32:64], AF.Exp, [bk], [smt.k(6)])
                        act(tmp, tmp, AF.Exp, [smt.k(7)], [smt.k(7)])
                        tt(wj, tmp, dtS, ALU.mult, [smt.k(7), smt.k(0)], [smt.k(5)])
                        tt(xw.ap[0:R, :].rearrange("p (h q) -> p h q", q=64),
                           xtok.ap[0:R, t, :].rearrange("p (h q) -> p h q", q=64),
                           wj.unsqueeze(2).to_broadcast([R, 32, 64]), ALU.mult, XTK + [smt.k(5)], [xw.k()])
                        bank, bk = PS(pool=(0, 1))
                        mm_multi([(bank[0:R, gq * 128:gq * 128 + R], BT.ap[:, gq, l0:l0 + R], CT.ap[:, gq, l0:l0 + R], True, True, None)
                                  for gq in range(4)], [BT.k(q) for q in range(4)] + [CT.k(q) for q in range(4)], [bk])
                        copy(cbm.ap[0:R, :, 0:R], bank[0:R, :].rearrange("p (q d) -> p q d", q=4)[:, :, 0:R], [bk], [cbm.k()])

                    def middle(t, part):
                        l0 = 128 * t
                        smt, xw, cbm = smts[t % 2], xws[t % 2], cbms[t % 2]
                        dtS, aS, cumS, ncum = smt.ap[0:R, 0, :], smt.ap[0:R, 1, :], smt.ap[0:R, 2, :], smt.ap[0:R, 3, :]
                        ecum = smt.ap[0:R, 4, :]
                        dec = smt.ap[:, 6, :]
                        if part == "pre":
                            cutpoint(4)
                            if isms:
                                sample_ssd(CT, btok, xw, smt, aS, ecum, yis)
                                for gq in range(4):
                                    bank, bk = PS(pool=(0, 1))
                                    mm(bank[:, 0:512], [(btok.ap[64:80, 0, gq * 128:(gq + 1) * 128], xw.ap[64:80, gq * 512:(gq + 1) * 512])],
                                       BTK + [xw.k()], [bk], tile_position=(64, 0))
                                    copy(ST.ap[:, gq * 512:(gq + 1) * 512], bank[:, 0:512], [bk], [ST.k(gq)], eng="vector")
                                    copy(STb.ap[:, gq * 512:(gq + 1) * 512], bank[:, 0:512], [bk], [STb.k(gq)], eng="scalar")
                            else:
                                for gq in range(4):
                                    cs_ = slice(gq * 512, (gq + 1) * 512)
                                    bank, bk = PS(pool=(0, 1))
                                    mm(bank[0:R, 0:512], [(CT.ap[:, gq, l0:l0 + R], STb.ap[:, cs_])], [CT.k(gq), STb.k(gq)], [bk])
                                    tt(yis.ap[0:R, cs_].rearrange("p (h q) -> p h q", q=64),
                                       bank[0:R, 0:512].rearrange("p (h q) -> p h q", q=64),
                                       ecum[:, 8 * gq:8 * gq + 8].unsqueeze(2).to_broadcast([R, 8, 64]), ALU.mult, [bk, smt.k(4)],
                                       [yis.k(gq)])
                                for gq in range(4):
                                    cs_ = slice(gq * 512, (gq + 1) * 512)
                                    bank, bk = PS(pool=(0, 1))
                                    mm(bank[:, 0:512], [(btok.ap[0:R, t, gq * 128:(gq + 1) * 128], xw.ap[0:R, cs_])], BTK + [xw.k()], [bk])
                                    tt(ST.ap[:, cs_].rearrange("p (h q) -> p h q", q=64), ST.ap[:, cs_].rearrange("p (h q) -> p h q", q=64),
                                       dec[:, 8 * gq:8 * gq + 8].unsqueeze(2).to_broadcast([128, 8, 64]), ALU.mult, [ST.k(gq), smt.k(6)],
                                       [ST.k(gq)])
                                    tt(ST.ap[:, cs_], ST.ap[:, cs_], bank[:, 0:512], ALU.add, [ST.k(gq), bk], [ST.k(gq)])
                                    copy(STb.ap[:, cs_], ST.ap[:, cs_], [ST.k(gq)], [STb.k(gq)], eng="scalar")
                            return
                        cutpoint(5)

                        def stageA(h):
                            Dg = Dgs[h % 4]
                            ts(Dg.ap[0:R, 0:R], identf[0:R, 0:R], cumS[:, h:h + 1], 1.0, ALU.mult, ALU.mult, [smt.k(2), mk.k()],
                               [Dg.k()], eng="gpsimd")
                            rbank, rbk = PS(pool=(1, 2, 3))
                            mm_multi([(rbank[0:R, 0:R], onesf[0:R, 0:R], Dg.ap[0:R, 0:R], True, False, None),
                                      (rbank[0:R, 0:R], identb[0:R, 0:R], nmaskb[0:R, 0:R], False, True, None)],
                                     [Dg.k(), mk.k()] + CBK, [rbk])
                            return rbank, rbk

                        def stageB(h, rbank, rbk):
                            gq = h // 8
                            L, M = Ls[h % 4], Ms[h % 4]
                            act(L.ap[0:R, 0:R], rbank[0:R, 0:R], AF.Exp, [rbk, smt.k(3)], [L.k()], bias=ncum[:, h:h + 1])
                            tt(M.ap[0:R, 0:R], L.ap[0:R, 0:R], cbm.ap[0:R, gq, 0:R], ALU.mult, [L.k(), cbm.k()], [M.k()])

                        def stageC(h):
                            gq = h // 8
                            M = Ms[h % 4]
                            mm(banks[4 + gq][0:R, (h % 8) * 64:(h % 8) * 64 + 64],
                               [(M.ap[0:R, 0:R], xtok.ap[0:R, t, h * 64:(h + 1) * 64]),
                                (diagD.ap[0:R, h, 0:R], xtok.ap[0:R, t, h * 64:(h + 1) * 64])],
                               [M.k(), diagD.k(h)] + XTK, [ybk[gq]])

                        DLY = 3
                        rbs = {}
                        for step in range(32 + DLY):
                            if step < 32:
                                rbs[step] = stageA(step)
                            if 0 <= step - 1 < 32:
                                stageB(step - 1, *rbs.pop(step - 1))
                            if 0 <= step - DLY < 32:
                                stageC(step - DLY)
                        cutpoint(6)

                    def epiA(t):
                        for gq in range(4):
                            cs_ = slice(gq * 512, (gq + 1) * 512)
                            if not isms:
                                tt(y.ap[0:R, cs_], banks[4 + gq][0:R, 0:512], yis.ap[0:R, cs_], ALU.add, [ybk[gq], yis.k(gq)], [y.k(gq)])
                            else:
                                tt(y.ap[0:64, cs_], banks[4 + gq][0:64, 0:512], yis.ap[0:64, cs_], ALU.add, [ybk[gq], yis.k(gq)],
                                   [y.k(gq)])
                                copy(y.ap[64:80, cs_], banks[4 + gq][64:80, 0:512], [ybk[gq]], [y.k(gq, 1)], eng="vector")

                    def epiB1(t):
                        for gq in range(4):
                            cs_ = slice(gq * 512, (gq + 1) * 512)
                            yk = [y.k(gq)] + ([y.k(gq, 1)] if isms else [])
                            tt(y.ap[0:R, cs_], y.ap[0:R, cs_], zs.ap[0:R, t, cs_], ALU.mult, yk + [zs.k(t, gq)], yk, eng="gpsimd")
                            act_acc(junk.ap[0:R, :], y.ap[0:R, cs_], AF.Square, ssq.ap[0:R, gq:gq + 1], yk, [junk.k(), ssq.k(gq)])
                        SSK = [ssq.k(gq) for gq in range(4)]
                        act(ssq.ap[0:R, 4:8], ssq.ap[0:R, 0:4], AF.Ln, SSK, [ssq.k(9)], scale=1.0 / 512.0, bias=epsb.ap[0:R, 0:1])
                        act(ssq.ap[0:R, 4:8], ssq.ap[0:R, 4:8], AF.Exp, [ssq.k(9)], [ssq.k(9)], scale=-0.5)
                        for gq in range(4):
                            cs_ = slice(gq * 512, (gq + 1) * 512)
                            yk = [y.k(gq)] + ([y.k(gq, 1)] if isms else [])
                            stt(y3.ap[0:R, cs_], y.ap[0:R, cs_], ssq.ap[0:R, 4 + gq:5 + gq], cv.ap[0:R, ono + gq * 512:ono + (gq + 1) * 512],
                                ALU.mult, ALU.mult, yk + [ssq.k(9), cv.k()], [y3.k(gq)])

                    def epiB2(t):
                        l0 = 128 * t
                        for half in range(2):
                            bank, bk = PS(pool=(0, 1, 2, 3))
                            bv = bfview(bank)
                            transpose_multi([(bv[:, q * R:(q + 1) * R], y3.ap[0:R, (8 * half + q) * 128:(8 * half + q + 1) * 128],
                                              identb[0:R, 0:R]) for q in range(8)], [y3.k(2 * half), y3.k(2 * half + 1)] + CBK, [bk])
                            copy(y3T.ap[:, 8 * half:8 * half + 8, l0:l0 + R], bv[:, 0:8 * R].rearrange("p (q r) -> p q r", q=8), [bk],
                                 [y3T.k(t, half)])
                        cutpoint(7)

                    prologue(0)
                    for t in range(ntile):
                        middle(t, "pre")
                        if t > 0:
                            epiB1(t - 1)
                        middle(t, "loop")
                        if t > 0:
                            epiB2(t - 1)
                        epiA(t)
                        if t + 1 < ntile:
                            prologue(t + 1)
                    epiB1(ntile - 1)
                    epiB2(ntile - 1)
                cutpoint(8)
                if isms:
                    with A.scope():
                        tokS = A.alloc("tokS", [64, 3072])
                        for b6 in range(6):
                            bank, bk = PS()
                            transpose_multi([(bank[0:64, q * 128:(q + 1) * 128], rawSf.ap[:, 4 * b6 + q, :], identf) for q in range(4)],
                                            [rawSf.k(4 * b6 + q) for q in range(4)] + [mk.k()], [bk])
                            copy(tokS.ap[0:64, 512 * b6:512 * (b6 + 1)], bank[0:64, :], [bk], [tokS.k(b6)])
                        for s in range(16):
                            okey = ("out_cs", s)
                            dma(cs_d[s, :, :], tokS.ap[4 * s + 1:4 * s + 4, :], [tokS.k(b6) for b6 in range(6)], [okey],
                                eng=("sync" if s % 2 else "scalar"))
                            out_keys.append(okey)
                cutpoint(9)
                Y3K = [y3T.k(t, half) for t in range(ntile) for half in range(2)]
                with A.scope():
                    mix = A.alloc("mix1", [128, 8, n])
                    wob = [A.alloc("wob", [128, 16, 128], BF16) for _ in range(3)]
                    for oc in range(8):
                        wo = wob[oc % 3]
                        dma(wo.ap[:, :, :], wout1_d[:, oc * 128:(oc + 1) * 128].rearrange("(k p) n -> p k n", p=128), [], [wo.k()],
                            eng="gpsimd")
                        bank, bk = PS()
                        mm(bank[:, 0:n], [(wo.ap[:, kc, :], y3T.ap[:, kc, 0:n]) for kc in range(16)], Y3K + [wo.k()], [bk])
                        copy(mix.ap[:, oc, 0:n], bank[:, 0:n], [bk], [mix.k(oc)])
                    postnorm_add(g, "nmpost", 1, mix, sq, rstd)

        def sample_ssd(CT, btok, xw, smt, aS, ecum, yis):
            with A.scope():
                CmT = A.alloc("CmT", [128, 4, 1088], BF16)
                P.op("gpsimd", lambda e: e.memset(CmT.ap[:, :, :], 0.0), writes=[CmT.k()])
                for gq in range(4):
                    copy(CmT.ap[:, gq, :].rearrange("p (s r) -> p s r", r=68)[:, :, 0:4],
                         CT.ap[:, gq, 0:64].rearrange("p (s j) -> p s j", j=4), [CT.k(gq), CmT.k()], [CmT.k(gq)], eng="vector")
                CMK = [CmT.k(gq) for gq in range(4)]
                decn = A.alloc("decn", [128, 256])
                with A.scope():
                    aexp = A.alloc("aexp", [64, 2048])
                    copy(aexp.ap[:, :].rearrange("p (h q) -> p h q", q=64), aS[0:64, :].unsqueeze(2).to_broadcast([64, 32, 64]),
                         [smt.k(1)], [aexp.k()], eng="vector")
                    bank, bk = PS(pool=(0, 1))
                    mm_multi([(bank[:, hb * 16:(hb + 1) * 16], aexp.ap[0:64, hb * 128:(hb + 1) * 128], MK("seg", parts=64, cols=16),
                               True, True, None) for hb in range(16)], [aexp.k(), mk.k()], [bk])
                    act(decn.ap[:, :], bank[:, 0:256], AF.Exp, [bk], [decn.k()])
                S0s = [A.alloc("S0s", [128, 16, 128]) for _ in range(3)]
                S0Ts = [A.alloc("S0T", [128, 2048], BF16) for _ in range(2)]
                Sns = [A.alloc("Sns", [128, 16, 128]) for _ in range(2)]
                Bms = [A.alloc("Bm", [64, 512], BF16) for _ in range(2)]
                yk = [("ps", 4 + gq) for gq in range(4)]
                def sload(s):
                    S0 = S0s[s % 3]
                    dma(S0.ap[:, :, :], sts_d[s, :, :, :].rearrange("(hb two) p n -> (two p) hb n", two=2), [], [S0.k()], eng="sync")

                def sfront(s):
                    S0, S0T = S0s[s % 3], S0Ts[s % 2]
                    for qd in range(4):
                        bank, bk = PS(pool=(0, 1))
                        transpose_multi([(bank[:, q * 128:(q + 1) * 128], S0.ap[:, 4 * qd + q, :], identf) for q in range(4)],
                                        [S0.k(), mk.k()], [bk])
                        copy(S0T.ap[:, qd * 512:(qd + 1) * 512], bank[:, :], [bk], [S0T.k(qd)])

                def sback(s):
                    S0, S0T, Sn, Bm = S0s[s % 3], S0Ts[s % 2], Sns[s % 2], Bms[s % 2]
                    mm_multi([(banks[4 + gq][0:64, 0:512], CmT.ap[:, gq, s * 64:(s + 1) * 64], S0T.ap[:, gq * 512:(gq + 1) * 512],
                               s == 0, s == 15, None) for gq in range(4)], CMK + [S0T.k(qd) for qd in range(4)], yk)
                    ts(Bm.ap[:, :], btok.ap[0:64, 0, :], MK("seg", parts=64, cols=16)[:, s:s + 1], 1.0, ALU.mult, ALU.mult,
                       [btok.k(q) for q in range(4)] + [mk.k()], [Bm.k()], eng="gpsimd")
                    for qd in range(4):
                        bank, bk = PS(pool=(2, 3))
                        mm_multi([(bank[:, q * 128:(q + 1) * 128], xw.ap[0:64, (4 * qd + q) * 128:(4 * qd + q + 1) * 128],
                                   Bm.ap[0:64, qd * 128:(qd + 1) * 128], True, True, None) for q in range(4)], [xw.k(), Bm.k()], [bk])
                        for q in range(4):
                            hb = 4 * qd + q
                            stt(Sn.ap[:, hb, :], S0.ap[:, hb, :], decn.ap[:, hb * 16 + s:hb * 16 + s + 1], bank[:, q * 128:(q + 1) * 128],
                                ALU.mult, ALU.add, [S0.k(), decn.k(), bk], [Sn.k(hb)])
                    okey = ("out_ss", s)
                    dma(ss_d[s, :, :, :].rearrange("(hb two) p n -> (two p) hb n", two=2), Sn.ap[:, :, :],
                        [Sn.k(hb) for hb in range(16)], [okey], eng="gpsimd")
                    out_keys.append(okey)

                sload(0)
                sload(1)
                sfront(0)
                for s in range(16):
                    if s + 2 < 16:
                        sload(s + 2)
                    if s + 1 < 16:
                        sfront(s + 1)
                    sback(s)
                for gq in range(4):
                    tt(yis.ap[0:64, gq * 512:(gq + 1) * 512].rearrange("p (h q) -> p h q", q=64),
                       banks[4 + gq][0:64, 0:512].rearrange("p (h q) -> p h q", q=64),
                       ecum[0:64, 8 * gq:8 * gq + 8].unsqueeze(2).to_broadcast([64, 8, 64]), ALU.mult, [yk[gq], smt.k(4)], [yis.k(gq)])

        if not int(os.environ.get("SKIP0", "0")):
            layer0_mixer()
        A.pop()
        if not int(os.environ.get("SKIP0", "0")):
            if stage >= 2:
                ffn(0)
        if stage >= 3:
            layer1_mixer()
        if stage >= 4:
            ffn(1)
        store_out()
        if pi == 1 or True:
            pass

    ST = A.alloc("ST", [128, 2048])
    STb = A.alloc("STb", [128, 2048], BF16)
    hist32 = A.alloc("hist32", [128, 24, 4])
    if int(os.environ.get("SKIP0", "0")):
        for b_ in (S_h, S_g):
            P.op("vector", lambda e, b_=b_: e.memset(b_.ap, 0.0), writes=[b_.k()])
    epsb = A.alloc("epsb", [128, 2])
    P.op("vector", lambda e: e.memset(epsb.ap[:, :], EPS), writes=["epsb_key"])

    A.push()
    run_pass(0)
    A.pop()
    A.push()
    run_pass(1)
    A.pop()

    okey = ("out_hp",)
    dma(hp_d[:, :, :].rearrange("h k v -> k h v"), S_h.ap[:, :, :], [S_h.k()], [okey])
    out_keys.append(okey)
    okey = ("out_gp",)
    dma(gp_d[:, :, :].rearrange("h k v -> k h v"), S_g.ap[:, :, :], [S_g.k()], [okey])
    out_keys.append(okey)

    if stage >= 3:
        A.push()
        spn = A.alloc("spn", [128, 16, 128])
        for qd in range(4):
            bank, bk = PS()
            transpose_multi([(bank[:, q * 128:(q + 1) * 128], ST.ap[:, (4 * qd + q) * 128:(4 * qd + q + 1) * 128], identf)
                             for q in range(4)], [ST.k(g_) for g_ in range(4)] + [mk.k()], [bk])
            copy(spn.ap[:, 4 * qd:4 * qd + 4, :], bank[:, :].rearrange("p (q d) -> p q d", q=4), [bk], [spn.k(qd)])
        okey = ("out_sp",)
        dma(sp_d.rearrange("(hb two) p n -> (two p) hb n", two=2), spn.ap[:, :, :], [spn.k(qd) for qd in range(4)], [okey])
        out_keys.append(okey)
        cpst = A.alloc("cpst", [3, 3072])
        for b6 in range(6):
            bank, bk = PS()
            transpose_multi([(bank[0:3, q * 128:(q + 1) * 128], hist32.ap[:, 4 * b6 + q, 1:4], identf) for q in range(4)],
                            [hist32.k(4 * b6 + q) for q in range(4)] + [mk.k()], [bk])
            copy(cpst.ap[0:3, 512 * b6:512 * (b6 + 1)], bank[0:3, :], [bk], [cpst.k(b6)])
        okey = ("out_cp",)
        dma(cp_d[:, :], cpst.ap[:, :], [cpst.k(b6) for b6 in range(6)], [okey])
        out_keys.append(okey)
        A.pop()
    P.finish_wait("sync", out_keys)
    P.emit(es)
    nc._dbgP = P
    es.close()
    return nc, A.peak


def _pack_cvec(inp):
    cvv = np.zeros((128, CV_N), np.float32)

    def put(name, arr):
        o, w = CV_LAY[name]
        assert arr.shape == (128, w), (name, arr.shape, w)
        cvv[:, o:o + w] = arr

    def fm(v):
        L = v.shape[0]
        return np.ascontiguousarray(v.reshape(L, 8, 128).transpose(2, 0, 1).reshape(128, L * 8))

    put("nmpre", fm(inp["norm_mix_pre"]))
    put("nmpost", fm(inp["norm_mix_post"]))
    put("nfpre", fm(inp["norm_ffn_pre"]))
    put("nfpost", fm(inp["norm_ffn_post"]))
    put("gamma", np.ascontiguousarray(inp["hgrn_gamma"].reshape(3, 4, 128).transpose(2, 0, 1).reshape(128, 12)))
    ba = np.zeros((128, 4), np.float32)
    ba[0:64, :] = inp["ev_b_alpha"][0].reshape(4, 64).T
    put("balpha", ba)
    put("norma", inp["ev_norm_a"][0].reshape(128, 1))
    put("normb", inp["ev_norm_b"][0].reshape(128, 1))
    put("convw", np.ascontiguousarray(inp["od_conv_w"][0].reshape(4, 24, 128).transpose(2, 0, 1).reshape(128, 96)))
    put("convb", np.ascontiguousarray(inp["od_conv_b"][0].reshape(24, 128).T))
    put("dtb", np.broadcast_to(inp["od_dt_bias"][0][None, :], (128, 32)))
    put("alog", np.broadcast_to(inp["od_a_log"][0][None, :], (128, 32)))
    put("dskip", np.broadcast_to(inp["od_d_skip"][0][None, :], (128, 32)))
    put("odnorm", np.broadcast_to(inp["od_norm"][0][None, :], (128, 2048)))
    return cvv


_PROG_CACHE = {}


def kernel(**inputs):
    inp = {k: np.asarray(v) for k, v in inputs.items()}
    SEQ = inp["x_prompt"].shape[1]
    stage = int(inp.pop("_stage", 99)) if "_stage" in inp else 99
    key = (SEQ, stage)
    if key not in _PROG_CACHE:
        _PROG_CACHE[key] = build_program(SEQ, stage)
    nc, _ = _PROG_CACHE[key]
    cvec = _pack_cvec(inp)
    masks = _build_masks()
    f = lambda a: np.ascontiguousarray(a, dtype=np.float32)
    shared = {
        "meta": f(inp["meta_tokens"]), "w_in0": f(inp["ev_w_in"][0]), "w_up": f(inp["ev_w_alpha_up"][0]),
        "w_out0": f(inp["ev_w_out"][0]), "w_in1": f(inp["od_w_in"][0]), "w_out1": f(inp["od_w_out"][0]),
        "w_g": f(inp["ffn_w_gate"]), "w_u": f(inp["ffn_w_up"]), "w_d": f(inp["ffn_w_down"]),
        "cvec": cvec, "masks": masks,
    }
    in_maps = []
    for c in range(8):
        m = dict(shared)
        m["xp"] = f(inp["x_prompt"][c])
        m["xs"] = f(inp["x_sample"][16 * c:16 * c + 16].reshape(64, D))
        m["st_h"] = f(inp["state_hgrn"][0, 16 * c:16 * c + 16])
        m["st_g"] = f(inp["state_gla"][0, 16 * c:16 * c + 16])
        m["st_s"] = f(inp["state_ssm"][0, 16 * c:16 * c + 16])
        m["st_c"] = f(inp["state_conv"][0, 16 * c:16 * c + 16])
        in_maps.append(m)
    if ONECORE:
        res = run_bass_kernel_spmd(nc, in_maps[:1], core_ids=[0])
        R = [res.results[0]] * 8
    else:
        res = run_bass_kernel_spmd(nc, in_maps, core_ids=list(range(8)))
        R = res.results
    cat = lambda k: np.stack([np.asarray(r[k]) for r in R], axis=0)
    y_prompt = cat("yp")
    y_sample = np.concatenate([np.asarray(r["ys"]).reshape(16, 4, D) for r in R], axis=0)
    hgrn_p = cat("hp")[None]
    gla_p = cat("gp")[None]
    ssm_p = cat("sp")[None]
    conv_p = cat("cp")[None]
    hgrn_s = np.concatenate([np.asarray(r["hs"]) for r in R], axis=0)[None]
    gla_s = np.concatenate([np.asarray(r["gs"]) for r in R], axis=0)[None]
    ssm_s = np.concatenate([np.asarray(r["ss"]) for r in R], axis=0)[None]
    conv_s = np.concatenate([np.asarray(r["cs"]) for r in R], axis=0)[None]
    return (y_prompt, y_sample, hgrn_p, gla_p, ssm_p, conv_p, hgrn_s, gla_s, ssm_s, conv_s)
```
